# Optimizing a Trainium2 kernel written in Bass

```python
import math
import jax, jax.numpy as jnp
from jax import lax
import numpy as np

D_MODEL = 1024
BATCH = 4
SEQ = 8192
DEPTH = 4

N_MEM = 256
POOL_WIDTH = 256
POOL_GROUPS = 4
POOL_GROUP_DIM = POOL_WIDTH // POOL_GROUPS
POOL_WINDOWS = (2, 4, 8, 16)
DA_HEADS = 4
DA_HEAD_DIM = 64
DA_V_DIM = 2 * DA_HEAD_DIM
DA_WIDTH = DA_HEADS * DA_V_DIM
CA_HEADS = 4
CA_HEAD_DIM = 64
CA_WIDTH = CA_HEADS * CA_HEAD_DIM
IN_SPLITS = (POOL_WIDTH, POOL_WIDTH + DA_WIDTH, POOL_WIDTH + 2 * DA_WIDTH, POOL_WIDTH + 3 * DA_WIDTH)
IN_WIDTH = POOL_WIDTH + 3 * DA_WIDTH + CA_WIDTH
N_BRANCH = 3
D_FF = 2816
ROPE_THETA = 500000.0
ROPE_DIM = DA_HEAD_DIM // 4
Q_BLOCK = 128
NORM_EPS = 1e-6
LAMBDA_STD = 0.1

kernel_name = "hybrid_pool_diffattn_memxattn_macaron_encoder"


def rms_norm(x, g):
    xf = x.astype(jnp.float32)
    xf = xf * lax.rsqrt(jnp.mean(xf * xf, axis=-1, keepdims=True) + NORM_EPS)
    return (xf * g.astype(jnp.float32)).astype(x.dtype)


def swiglu(h, w_up, w_down):
    a, b = jnp.split(h @ w_up, 2, axis=-1)
    return (jax.nn.silu(a) * b) @ w_down


def rope_tables(positions):
    inv = ROPE_THETA ** (-jnp.arange(0, ROPE_DIM, 2, dtype=jnp.float32) / ROPE_DIM)
    ang = positions.astype(jnp.float32)[..., None] * inv
    return jnp.cos(ang), jnp.sin(ang)


def apply_partial_rope(x, cos, sin):
    half = ROPE_DIM // 2
    x1, x2, xp = x[..., :half], x[..., half:ROPE_DIM], x[..., ROPE_DIM:]
    c = cos[:, :, None, None, :].astype(x.dtype)
    s = sin[:, :, None, None, :].astype(x.dtype)
    return jnp.concatenate([x1 * c - x2 * s, x2 * c + x1 * s, xp], axis=-1)


def multiscale_pool(u, w_group, scale):
    B, S, _ = u.shape
    uf = u.astype(jnp.float32)
    csum = jnp.concatenate([jnp.zeros_like(uf[:, :1]), jnp.cumsum(uf, axis=1)], axis=1)
    pos = jnp.arange(S)
    outs = []
    for gi, w in enumerate(POOL_WINDOWS):
        lo = jnp.clip(pos - w // 2, 0, S)
        hi = jnp.clip(pos + w // 2, 0, S)
        sl = slice(gi * POOL_GROUP_DIM, (gi + 1) * POOL_GROUP_DIM)
        cg = csum[:, :, sl]
        cnt = (hi - lo).astype(jnp.float32)[None, :, None]
        outs.append((cg[:, hi] - cg[:, lo]) / cnt - uf[:, :, sl])
    pooled = jnp.stack(outs, axis=2).astype(u.dtype)
    mixed = jnp.einsum('bsgc,gcd->bsgd', pooled, w_group)
    return mixed.reshape(B, S, POOL_WIDTH) * scale


def diff_attention(q, k, v, lam):
    B, S = q.shape[:2]
    nblk = S // Q_BLOCK
    qb = q.reshape(B, nblk, Q_BLOCK, DA_HEADS, 2, DA_HEAD_DIM).transpose(1, 0, 2, 3, 4, 5)
    scale = DA_HEAD_DIM ** -0.5

    def block(qi):
        s = jnp.einsum('bqhcd,bkhcd->bhcqk', qi, k, preferred_element_type=jnp.float32) * scale
        p = jax.nn.softmax(s, axis=-1)
        a = p[:, :, 0] - lam * p[:, :, 1]
        return jnp.einsum('bhqk,bkhe->bqhe', a.astype(v.dtype), v)

    o = lax.map(block, qb)
    return o.transpose(1, 0, 2, 3, 4).reshape(B, S, DA_HEADS, DA_V_DIM)


def memory_cross_attention(q, kv):
    B, M, _ = kv.shape
    k, v = jnp.split(kv, 2, axis=-1)
    k = k.reshape(B, M, CA_HEADS, CA_HEAD_DIM)
    v = v.reshape(B, M, CA_HEADS, CA_HEAD_DIM)
    s = jnp.einsum('bshd,bmhd->bhsm', q, k, preferred_element_type=jnp.float32) * (CA_HEAD_DIM ** -0.5)
    p = jax.nn.softmax(s, axis=-1).astype(v.dtype)
    return jnp.einsum('bhsm,bmhd->bshd', p, v)


def setup_inputs(seed: int = 0) -> dict:
    key = jax.random.key(seed)
    ks = jax.random.split(key, 32)
    f32 = jnp.float32

    def nrm(k, shape, fan_in):
        return jax.random.normal(k, shape, f32) * (fan_in ** -0.5)

    def gain(k, shape):
        return 1.0 + 0.05 * jax.random.normal(k, shape, f32)

    L, D = DEPTH, D_MODEL
    return {
        "x": jax.random.normal(ks[0], (BATCH, SEQ, D), f32),
        "mem": jax.random.normal(ks[1], (BATCH, N_MEM, D), f32),
        "positions": jnp.broadcast_to(jnp.arange(SEQ, dtype=jnp.int32), (BATCH, SEQ)),
        "ffn1_pre_g": gain(ks[2], (L, D)),
        "ffn1_w_up": nrm(ks[3], (L, D, 2 * D_FF), D),
        "ffn1_w_down": nrm(ks[4], (L, D_FF, D), D_FF),
        "ffn1_post_g": gain(ks[5], (L, D)),
        "mix_pre_g": gain(ks[6], (L, D)),
        "w_in": nrm(ks[7], (L, D, IN_WIDTH), D),
        "pool_w": nrm(ks[8], (L, POOL_GROUPS, POOL_GROUP_DIM, POOL_GROUP_DIM), POOL_GROUP_DIM),
        "pool_scale": gain(ks[9], (L, POOL_WIDTH)),
        "da_lambda_q1": LAMBDA_STD * jax.random.normal(ks[10], (L, DA_HEAD_DIM), f32),
        "da_lambda_k1": LAMBDA_STD * jax.random.normal(ks[11], (L, DA_HEAD_DIM), f32),
        "da_lambda_q2": LAMBDA_STD * jax.random.normal(ks[12], (L, DA_HEAD_DIM), f32),
        "da_lambda_k2": LAMBDA_STD * jax.random.normal(ks[13], (L, DA_HEAD_DIM), f32),
        "da_subln_g": gain(ks[14], (L, DA_V_DIM)),
        "mem_norm_g": gain(ks[15], (L, D)),
        "w_mem_kv": nrm(ks[16], (L, D, 2 * CA_WIDTH), D),
        "w_gate": nrm(ks[17], (L, D, N_BRANCH * D), D),
        "b_gate": 0.01 * jax.random.normal(ks[18], (L, N_BRANCH * D), f32),
        "w_br_pool": nrm(ks[19], (L, POOL_WIDTH, D), POOL_WIDTH),
        "w_br_da": nrm(ks[20], (L, DA_WIDTH, D), DA_WIDTH),
        "w_br_ca": nrm(ks[21], (L, CA_WIDTH, D), CA_WIDTH),
        "w_out": nrm(ks[22], (L, D, D), D),
        "mix_post_g": gain(ks[23], (L, D)),
        "ffn2_pre_g": gain(ks[24], (L, D)),
        "ffn2_w_up": nrm(ks[25], (L, D, 2 * D_FF), D),
        "ffn2_w_down": nrm(ks[26], (L, D_FF, D), D_FF),
        "ffn2_post_g": gain(ks[27], (L, D)),
    }


def reference(x, mem, positions, ffn1_pre_g, ffn1_w_up, ffn1_w_down, ffn1_post_g,
              mix_pre_g, w_in, pool_w, pool_scale, da_lambda_q1, da_lambda_k1,
              da_lambda_q2, da_lambda_k2, da_subln_g, mem_norm_g, w_mem_kv,
              w_gate, b_gate, w_br_pool, w_br_da, w_br_ca, w_out, mix_post_g,
              ffn2_pre_g, ffn2_w_up, ffn2_w_down, ffn2_post_g):
    B, S, _ = x.shape
    cos, sin = rope_tables(positions)
    f32 = jnp.float32
    for l in range(DEPTH):
        lam_init = 0.8 - 0.6 * math.exp(-0.3 * l)

        h = rms_norm(x, ffn1_pre_g[l])
        x = x + 0.5 * rms_norm(swiglu(h, ffn1_w_up[l], ffn1_w_down[l]), ffn1_post_g[l])

        h = rms_norm(x, mix_pre_g[l])
        u_pool, q_da, k_da, v_da, q_ca = jnp.split(h @ w_in[l], IN_SPLITS, axis=-1)

        y_pool = multiscale_pool(u_pool, pool_w[l], pool_scale[l])

        q = apply_partial_rope(q_da.reshape(B, S, DA_HEADS, 2, DA_HEAD_DIM), cos, sin)
        k = apply_partial_rope(k_da.reshape(B, S, DA_HEADS, 2, DA_HEAD_DIM), cos, sin)
        v = v_da.reshape(B, S, DA_HEADS, DA_V_DIM)
        lam = (jnp.exp(jnp.sum(da_lambda_q1[l].astype(f32) * da_lambda_k1[l].astype(f32)))
               - jnp.exp(jnp.sum(da_lambda_q2[l].astype(f32) * da_lambda_k2[l].astype(f32)))
               + lam_init)
        o = diff_attention(q, k, v, lam)
        y_da = (rms_norm(o, da_subln_g[l]) * (1.0 - lam_init)).reshape(B, S, DA_WIDTH)

        kv = rms_norm(mem, mem_norm_g[l]) @ w_mem_kv[l]
        y_ca = memory_cross_attention(q_ca.reshape(B, S, CA_HEADS, CA_HEAD_DIM), kv).reshape(B, S, CA_WIDTH)

        g_pool, g_da, g_ca = jnp.split(jax.nn.sigmoid(h @ w_gate[l] + b_gate[l]), N_BRANCH, axis=-1)
        merged = (g_pool * (y_pool @ w_br_pool[l])
                  + g_da * (y_da @ w_br_da[l])
                  + g_ca * (y_ca @ w_br_ca[l]))
        x = x + rms_norm(merged @ w_out[l], mix_post_g[l])

        h = rms_norm(x, ffn2_pre_g[l])
        x = x + 0.5 * rms_norm(swiglu(h, ffn2_w_up[l], ffn2_w_down[l]), ffn2_post_g[l])
    return x
```

```python
import numpy as np
from contextlib import ExitStack
import concourse.bass as bass
import concourse.mybir as mybir
from concourse.bass_utils import run_bass_kernel_spmd

F32 = mybir.dt.float32
BF16 = mybir.dt.bfloat16
I32 = mybir.dt.int32
AF = mybir.ActivationFunctionType
ALU = mybir.AluOpType

D = 1024
DFF = 2816
NTOK = 4096
SEQ = 8192
DEPTH = 4
NMEM = 256
EPS = 1e-6
NCORES = 8


class Buf:
    __slots__ = ("w", "r")

    def __init__(self):
        self.w = None
        self.r = {}


class Tl:
    def __init__(self, t):
        self.t = t
        self.b = Buf()
        self.g = None


class Grp:
    def __init__(self, sem):
        self.sem = sem
        self.n = 0


class Phase:
    ENG = ("pe", "act", "dve", "pool", "sp")

    def __init__(self, nc, name):
        self.nc = nc
        self.name = name
        self.stack = ExitStack()
        self.all_sems = []
        self.sem = {e: self._sem(f"{name}_{e}") for e in self.ENG}
        self.cnt = {e: 0 for e in self.ENG}
        self.thunks = {e: [] for e in self.ENG}
        self.seen = {e: {} for e in self.ENG}
        self.unsig = {e: False for e in self.ENG}
        self.grps = []
        self.nt = 0

    def _sem(self, name):
        h = self.nc.alloc_semaphore(name=name)
        self.all_sems.append(h)
        return h

    def sbuf(self, shape, dt, name=None):
        self.nt += 1
        return Tl(self.stack.enter_context(self.nc.sbuf_tensor(f"{self.name}_{name or 't'}{self.nt}", list(shape), dt)))

    def psum(self, shape, dt, name=None):
        self.nt += 1
        return Tl(self.stack.enter_context(self.nc.psum_tensor(f"{self.name}_{name or 'p'}{self.nt}", list(shape), dt)))

    def grp(self):
        self.nt += 1
        g = Grp(self._sem(f"{self.name}_g{self.nt}"))
        self.grps.append(g)
        return g

    def _wait(self, eng, key, val):
        if key == eng and eng == "pe":
            return
        s = self.seen[eng]
        if s.get(key, 0) >= val:
            return
        s[key] = val
        sem = self.sem[key] if isinstance(key, str) else key.sem
        self.thunks[eng].append(lambda e, sem=sem, val=val: e.wait_ge(sem, val))

    def _deps(self, eng, reads, writes):
        for b in reads:
            if b.w is not None:
                self._wait(eng, *b.w)
        for b in writes:
            if b.w is not None:
                self._wait(eng, *b.w)
            for k, v in b.r.items():
                self._wait(eng, k, v)

    def _record(self, ev, reads, writes):
        k, v = ev
        for b in reads:
            if b.r.get(k, 0) < v:
                b.r[k] = v
        for b in writes:
            b.w = ev
            b.r = {}

    def op(self, eng, fn, reads=(), writes=(), signal=True):
        reads = [x.b if isinstance(x, Tl) else x for x in reads]
        writes = [x.b if isinstance(x, Tl) else x for x in writes]
        self._deps(eng, reads, writes)
        if signal:
            self.cnt[eng] += 1
            ev = (eng, self.cnt[eng])
            sem = self.sem[eng]
            self.thunks[eng].append(lambda e, fn=fn, sem=sem: fn(e).then_inc(sem, 1))
            self.unsig[eng] = False
        else:
            assert eng == "pe"
            ev = (eng, self.cnt[eng] + 1)
            self.thunks[eng].append(fn)
            self.unsig[eng] = True
        self._record(ev, reads, writes)

    def dma(self, eng, tl, out, in_, reads=(), writes=(), **kw):
        if tl.g is None:
            tl.g = self.grp()
        grp = tl.g
        reads = [x.b if isinstance(x, Tl) else x for x in reads]
        writes = [x.b if isinstance(x, Tl) else x for x in writes]
        self._deps(eng, reads, writes)
        grp.n += 1
        ev = (grp, grp.n * 16)
        sem = grp.sem
        self.thunks[eng].append(lambda e, out=out, in_=in_, sem=sem: e.dma_start(out=out, in_=in_, **kw).then_inc(sem, 16))
        self._record(ev, reads, writes)

    def flush(self):
        nc = self.nc
        for e in self.ENG:
            assert not self.unsig[e], (self.name, e)
        for g in self.grps:
            if g.n:
                self._wait("sp", g, g.n * 16)
        th = self.thunks
        with nc.Block() as block:
            @block.tensor
            def _(e):
                for f in th["pe"]:
                    f(e)

            @block.scalar
            def _(e):
                for f in th["act"]:
                    f(e)

            @block.vector
            def _(e):
                for f in th["dve"]:
                    f(e)

            @block.gpsimd
            def _(e):
                for f in th["pool"]:
                    f(e)

            @block.sync
            def _(e):
                for f in th["sp"]:
                    f(e)
        nc.clear_and_free_semaphores(self.all_sems)
        nc.all_engine_barrier()
        self.stack.close()


def rot(lst, i):
    return lst[i % len(lst)]


def emit_rstd(ph, ss, v, r, cst, scale=1.0):
    ph.op("dve", lambda e: e.tensor_scalar(out=v.t[:, 0:1], in0=ss.t[:, 0:1], scalar1=1.0 / D, scalar2=EPS,
                                           op0=ALU.mult, op1=ALU.add), reads=[ss], writes=[v])
    ph.op("pool", lambda e: e.tensor_tensor(out=r.t[:, 0:1], in0=v.t[:, 0:1], in1=cst["mhalf"].t[:, 0:1], op=ALU.pow),
          reads=[v, cst["mhalf"]], writes=[r])


def load_consts(ph, ident_d):
    cst = {}
    cst["ident"] = ph.sbuf([128, 128], BF16, "ident")
    cst["mhalf"] = ph.sbuf([128, 1], F32, "mhalf")
    ph.dma("pool", cst["ident"], cst["ident"].t[:], ident_d, writes=[cst["ident"]])
    ph.op("pool", lambda e: e.memset(cst["mhalf"].t[:], -0.5), writes=[cst["mhalf"]])
    return cst


def bcast_row(ap_row, n=128):
    return ap_row.partition_broadcast(n)


def ffn_phase(nc, name, x_src, x_dst, w_up, w_down, g_pre, g_post, ident_d, T=1024):
    ph = Phase(nc, name)
    cst = load_consts(ph, ident_d)
    NSB = NTOK // T
    TT = T // 128
    NB = T // 512
    NJ = DFF // 128
    JG = 2
    NG = NJ // JG
    w_up_v = w_up.rearrange("(kc p) n -> p kc n", p=128)
    w_dn_v = w_down.rearrange("(j p) n -> p j n", p=128)

    gpre = ph.sbuf([128, D], F32, "gpre")
    gpost = ph.sbuf([128, D], F32, "gpost")
    wd = ph.sbuf([128, NJ, D], BF16, "wd")
    hT = [ph.sbuf([128, 8, T], BF16, "hT") for _ in range(2)]
    gT = ph.sbuf([128, NJ, T], BF16, "gT")
    gTb = [Buf() for _ in range(NJ)]
    wu = [ph.sbuf([128, 8, 2 * JG * 128], BF16, "wu") for _ in range(3)]
    xt = [ph.sbuf([128, D], F32, "xt") for _ in range(3)]
    xr = [ph.sbuf([128, D], F32, "xr") for _ in range(2)]
    hb = [ph.sbuf([128, D], BF16, "hb") for _ in range(2)]
    junk = [ph.sbuf([128, D], BF16, "junk") for _ in range(2)]
    yb = [ph.sbuf([128, D], F32, "yb") for _ in range(2)]
    sa = [ph.sbuf([128, 512], F32, "sa") for _ in range(2)]
    st = [ph.sbuf([128, 4], F32, "st") for _ in range(4)]
    po = ph.psum([128, D], F32, "po")
    pT = [ph.psum([128, 8, 128], BF16, "pT") for _ in range(2)]
    pa = [ph.psum([128, 512], F32, "pa") for _ in range(2)]
    pb = [ph.psum([128, 512], F32, "pb") for _ in range(2)]

    ph.dma("sp", gpre, gpre.t[:], bcast_row(g_pre), writes=[gpre])
    ph.dma("sp", gpost, gpost.t[:], bcast_row(g_post), writes=[gpost])
    ph.op("dve", lambda e: e.tensor_scalar(out=gpost.t[:], in0=gpost.t[:], scalar1=0.5, scalar2=None, op0=ALU.mult),
          reads=[gpost], writes=[gpost])
    for q in range(0, NJ, 6):
        q1 = min(NJ, q + 6)
        ph.dma("pool", wd, wd.t[:, q:q1, :], w_dn_v[:, q:q1, :], writes=[wd])

    cnt = {"x": 0, "st": 0, "hb": 0, "pT": 0}

    def s1_gen(sb):
        for t in range(TT):
            r0 = sb * T + t * 128
            x = rot(xt, cnt["x"])
            cnt["x"] += 1
            s = rot(st, cnt["st"])
            cnt["st"] += 1
            jk = rot(junk, cnt["st"])
            h = rot(hb, cnt["hb"])
            cnt["hb"] += 1
            p = rot(pT, cnt["pT"])
            cnt["pT"] += 1
            ph.dma("sp", x, x.t[:], x_src[r0:r0 + 128, :], writes=[x])
            ph.op("act", lambda e, x=x, jk=jk, s=s: e.activation(out=jk.t[:], in_=x.t[:], func=AF.Square,
                                                                   accum_out=s.t[:, 0:1]),
                  reads=[x], writes=[jk, s])
            ph.op("dve", lambda e, s=s: e.tensor_scalar(out=s.t[:, 1:2], in0=s.t[:, 0:1], scalar1=1.0 / D, scalar2=EPS,
                                                        op0=ALU.mult, op1=ALU.add), reads=[s], writes=[s])
            ph.op("pool", lambda e, s=s: e.tensor_tensor(out=s.t[:, 2:3], in0=s.t[:, 1:2], in1=cst["mhalf"].t[:, 0:1],
                                                         op=ALU.pow), reads=[s, cst["mhalf"]], writes=[s])
            ph.op("dve", lambda e, x=x, s=s, h=h: e.scalar_tensor_tensor(out=h.t[:], in0=x.t[:], scalar=s.t[:, 2:3],
                                                                         in1=gpre.t[:], op0=ALU.mult, op1=ALU.mult),
                  reads=[x, s, gpre], writes=[h])
            for kc in range(8):
                ph.op("pe", lambda e, p=p, h=h, kc=kc: e.transpose(out=p.t[:, kc, :], in_=h.t[:, kc * 128:(kc + 1) * 128],
                                                                   identity=cst["ident"].t[:]),
                      reads=[h, cst["ident"]], writes=[p], signal=(kc == 7))
            hd = hT[sb % 2]
            ph.op("act", lambda e, p=p, hd=hd, t=t: e.activation(out=hd.t[:, :, t * 128:(t + 1) * 128], in_=p.t[:],
                                                                  func=AF.Copy),
                  reads=[p], writes=[hd])
            yield

    def s2_group(sb, jg, gi):
        w = rot(wu, gi)
        c0 = jg * JG * 128
        W = JG * 128
        ph.dma("pool", w, w.t[:, :, 0:W], w_up_v[:, :, c0:c0 + W], writes=[w])
        ph.dma("pool", w, w.t[:, :, W:2 * W], w_up_v[:, :, DFF + c0:DFF + c0 + W], writes=[w])
        hs = hT[sb % 2]
        k = 0
        for jj in range(JG):
            j = jg * JG + jj
            for nb in range(NB):
                a = rot(pa, gi * JG * NB + k)
                b = rot(pb, gi * JG * NB + k)
                s_ = rot(sa, gi * JG * NB + k)
                k += 1
                for kc in range(8):
                    ph.op("pe", lambda e, a=a, w=w, hs=hs, kc=kc, jj=jj, nb=nb: e.matmul(
                        a.t[:], w.t[:, kc, jj * 128:(jj + 1) * 128], hs.t[:, kc, nb * 512:(nb + 1) * 512],
                        start=(kc == 0), stop=(kc == 7)), reads=[w, hs], writes=[a], signal=(kc == 7))
                for kc in range(8):
                    ph.op("pe", lambda e, b=b, w=w, hs=hs, kc=kc, jj=jj, nb=nb: e.matmul(
                        b.t[:], w.t[:, kc, W + jj * 128:W + (jj + 1) * 128], hs.t[:, kc, nb * 512:(nb + 1) * 512],
                        start=(kc == 0), stop=(kc == 7)), reads=[w, hs], writes=[b], signal=(kc == 7))
                ph.op("act", lambda e, a=a, s_=s_: e.activation(out=s_.t[:], in_=a.t[:], func=AF.Silu),
                      reads=[a], writes=[s_])
                ph.op("dve", lambda e, b=b, s_=s_, j=j, nb=nb: e.tensor_tensor(
                    out=gT.t[:, j, nb * 512:(nb + 1) * 512], in0=s_.t[:], in1=b.t[:], op=ALU.mult),
                    reads=[s_, b], writes=[gTb[j]])

    def s3(sb):
        for t in range(TT):
            r0 = sb * T + t * 128
            x = rot(xr, t)
            y = rot(yb, t)
            s = rot(st, cnt["st"])
            cnt["st"] += 1
            jk = rot(junk, cnt["st"])
            ph.dma("sp", x, x.t[:], x_src[r0:r0 + 128, :], writes=[x])
            for hf in range(2):
                for j in range(NJ):
                    ph.op("pe", lambda e, j=j, hf=hf, t=t: e.matmul(
                        po.t[:, hf * 512:(hf + 1) * 512], gT.t[:, j, t * 128:(t + 1) * 128],
                        wd.t[:, j, hf * 512:(hf + 1) * 512], start=(j == 0), stop=(j == NJ - 1)),
                        reads=[gTb[j], wd], writes=[po], signal=(j == NJ - 1))
            ph.op("act", lambda e, y=y: e.activation(out=y.t[:], in_=po.t[:], func=AF.Copy), reads=[po], writes=[y])
            ph.op("act", lambda e, jk=jk, s=s, y=y: e.activation(out=jk.t[:], in_=y.t[:], func=AF.Square,
                                                                 accum_out=s.t[:, 0:1]), reads=[y], writes=[jk, s])
            ph.op("dve", lambda e, s=s: e.tensor_scalar(out=s.t[:, 1:2], in0=s.t[:, 0:1], scalar1=1.0 / D, scalar2=EPS,
                                                        op0=ALU.mult, op1=ALU.add), reads=[s], writes=[s])
            ph.op("pool", lambda e, s=s: e.tensor_tensor(out=s.t[:, 2:3], in0=s.t[:, 1:2], in1=cst["mhalf"].t[:, 0:1],
                                                         op=ALU.pow), reads=[s, cst["mhalf"]], writes=[s])
            ph.op("dve", lambda e, s=s, y=y: e.scalar_tensor_tensor(out=y.t[:], in0=y.t[:], scalar=s.t[:, 2:3],
                                                                    in1=gpost.t[:], op0=ALU.mult, op1=ALU.mult),
                  reads=[y, s, gpost], writes=[y])
            ph.op("pool", lambda e, x=x, y=y: e.tensor_tensor(out=x.t[:], in0=x.t[:], in1=y.t[:], op=ALU.add),
                  reads=[x, y], writes=[x])
            ph.dma("sp", x, x_dst[r0:r0 + 128, :], x.t[:], reads=[x])

    for _ in s1_gen(0):
        pass
    gi = 0
    for sb in range(NSB):
        nxt = s1_gen(sb + 1) if sb + 1 < NSB else None
        for jg in range(NG):
            s2_group(sb, jg, gi)
            gi += 1
            if nxt is not None:
                next(nxt, None)
        if nxt is not None:
            for _ in nxt:
                pass
        s3(sb)
    ph.flush()


TWO_PI = 6.283185307179586
C1 = 6.28125
C2 = TWO_PI - 6.28125
MAGIC = 12582912.0
PI_LO = 3.1415925


def rope_phase(nc, name, pos_d, ropec_d, cos_d, sin_d):
    ph = Phase(nc, name)
    posi = ph.sbuf([128, NTOK], I32, "posi")
    ang = ph.sbuf([128, NTOK], F32, "ang")
    t1 = ph.sbuf([128, NTOK], F32, "t1")
    t2 = ph.sbuf([128, NTOK], F32, "t2")
    rc = ph.sbuf([128, 2], F32, "rc")
    one = ph.sbuf([128, 1], F32, "one")
    ph.dma("sp", posi, posi.t[:], pos_d.partition_broadcast(128), writes=[posi])
    ph.dma("sp", rc, rc.t[:], ropec_d, writes=[rc])
    ph.op("pool", lambda e: e.memset(one.t[:], 1.0), writes=[one])
    ph.op("dve", lambda e: e.tensor_copy(out=ang.t[:], in_=posi.t[:]), reads=[posi], writes=[ang])
    ph.op("dve", lambda e: e.tensor_scalar(out=ang.t[:], in0=ang.t[:], scalar1=rc.t[:, 0:1], scalar2=None, op0=ALU.mult),
          reads=[ang, rc], writes=[ang])
    for which, dst, scale_ap in (("sin", sin_d, rc), ("cos", cos_d, one)):
        off = 0.0 if which == "sin" else TWO_PI / 4
        ph.op("dve", lambda e, off=off: e.tensor_scalar(out=t1.t[:], in0=ang.t[:], scalar1=off, scalar2=None, op0=ALU.add),
              reads=[ang], writes=[t1])
        ph.op("dve", lambda e: e.tensor_scalar(out=t2.t[:], in0=t1.t[:], scalar1=1.0 / TWO_PI, scalar2=MAGIC,
                                               op0=ALU.mult, op1=ALU.add), reads=[t1], writes=[t2])
        ph.op("dve", lambda e: e.tensor_scalar(out=t2.t[:], in0=t2.t[:], scalar1=-MAGIC, scalar2=None, op0=ALU.add),
              reads=[t2], writes=[t2])
        ph.op("dve", lambda e: e.scalar_tensor_tensor(out=t1.t[:], in0=t2.t[:], scalar=-C1, in1=t1.t[:],
                                                      op0=ALU.mult, op1=ALU.add), reads=[t1, t2], writes=[t1])
        ph.op("dve", lambda e: e.scalar_tensor_tensor(out=t1.t[:], in0=t2.t[:], scalar=-C2, in1=t1.t[:],
                                                      op0=ALU.mult, op1=ALU.add), reads=[t1, t2], writes=[t1])
        ph.op("dve", lambda e: e.tensor_scalar(out=t1.t[:], in0=t1.t[:], scalar1=-PI_LO, scalar2=PI_LO,
                                               op0=ALU.max, op1=ALU.min), reads=[t1], writes=[t1])
        sc = scale_ap.t[:, 1:2] if which == "sin" else scale_ap.t[:, 0:1]
        ph.op("act", lambda e, sc=sc: e.activation(out=t2.t[:], in_=t1.t[:], func=AF.Sin, scale=sc),
              reads=[t1, scale_ap], writes=[t2])
        ph.dma("sp", t2, dst, t2.t[:], reads=[t2])
    ph.flush()


def proj_phase(nc, name, x_src, g_pre, w_in, ident_d, cos_d, sin_d, h2T_d, qT_d, kown_d, vown_d, upT_d, qcaT_d, edge_d):
    ph = Phase(nc, name)
    cst = load_consts(ph, ident_d)
    w_in_v = w_in.rearrange("(kc p) n -> p kc n", p=128)
    gpre = ph.sbuf([128, D], F32, "gpre")
    win = ph.sbuf([128, 8, 2048], BF16, "win")
    wsw = ph.sbuf([128, 8, 1024], BF16, "wsw")
    hT = [ph.sbuf([128, 8, 512], BF16, "hT") for _ in range(2)]
    xt = [ph.sbuf([128, D], F32, "xt") for _ in range(3)]
    hb = [ph.sbuf([128, D], BF16, "hb") for _ in range(2)]
    junk = [ph.sbuf([128, D], BF16, "junk") for _ in range(2)]
    st = [ph.sbuf([128, 4], F32, "st") for _ in range(4)]
    cosb = [ph.sbuf([128, 512], F32, "cosb") for _ in range(2)]
    sinb = [ph.sbuf([128, 512], F32, "sinb") for _ in range(2)]
    r1 = [ph.sbuf([128, 512], F32, "r1") for _ in range(2)]
    r2 = [ph.sbuf([128, 512], F32, "r2") for _ in range(2)]
    ob = [ph.sbuf([128, 512], BF16, "ob") for _ in range(4)]
    of = [ph.sbuf([128, 512], F32, "of") for _ in range(2)]
    pT = [ph.psum([128, 8, 128], BF16, "pT") for _ in range(2)]
    pp = [ph.psum([128, 512], F32, "pp") for _ in range(6)]
    c = {"x": 0, "st": 0, "hb": 0, "pT": 0, "pp": 0, "ob": 0, "of": 0, "r": 0}

    ph.dma("sp", gpre, gpre.t[:], bcast_row(g_pre), writes=[gpre])
    for kc in range(8):
        ph.dma("pool", win, win.t[:, kc, :], w_in_v[:, kc, :], writes=[win])
    ph.op("pool", lambda e: e.memset(wsw.t[:], 0.0), writes=[wsw])
    src4 = win.t[:, :, 256:1280].rearrange("p k (b d) -> p k b d", d=64)
    dst4 = wsw.t[:].rearrange("p k (b d) -> p k b d", d=64)
    for kc in range(8):
        ph.op("pool", lambda e, kc=kc: e.tensor_copy(out=dst4[:, kc, :, 0:8], in_=src4[:, kc, :, 8:16]), reads=[win], writes=[wsw])
        ph.op("pool", lambda e, kc=kc: e.tensor_copy(out=dst4[:, kc, :, 8:16], in_=src4[:, kc, :, 0:8]), reads=[win], writes=[wsw])

    def pbank():
        p = rot(pp, c["pp"])
        c["pp"] += 1
        return p

    def fm_proj(p, wt, col0, hs):
        for kc in range(8):
            ph.op("pe", lambda e, kc=kc: e.matmul(p.t[:], wt.t[:, kc, col0:col0 + 128], hs.t[:, kc, :],
                                                  start=(kc == 0), stop=(kc == 7)), reads=[wt, hs], writes=[p], signal=(kc == 7))

    for blk in range(NTOK // 512):
        hs = hT[blk % 2]
        t0 = blk * 512
        for t in range(4):
            r0 = t0 + t * 128
            x = rot(xt, c["x"]); c["x"] += 1
            s = rot(st, c["st"]); c["st"] += 1
            jk = rot(junk, c["st"])
            h = rot(hb, c["hb"]); c["hb"] += 1
            p = rot(pT, c["pT"]); c["pT"] += 1
            ph.dma("sp", x, x.t[:], x_src[r0:r0 + 128, :], writes=[x])
            ph.op("act", lambda e, x=x, jk=jk, s=s: e.activation(out=jk.t[:], in_=x.t[:], func=AF.Square,
                                                                   accum_out=s.t[:, 0:1]), reads=[x], writes=[jk, s])
            ph.op("dve", lambda e, s=s: e.tensor_scalar(out=s.t[:, 1:2], in0=s.t[:, 0:1], scalar1=1.0 / D, scalar2=EPS,
                                                        op0=ALU.mult, op1=ALU.add), reads=[s], writes=[s])
            ph.op("pool", lambda e, s=s: e.tensor_tensor(out=s.t[:, 2:3], in0=s.t[:, 1:2], in1=cst["mhalf"].t[:, 0:1],
                                                         op=ALU.pow), reads=[s, cst["mhalf"]], writes=[s])
            ph.op("dve", lambda e, x=x, s=s, h=h: e.scalar_tensor_tensor(out=h.t[:], in0=x.t[:], scalar=s.t[:, 2:3],
                                                                         in1=gpre.t[:], op0=ALU.mult, op1=ALU.mult),
                  reads=[x, s, gpre], writes=[h])
            for kc in range(8):
                ph.op("pe", lambda e, p=p, h=h, kc=kc: e.transpose(out=p.t[:, kc, :], in_=h.t[:, kc * 128:(kc + 1) * 128],
                                                                   identity=cst["ident"].t[:]),
                      reads=[h, cst["ident"]], writes=[p], signal=(kc == 7))
            ph.op("act", lambda e, p=p, hs=hs, t=t: e.activation(out=hs.t[:, :, t * 128:(t + 1) * 128], in_=p.t[:],
                                                                  func=AF.Copy), reads=[p], writes=[hs])
        ph.dma("sp", hs, h2T_d[:, :, t0:t0 + 512], hs.t[:], reads=[hs])
        cb = cosb[blk % 2]
        sb_ = sinb[blk % 2]
        ph.dma("sp", cb, cb.t[:], cos_d[:, t0:t0 + 512], writes=[cb])
        ph.dma("sp", sb_, sb_.t[:], sin_d[:, t0:t0 + 512], writes=[sb_])
        for cc in range(2):
            p = pbank()
            fm_proj(p, win, cc * 128, hs)
            o = rot(of, c["of"]); c["of"] += 1
            ph.op("act", lambda e, p=p, o=o: e.activation(out=o.t[:], in_=p.t[:], func=AF.Copy), reads=[p], writes=[o])
            ph.dma("sp", o, upT_d[cc, :, t0:t0 + 512], o.t[:], reads=[o])
            if blk == 0:
                ph.dma("sp", o, edge_d[cc * 128:(cc + 1) * 128, 0:8], o.t[:, 0:8], reads=[o])
            if blk == NTOK // 512 - 1:
                ph.dma("sp", o, edge_d[cc * 128:(cc + 1) * 128, 8:16], o.t[:, 504:512], reads=[o])
        for which, dstd in (("q", qT_d), ("k", kown_d)):
            base = 256 if which == "q" else 768
            for hh in range(4):
                p = pbank()
                ps = pbank()
                fm_proj(p, win, base + hh * 128, hs)
                fm_proj(ps, wsw, (base - 256) + hh * 128, hs)
                a = rot(r1, c["r"]); b = rot(r2, c["r"]); c["r"] += 1
                o = rot(ob, c["ob"]); c["ob"] += 1
                ph.op("dve", lambda e, p=p, a=a, cb=cb: e.tensor_tensor(out=a.t[:], in0=p.t[:], in1=cb.t[:], op=ALU.mult),
                      reads=[p, cb], writes=[a])
                ph.op("dve", lambda e, ps=ps, b=b, sb_=sb_: e.tensor_tensor(out=b.t[:], in0=ps.t[:], in1=sb_.t[:], op=ALU.mult),
                      reads=[ps, sb_], writes=[b])
                ph.op("pool", lambda e, a=a, b=b, o=o: e.tensor_tensor(out=o.t[:], in0=a.t[:], in1=b.t[:], op=ALU.add),
                      reads=[a, b], writes=[o])
                ph.dma("sp", o, dstd[hh * 128:(hh + 1) * 128, t0:t0 + 512], o.t[:], reads=[o])
        for cc in range(2):
            p = pbank()
            fm_proj(p, win, 1792 + cc * 128, hs)
            o = rot(ob, c["ob"]); c["ob"] += 1
            ph.op("act", lambda e, p=p, o=o: e.activation(out=o.t[:], in_=p.t[:], func=AF.Copy), reads=[p], writes=[o])
            ph.dma("sp", o, qcaT_d[cc * 128:(cc + 1) * 128, t0:t0 + 512], o.t[:], reads=[o])
        for t in range(4):
            r0 = t0 + t * 128
            p = pbank()
            for kc in range(8):
                ph.op("pe", lambda e, p=p, kc=kc, t=t, hs=hs: e.matmul(p.t[:], hs.t[:, kc, t * 128:(t + 1) * 128],
                                                                  win.t[:, kc, 1280:1792], start=(kc == 0), stop=(kc == 7)),
                      reads=[win, hs], writes=[p], signal=(kc == 7))
            o = rot(ob, c["ob"]); c["ob"] += 1
            ph.op("act", lambda e, p=p, o=o: e.activation(out=o.t[:], in_=p.t[:], func=AF.Copy), reads=[p], writes=[o])
            ph.dma("sp", o, vown_d[r0:r0 + 128, :], o.t[:], reads=[o])
    ph.flush()


PAIRS = [[0, 1], [2, 3], [4, 5], [6, 7]]


def exchange_phase(nc, name, items):
    sems = [nc.alloc_semaphore(name=f"{name}_cc{i}") for i in range(len(items))]
    with nc.Block() as block:
        @block.gpsimd
        def _(g):
            for (src, dst), sem in zip(items, sems):
                g.collective_compute("AllGather", ALU.bypass, replica_groups=PAIRS, ins=[src], outs=[dst]).then_inc(sem)
            for sem in sems:
                g.wait_ge(sem, 1)
    nc.clear_and_free_semaphores(sems)
    nc.all_engine_barrier()


def attn_phase(nc, name, qT_d, kfull_d, vfull_d, lamq1, lamk1, lamq2, lamk2, subg, lam_init, ydaT_d):
    ph = Phase(nc, name)
    NKT = SEQ // 128
    kT = [ph.sbuf([128, SEQ], BF16, "kT") for _ in range(2)]
    vt = [ph.sbuf([128, NKT, 128], BF16, "vt") for _ in range(2)]
    ones = ph.sbuf([128, 128], BF16, "ones")
    epsb = ph.sbuf([128, 1], F32, "epsb")
    lam4 = ph.sbuf([128, 4, 64], F32, "lam4")
    lj = ph.sbuf([128, 64], F32, "lj")
    lc = ph.sbuf([128, 8], F32, "lc")
    gs = ph.sbuf([128, 1], F32, "gs")
    qb = [ph.sbuf([128, 512], BF16, "qb") for _ in range(2)]
    pe_ = [ph.sbuf([128, 512], BF16, "pe") for _ in range(6)]
    rr = [ph.sbuf([128, 512], F32, "rr") for _ in range(2)]
    o1 = [ph.sbuf([128, 512], F32, "o1") for _ in range(2)]
    o2 = [ph.sbuf([128, 512], F32, "o2") for _ in range(2)]
    sq = [ph.sbuf([128, 512], BF16, "sq") for _ in range(2)]
    rs = [ph.sbuf([128, 512], F32, "rs") for _ in range(2)]
    yo = [ph.sbuf([128, 512], BF16, "yo") for _ in range(2)]
    accO = [ph.psum([128, 512], F32, "accO") for _ in range(2)]
    accS = [ph.psum([128, 512], F32, "accS") for _ in range(2)]
    scp = [[ph.psum([128, 512], F32, "sc") for _ in range(2)] for _ in range(2)]

    ph.op("pool", lambda e: e.memset(ones.t[:], 1.0), writes=[ones])
    ph.op("pool", lambda e: e.memset(epsb.t[:], EPS), writes=[epsb])
    for i, v in enumerate((lamq1, lamk1, lamq2, lamk2)):
        ph.dma("sp", lam4, lam4.t[:, i, :], v.partition_broadcast(128), writes=[lam4])
    ph.dma("sp", gs, gs.t[:], subg.rearrange("(p o) -> p o", o=1), writes=[gs])
    for i in range(2):
        ph.op("dve", lambda e, i=i: e.scalar_tensor_tensor(out=lj.t[:], in0=lam4.t[:, 2 * i, :], scalar=1.0,
                                                            in1=lam4.t[:, 2 * i + 1, :], op0=ALU.mult, op1=ALU.mult,
                                                            accum_out=lc.t[:, i:i + 1]), reads=[lam4], writes=[lj, lc])
    ph.op("act", lambda e: e.activation(out=lc.t[:, 2:4], in_=lc.t[:, 0:2], func=AF.Exp), reads=[lc], writes=[lc])
    ph.op("dve", lambda e: e.tensor_tensor(out=lc.t[:, 4:5], in0=lc.t[:, 2:3], in1=lc.t[:, 3:4], op=ALU.subtract),
          reads=[lc], writes=[lc])
    ph.op("dve", lambda e: e.tensor_scalar(out=lc.t[:, 5:6], in0=lc.t[:, 4:5], scalar1=lam_init, scalar2=-1.0,
                                           op0=ALU.add, op1=ALU.mult), reads=[lc], writes=[lc])
    ph.op("dve", lambda e: e.tensor_scalar(out=lc.t[:, 6:7], in0=gs.t[:, 0:1], scalar1=1.0 - lam_init, scalar2=None,
                                           op0=ALU.mult), reads=[gs, lc], writes=[lc])
    cq = 0
    cp_ = 0
    for h in range(4):
        k = kT[h % 2]
        v = vt[h % 2]
        for r in range(2):
            ph.dma("sp", k, k.t[:, r * NTOK:(r + 1) * NTOK], kfull_d[h][r * 128:(r + 1) * 128, :], writes=[k])
            for pc in range(4):
                kt0 = r * 32 + pc * 8
                ph.dma("sp", v, v.t[:, kt0:kt0 + 8, :],
                       vfull_d[pc][r * 1024:(r + 1) * 1024, h * 128:(h + 1) * 128].rearrange("(kt p) e -> p kt e", p=128),
                       writes=[v])
        for blk in range(NTOK // 512):
            t0 = blk * 512
            q = rot(qb, cq)
            cq += 1
            ph.dma("sp", q, q.t[:], qT_d[h * 128:(h + 1) * 128, t0:t0 + 512], writes=[q])

            def score(kt, q=q, k=k):
                for c in range(2):
                    s_ = scp[c][kt % 2]
                    ph.op("pe", lambda e, s_=s_, c=c, kt=kt, q=q, k=k: e.matmul(
                        s_.t[:], k.t[c * 64:(c + 1) * 64, kt * 128:(kt + 1) * 128], q.t[c * 64:(c + 1) * 64, :],
                        start=True, stop=True), reads=[k, q], writes=[s_])

            score(0)
            for kt in range(NKT):
                if kt + 1 < NKT:
                    score(kt + 1)
                for c in range(2):
                    s_ = scp[c][kt % 2]
                    p = rot(pe_, cp_)
                    cp_ += 1
                    ph.op("act", lambda e, s_=s_, p=p: e.activation(out=p.t[:], in_=s_.t[:], func=AF.Exp, scale=0.125),
                          reads=[s_], writes=[p])
                    ph.op("pe", lambda e, c=c, kt=kt, p=p, v=v: e.matmul(accO[c].t[:], v.t[:, kt, :], p.t[:],
                                                                         start=(kt == 0), stop=(kt == NKT - 1)),
                          reads=[v, p], writes=[accO[c]], signal=False)
                    ph.op("pe", lambda e, c=c, kt=kt, p=p: e.matmul(accS[c].t[:], ones.t[:], p.t[:],
                                                                    start=(kt == 0), stop=(kt == NKT - 1)),
                          reads=[ones, p], writes=[accS[c]])
            i2 = blk % 2
            a1, a2, sq_, rs_, y = o1[i2], o2[i2], sq[i2], rs[i2], yo[i2]
            for c, a_ in ((0, a1), (1, a2)):
                r_ = rr[c]
                ph.op("dve", lambda e, r_=r_, c=c: e.reciprocal(out=r_.t[:], in_=accS[c].t[:]), reads=[accS[c]], writes=[r_])
                ph.op("dve", lambda e, r_=r_, a_=a_, c=c: e.tensor_tensor(out=a_.t[:], in0=accO[c].t[:], in1=r_.t[:], op=ALU.mult),
                      reads=[accO[c], r_], writes=[a_])
            ph.op("dve", lambda e, a1=a1, a2=a2: e.scalar_tensor_tensor(out=a1.t[:], in0=a2.t[:], scalar=lc.t[:, 5:6],
                                                                        in1=a1.t[:], op0=ALU.mult, op1=ALU.add),
                  reads=[a1, a2, lc], writes=[a1])
            ph.op("act", lambda e, a1=a1, sq_=sq_: e.activation(out=sq_.t[:], in_=a1.t[:], func=AF.Square),
                  reads=[a1], writes=[sq_])
            npz = accS[0]
            ph.op("pe", lambda e, sq_=sq_, npz=npz: e.matmul(npz.t[:], ones.t[:], sq_.t[:], start=True, stop=True),
                  reads=[ones, sq_], writes=[npz])
            ph.op("act", lambda e, rs_=rs_, npz=npz: e.activation(out=rs_.t[:], in_=npz.t[:], func=AF.Sqrt, scale=1.0 / 128,
                                                                   bias=epsb.t[:, 0:1]), reads=[npz, epsb], writes=[rs_])
            ph.op("dve", lambda e, rs_=rs_: e.reciprocal(out=rs_.t[:], in_=rs_.t[:]), reads=[rs_], writes=[rs_])
            ph.op("dve", lambda e, a1=a1, rs_=rs_, y=y: e.scalar_tensor_tensor(out=y.t[:], in0=a1.t[:], scalar=lc.t[:, 6:7],
                                                                               in1=rs_.t[:], op0=ALU.mult, op1=ALU.mult),
                  reads=[a1, rs_, lc], writes=[y])
            ph.dma("sp", y, ydaT_d[h * 128:(h + 1) * 128, t0:t0 + 512], y.t[:], reads=[y])
    ph.flush()


def merge_phase(nc, name, x_src, x_dst, ident_d, h2T_d, upT_d, edgefull_d, hmask_d, rcnt_d, qcaT_d, ydaT_d, mem_d,
                pool_w, pool_scale, mem_g, w_mem_kv, w_gate, b_gate, w_bp, w_bd, w_bc, w_out, g_post, dbg=False):
    ph = Phase(nc, name)

    def dump(nm, tl, shape, dt):
        if dbg:
            o = nc.dram_tensor("dscr_" + nm, shape, dt).ap()
            ph.dma("sp", tl, o, tl.t[:], reads=[tl])
            DBG_ITEMS.append((nm, o, shape, dt))
    cst = load_consts(ph, ident_d)
    wg = ph.sbuf([128, 8, 3072], BF16, "wg")
    wbp = ph.sbuf([128, 2, D], BF16, "wbp")
    wbd = ph.sbuf([128, 4, D], BF16, "wbd")
    wbc = ph.sbuf([128, 2, D], BF16, "wbc")
    wo = ph.sbuf([128, 8, D], BF16, "wo")
    wkv = ph.sbuf([128, 8, 512], BF16, "wkv")
    wblk = ph.sbuf([128, 2, 128], BF16, "wblk")
    bg = ph.sbuf([128, 24], F32, "bg")
    psc = ph.sbuf([128, 2], F32, "psc")
    hm = ph.sbuf([128, 2], F32, "hm")
    gpost = ph.sbuf([128, D], F32, "gpost")
    gmem = ph.sbuf([128, D], F32, "gmem")
    memT = ph.sbuf([128, 8, NMEM], BF16, "memT")
    kmT = ph.sbuf([128, 2, NMEM], BF16, "kmT")
    vpad = [ph.sbuf([128, 2, 2, 128], BF16, "vpad") for _ in range(2)]
    onesel = [ph.sbuf([128, 128], BF16, "onesel") for _ in range(2)]
    hT = [ph.sbuf([128, 8, 512], BF16, "hT") for _ in range(2)]
    U = [ph.sbuf([128, 2, 528], F32, "U") for _ in range(1)]
    A = [ph.sbuf([128, 2, 528], F32, "A") for _ in range(1)]
    Bt = [ph.sbuf([128, 2, 528], F32, "B") for _ in range(1)]
    rcn = [ph.sbuf([128, 2, 512], F32, "rcn") for _ in range(1)]
    pl = [ph.sbuf([128, 2, 512], BF16, "pl") for _ in range(1)]
    ypl = [ph.sbuf([128, 2, 512], BF16, "ypl") for _ in range(1)]
    qca = [ph.sbuf([128, 2, 512], BF16, "qca") for _ in range(1)]
    yca = [ph.sbuf([128, 2, 512], BF16, "yca") for _ in range(1)]
    yda = [ph.sbuf([128, 4, 512], BF16, "yda") for _ in range(1)]
    pex = [ph.sbuf([128, 512], BF16, "pex") for _ in range(3)]
    rcp = [ph.sbuf([128, 512], F32, "rcp") for _ in range(2)]
    tg = [ph.sbuf([128, 512], F32, "tg") for _ in range(4)]
    um = [ph.sbuf([128, 512], F32, "um") for _ in range(4)]
    mT = [ph.sbuf([128, 8, 512], BF16, "mT") for _ in range(1)]
    xt = [ph.sbuf([128, D], F32, "xt") for _ in range(2)]
    yb = [ph.sbuf([128, D], F32, "yb") for _ in range(2)]
    hb = [ph.sbuf([128, D], BF16, "hb") for _ in range(2)]
    junk = [ph.sbuf([128, D], BF16, "junk") for _ in range(1)]
    st = [ph.sbuf([128, 4], F32, "st") for _ in range(4)]
    po = ph.psum([128, D], F32, "po")
    pp = [ph.psum([128, 512], F32, "pp") for _ in range(6)]
    c = {"pp": 0, "x": 0, "st": 0, "pex": 0, "tg": 0, "um": 0}

    def pbank():
        p = rot(pp, c["pp"])
        c["pp"] += 1
        return p

    def wload(tl, src, n):
        v = src.rearrange("(kc p) n -> p kc n", p=128)
        for kc in range(n):
            ph.dma("pool", tl, tl.t[:, kc, :], v[:, kc, :], writes=[tl])

    wload(wg, w_gate, 8)
    wload(wbp, w_bp, 2)
    wload(wbd, w_bd, 4)
    wload(wbc, w_bc, 2)
    wload(wo, w_out, 8)
    wload(wkv, w_mem_kv, 8)
    for tl in (wbp, wbd, wbc):
        ph.op("pool", lambda e, tl=tl: e.tensor_scalar(out=tl.t[:], in0=tl.t[:], scalar1=0.5, scalar2=None, op0=ALU.mult),
              reads=[tl], writes=[tl])
    ph.dma("sp", bg, bg.t[:], b_gate.rearrange("(c p) -> p c", p=128), writes=[bg], allow_slow_non_contiguous=True)
    ph.op("dve", lambda e: e.tensor_scalar(out=bg.t[:], in0=bg.t[:], scalar1=0.5, scalar2=None, op0=ALU.mult),
          reads=[bg], writes=[bg])
    ph.dma("sp", psc, psc.t[:], pool_scale.rearrange("(c p) -> p c", p=128), writes=[psc], allow_slow_non_contiguous=True)
    ph.dma("sp", hm, hm.t[:], hmask_d, writes=[hm])
    ph.dma("sp", gpost, gpost.t[:], bcast_row(g_post), writes=[gpost])
    ph.dma("sp", gmem, gmem.t[:], bcast_row(mem_g), writes=[gmem])
    ph.op("pool", lambda e: e.memset(wblk.t[:], 0.0), writes=[wblk])
    for g in range(4):
        lo = (g % 2) * 64
        ph.dma("pool", wblk, wblk.t[lo:lo + 64, g // 2, lo:lo + 64], pool_w[g], writes=[wblk])
    for hh in range(2):
        ph.op("pool", lambda e, hh=hh: e.memset(onesel[hh].t[:], 0.0), writes=[onesel[hh]])
        ph.op("pool", lambda e, hh=hh: e.memset(onesel[hh].t[:, hh * 64:(hh + 1) * 64], 1.0), writes=[onesel[hh]])
        ph.op("pool", lambda e, hh=hh: e.memset(vpad[hh].t[:], 0.0), writes=[vpad[hh]])

    def norm_tile(x, g_t, out_t, src_t=None):
        s = rot(st, c["st"]); c["st"] += 1
        jk = rot(junk, c["st"])
        srcT = src_t if src_t is not None else x
        if src_t is not None:
            ph.op("act", lambda e: e.activation(out=out_t.t[:], in_=src_t.t[:], func=AF.Copy), reads=[src_t], writes=[out_t])
            srcT = out_t
        ph.op("act", lambda e: e.activation(out=jk.t[:], in_=srcT.t[:], func=AF.Square, accum_out=s.t[:, 0:1]),
              reads=[srcT], writes=[jk, s])
        ph.op("dve", lambda e: e.tensor_scalar(out=s.t[:, 1:2], in0=s.t[:, 0:1], scalar1=1.0 / D, scalar2=EPS,
                                               op0=ALU.mult, op1=ALU.add), reads=[s], writes=[s])
        ph.op("pool", lambda e: e.tensor_tensor(out=s.t[:, 2:3], in0=s.t[:, 1:2], in1=cst["mhalf"].t[:, 0:1], op=ALU.pow),
              reads=[s, cst["mhalf"]], writes=[s])
        ph.op("dve", lambda e: e.scalar_tensor_tensor(out=out_t.t[:], in0=srcT.t[:], scalar=s.t[:, 2:3], in1=g_t.t[:],
                                                      op0=ALU.mult, op1=ALU.mult), reads=[srcT, s, g_t], writes=[out_t])

    for m in range(2):
        x = rot(xt, c["x"]); c["x"] += 1
        h = hb[m]
        ph.dma("sp", x, x.t[:], mem_d[m * 128:(m + 1) * 128, :], writes=[x])
        norm_tile(x, gmem, h)
        for kc in range(8):
            pt = pbank()
            ph.op("pe", lambda e, pt=pt, h=h, kc=kc: e.transpose(out=pt.t[:].bitcast(BF16)[:, 0:128],
                                                                 in_=h.t[:, kc * 128:(kc + 1) * 128],
                                                                 identity=cst["ident"].t[:]),
                  reads=[h, cst["ident"]], writes=[pt])
            ph.op("act", lambda e, pt=pt, kc=kc, m=m: e.activation(out=memT.t[:, kc, m * 128:(m + 1) * 128],
                                                                    in_=pt.t[:].bitcast(BF16)[:, 0:128], func=AF.Copy),
                  reads=[pt], writes=[memT])
    for cc in range(2):
        p = pbank()
        for kc in range(8):
            ph.op("pe", lambda e, p=p, kc=kc, cc=cc: e.matmul(p.t[:, 0:NMEM], wkv.t[:, kc, cc * 128:(cc + 1) * 128],
                                                              memT.t[:, kc, :], start=(kc == 0), stop=(kc == 7)),
                  reads=[wkv, memT], writes=[p], signal=(kc == 7))
        ph.op("act", lambda e, p=p, cc=cc: e.activation(out=kmT.t[:, cc, :], in_=p.t[:, 0:NMEM], func=AF.Copy),
              reads=[p], writes=[kmT])
    for m in range(2):
        p = pbank()
        for kc in range(8):
            ph.op("pe", lambda e, p=p, kc=kc, m=m: e.matmul(p.t[:, 0:256], memT.t[:, kc, m * 128:(m + 1) * 128],
                                                            wkv.t[:, kc, 256:512], start=(kc == 0), stop=(kc == 7)),
                  reads=[wkv, memT], writes=[p], signal=(kc == 7))
        for cp in range(2):
            for hh in range(2):
                hd = 2 * cp + hh
                ph.op("act", lambda e, p=p, m=m, cp=cp, hh=hh, hd=hd: e.activation(
                    out=vpad[hh].t[:, m, cp, hh * 64:(hh + 1) * 64], in_=p.t[:, hd * 64:(hd + 1) * 64], func=AF.Copy),
                    reads=[p], writes=[vpad[hh]])

    NBLK = NTOK // 512
    for blk in range(NBLK):
        t0 = blk * 512
        i2 = blk % 2
        hs = hT[i2]
        ph.dma("sp", hs, hs.t[:], h2T_d[:, :, t0:t0 + 512], writes=[hs])
        u, a, b, rc_, pl_, ypl_ = U[0], A[0], Bt[0], rcn[0], pl[0], ypl[0]
        for cc in range(2):
            lo = max(t0 - 8, 0)
            hi = min(t0 + 520, NTOK)
            ph.dma("sp", u, u.t[:, cc, 8 - (t0 - lo):8 + (hi - t0)], upT_d[cc, :, lo:hi], writes=[u])
            if blk == 0:
                ph.dma("sp", u, u.t[:, cc, 0:8], edgefull_d[cc * 128:(cc + 1) * 128, 8:16], writes=[u])
            if blk == NBLK - 1:
                ph.dma("sp", u, u.t[:, cc, 520:528], edgefull_d[256 + cc * 128:256 + (cc + 1) * 128, 0:8], writes=[u])
            ph.dma("sp", rc_, rc_.t[:, cc, :], rcnt_d[cc, :, t0:t0 + 512], writes=[rc_])
        if blk == 0:
            ph.op("dve", lambda e, u=u: e.tensor_scalar(out=u.t[:, :, 0:8], in0=u.t[:, :, 0:8], scalar1=hm.t[:, 0:1],
                                                        scalar2=None, op0=ALU.mult), reads=[u, hm], writes=[u])
        if blk == NBLK - 1:
            ph.op("dve", lambda e, u=u: e.tensor_scalar(out=u.t[:, :, 520:528], in0=u.t[:, :, 520:528], scalar1=hm.t[:, 1:2],
                                                        scalar2=None, op0=ALU.mult), reads=[u, hm], writes=[u])
        ph.op("pool", lambda e, u=u, a=a: e.tensor_tensor(out=a.t[:, :, 0:527], in0=u.t[:, :, 0:527], in1=u.t[:, :, 1:528],
                                                          op=ALU.add), reads=[u], writes=[a])
        ph.op("pool", lambda e, a=a, b=b: e.tensor_tensor(out=b.t[:, :, 0:525], in0=a.t[:, :, 0:525], in1=a.t[:, :, 2:527],
                                                          op=ALU.add), reads=[a], writes=[b])
        ph.op("dve", lambda e, a=a, rc_=rc_: e.tensor_tensor(out=rc_.t[0:64, 0, :], in0=a.t[0:64, 0, 7:519],
                                                             in1=rc_.t[0:64, 0, :], op=ALU.mult), reads=[a, rc_], writes=[rc_])
        ph.op("dve", lambda e, b=b, rc_=rc_: e.tensor_tensor(out=rc_.t[64:128, 0, :], in0=b.t[64:128, 0, 6:518],
                                                             in1=rc_.t[64:128, 0, :], op=ALU.mult), reads=[b, rc_], writes=[rc_])
        ph.op("pool", lambda e, a=a, b=b: e.tensor_tensor(out=a.t[:, 1, 0:521], in0=b.t[:, 1, 0:521], in1=b.t[:, 1, 4:525],
                                                          op=ALU.add), reads=[b], writes=[a])
        ph.op("pool", lambda e, a=a, b=b: e.tensor_tensor(out=b.t[64:128, 1, 0:513], in0=a.t[64:128, 1, 0:513],
                                                          in1=a.t[64:128, 1, 8:521], op=ALU.add), reads=[a], writes=[b])
        ph.op("dve", lambda e, a=a, rc_=rc_: e.tensor_tensor(out=rc_.t[0:64, 1, :], in0=a.t[0:64, 1, 4:516],
                                                             in1=rc_.t[0:64, 1, :], op=ALU.mult), reads=[a, rc_], writes=[rc_])
        ph.op("dve", lambda e, b=b, rc_=rc_: e.tensor_tensor(out=rc_.t[64:128, 1, :], in0=b.t[64:128, 1, 0:512],
                                                             in1=rc_.t[64:128, 1, :], op=ALU.mult), reads=[b, rc_], writes=[rc_])
        ph.op("dve", lambda e, u=u, rc_=rc_, pl_=pl_: e.tensor_tensor(out=pl_.t[:], in0=rc_.t[:], in1=u.t[:, :, 8:520],
                                                                      op=ALU.subtract), reads=[u, rc_], writes=[pl_])
        for cc in range(2):
            p = pbank()
            ph.op("pe", lambda e, p=p, cc=cc, pl_=pl_: e.matmul(p.t[:], wblk.t[:, cc, :], pl_.t[:, cc, :], start=True, stop=True),
                  reads=[wblk, pl_], writes=[p])
            ph.op("act", lambda e, p=p, cc=cc, ypl_=ypl_: e.activation(out=ypl_.t[:, cc, :], in_=p.t[:], func=AF.Identity,
                                                                        scale=psc.t[:, cc:cc + 1]), reads=[p, psc], writes=[ypl_])
        qc, yc = qca[0], yca[0]
        for cc in range(2):
            ph.dma("sp", qc, qc.t[:, cc, :], qcaT_d[cc * 128:(cc + 1) * 128, t0:t0 + 512], writes=[qc])
        for cp in range(2):
            pO = pbank()
            pS = pbank()
            n = 0
            for hh in range(2):
                for m in range(2):
                    ps_ = pbank()
                    ph.op("pe", lambda e, ps_=ps_, hh=hh, m=m, cp=cp, qc=qc: e.matmul(
                        ps_.t[:], kmT.t[hh * 64:(hh + 1) * 64, cp, m * 128:(m + 1) * 128], qc.t[hh * 64:(hh + 1) * 64, cp, :],
                        start=True, stop=True), reads=[kmT, qc], writes=[ps_])
                    px = rot(pex, c["pex"]); c["pex"] += 1
                    ph.op("act", lambda e, ps_=ps_, px=px: e.activation(out=px.t[:], in_=ps_.t[:], func=AF.Exp, scale=0.125),
                          reads=[ps_], writes=[px])
                    ph.op("pe", lambda e, pO=pO, hh=hh, m=m, cp=cp, px=px, n=n: e.matmul(
                        pO.t[:], vpad[hh].t[:, m, cp, :], px.t[:], start=(n == 0), stop=(n == 3)),
                        reads=[vpad[hh], px], writes=[pO], signal=False)
                    ph.op("pe", lambda e, pS=pS, hh=hh, px=px, n=n: e.matmul(
                        pS.t[:], onesel[hh].t[:], px.t[:], start=(n == 0), stop=(n == 3)),
                        reads=[onesel[hh], px], writes=[pS])
                    n += 1
            r_ = rcp[cp]
            ph.op("dve", lambda e, r_=r_, pS=pS: e.reciprocal(out=r_.t[:], in_=pS.t[:]), reads=[pS], writes=[r_])
            ph.op("dve", lambda e, r_=r_, pO=pO, cp=cp, yc=yc: e.tensor_tensor(out=yc.t[:, cp, :], in0=pO.t[:], in1=r_.t[:],
                                                                                op=ALU.mult), reads=[pO, r_], writes=[yc])
        yd = yda[0]
        for hh in range(4):
            ph.dma("sp", yd, yd.t[:, hh, :], ydaT_d[hh * 128:(hh + 1) * 128, t0:t0 + 512], writes=[yd])
        mt = mT[0]
        for oc in range(8):
            brs = []
            for (wt, src, nk) in ((wbp, ypl_, 2), (wbd, yd, 4), (wbc, yc, 2)):
                p = pbank()
                for kc in range(nk):
                    ph.op("pe", lambda e, p=p, wt=wt, src=src, kc=kc, oc=oc, nk=nk: e.matmul(
                        p.t[:], wt.t[:, kc, oc * 128:(oc + 1) * 128], src.t[:, kc, :], start=(kc == 0), stop=(kc == nk - 1)),
                        reads=[wt, src], writes=[p], signal=(kc == nk - 1))
                brs.append(p)
            us = []
            for x_ in range(3):
                p = pbank()
                col = x_ * 1024 + oc * 128
                for kc in range(8):
                    ph.op("pe", lambda e, p=p, kc=kc, col=col, hs=hs: e.matmul(p.t[:], wg.t[:, kc, col:col + 128], hs.t[:, kc, :],
                                                                                start=(kc == 0), stop=(kc == 7)),
                          reads=[wg, hs], writes=[p], signal=(kc == 7))
                t_ = rot(tg, c["tg"]); c["tg"] += 1
                u_ = rot(um, c["um"]); c["um"] += 1
                bi = x_ * 8 + oc
                ph.op("act", lambda e, p=p, t_=t_, bi=bi: e.activation(out=t_.t[:], in_=p.t[:], func=AF.Tanh,
                                                                        bias=bg.t[:, bi:bi + 1], scale=0.5), reads=[p, bg], writes=[t_])
                br = brs[x_]
                ph.op("dve", lambda e, t_=t_, u_=u_, br=br: e.scalar_tensor_tensor(out=u_.t[:], in0=t_.t[:], scalar=1.0, in1=br.t[:],
                                                                                   op0=ALU.add, op1=ALU.mult), reads=[t_, br], writes=[u_])
                us.append(u_)
            ph.op("pool", lambda e, us=us: e.tensor_tensor(out=us[0].t[:], in0=us[0].t[:], in1=us[1].t[:], op=ALU.add),
                  reads=[us[0], us[1]], writes=[us[0]])
            ph.op("pool", lambda e, us=us, mt=mt, oc=oc: e.tensor_tensor(out=mt.t[:, oc, :], in0=us[0].t[:], in1=us[2].t[:], op=ALU.add),
                  reads=[us[0], us[2]], writes=[mt])
        if blk == 0:
            dump("memT", memT, [128, 8, NMEM], BF16)
            dump("kmT", kmT, [128, 2, NMEM], BF16)
            dump("vpad0", vpad[0], [128, 2, 2, 128], BF16)
            dump("vpad1", vpad[1], [128, 2, 2, 128], BF16)
            dump("pl", pl_, [128, 2, 512], BF16)
            dump("ypl", ypl_, [128, 2, 512], BF16)
            dump("yca", yc, [128, 2, 512], BF16)
            dump("mt", mt, [128, 8, 512], BF16)
            dump("U", u, [128, 2, 528], F32)
        for t in range(4):
            r0 = t0 + t * 128
            x = rot(xt, c["x"]); c["x"] += 1
            y = rot(yb, t)
            ph.dma("sp", x, x.t[:], x_src[r0:r0 + 128, :], writes=[x])
            for hf in range(2):
                for kc in range(8):
                    ph.op("pe", lambda e, kc=kc, hf=hf, t=t, mt=mt: e.matmul(
                        po.t[:, hf * 512:(hf + 1) * 512], mt.t[:, kc, t * 128:(t + 1) * 128], wo.t[:, kc, hf * 512:(hf + 1) * 512],
                        start=(kc == 0), stop=(kc == 7)), reads=[mt, wo], writes=[po], signal=(kc == 7))
            norm_tile(x, gpost, y, src_t=po)
            ph.op("pool", lambda e, x=x, y=y: e.tensor_tensor(out=x.t[:], in0=x.t[:], in1=y.t[:], op=ALU.add),
                  reads=[x, y], writes=[x])
            ph.dma("sp", x, x_dst[r0:r0 + 128, :], x.t[:], reads=[x])
    ph.flush()


W_NAMES = ["ffn1_pre_g", "ffn1_w_up", "ffn1_w_down", "ffn1_post_g", "mix_pre_g", "w_in", "pool_w", "pool_scale",
           "da_lambda_q1", "da_lambda_k1", "da_lambda_q2", "da_lambda_k2", "da_subln_g", "mem_norm_g", "w_mem_kv",
           "w_gate", "b_gate", "w_br_pool", "w_br_da", "w_br_ca", "w_out", "mix_post_g", "ffn2_pre_g", "ffn2_w_up",
           "ffn2_w_down", "ffn2_post_g"]
W_SHAPES = {"ffn1_pre_g": [D], "ffn1_w_up": [D, 2 * DFF], "ffn1_w_down": [DFF, D], "ffn1_post_g": [D], "mix_pre_g": [D],
            "w_in": [D, 2048], "pool_w": [4, 64, 64], "pool_scale": [256], "da_lambda_q1": [64], "da_lambda_k1": [64],
            "da_lambda_q2": [64], "da_lambda_k2": [64], "da_subln_g": [128], "mem_norm_g": [D], "w_mem_kv": [D, 512],
            "w_gate": [D, 3072], "b_gate": [3072], "w_br_pool": [256, D], "w_br_da": [512, D], "w_br_ca": [256, D],
            "w_out": [D, D], "mix_post_g": [D], "ffn2_pre_g": [D], "ffn2_w_up": [D, 2 * DFF], "ffn2_w_down": [DFF, D],
            "ffn2_post_g": [D]}


DBG_ITEMS = []


def debug_dump(nc, items):
    sem = nc.alloc_semaphore(name="dbg_sem")
    with nc.Block() as block:
        @block.sync
        def _(e):
            n = 0
            for name, ap, shape, dt in items:
                out = nc.dram_tensor("dbg_" + name, shape, dt, kind="ExternalOutput").ap()
                rows = shape[0]
                step = max(1, rows // 8)
                for r in range(0, rows, step):
                    e.dma_start(out=out[r:r + step], in_=ap[r:r + step]).then_inc(sem, 16)
                    n += 1
            e.wait_ge(sem, 16 * n)
    nc.clear_and_free_semaphores([sem])
    nc.all_engine_barrier()


def build(depth=DEPTH, ffn_T=1024, debug=False):
    import math
    nc = bass.Bass("TRN2", target_bir_lowering=False)
    x_in = nc.dram_tensor("x", [NTOK, D], F32, kind="ExternalInput").ap()
    mem_d = nc.dram_tensor("mem", [NMEM, D], F32, kind="ExternalInput").ap()
    pos_d = nc.dram_tensor("positions", [NTOK], I32, kind="ExternalInput").ap()
    ident_d = nc.dram_tensor("ident", [128, 128], F32, kind="ExternalInput").ap()
    ropec_d = nc.dram_tensor("ropec", [128, 2], F32, kind="ExternalInput").ap()
    hmask_d = nc.dram_tensor("hmask", [128, 2], F32, kind="ExternalInput").ap()
    rcnt_d = nc.dram_tensor("rcnt", [2, 128, NTOK], F32, kind="ExternalInput").ap()
    W = {}
    for n in W_NAMES:
        W[n] = nc.dram_tensor(n, [depth] + W_SHAPES[n], F32, kind="ExternalInput").ap()
    y_out = nc.dram_tensor("y", [NTOK, D], F32, kind="ExternalOutput").ap()

    def scr(name, shape, dt):
        return nc.dram_tensor(name, shape, dt).ap()

    xs = [scr(f"xs{i}", [NTOK, D], F32) for i in range(3)]
    cos_d = scr("cos_t", [128, NTOK], F32)
    sin_d = scr("sin_t", [128, NTOK], F32)
    h2T_d = scr("h2T", [128, 8, NTOK], BF16)
    qT_d = scr("qT", [512, NTOK], BF16)
    kown_d = scr("kown", [512, NTOK], BF16)
    vown_d = scr("vown", [NTOK, 512], BF16)
    upT_d = scr("upT", [2, 128, NTOK], F32)
    qcaT_d = scr("qcaT", [256, NTOK], BF16)
    edge_d = scr("edge", [256, 16], F32)
    edgefull_d = scr("edgefull", [512, 16], F32)
    kfull_d = [scr(f"kfull{h}", [256, NTOK], BF16) for h in range(4)]
    vfull_d = [scr(f"vfull{p}", [2048, 512], BF16) for p in range(4)]
    ydaT_d = scr("ydaT", [512, NTOK], BF16)

    rope_phase(nc, "R", pos_d, ropec_d, cos_d, sin_d)
    cur = x_in
    for l in range(depth):
        lam_init = 0.8 - 0.6 * math.exp(-0.3 * l)
        ffn_phase(nc, f"A{l}", cur, xs[0], W["ffn1_w_up"][l], W["ffn1_w_down"][l], W["ffn1_pre_g"][l], W["ffn1_post_g"][l],
                  ident_d, T=ffn_T)
        proj_phase(nc, f"B{l}", xs[0], W["mix_pre_g"][l], W["w_in"][l], ident_d, cos_d, sin_d, h2T_d, qT_d, kown_d, vown_d,
                   upT_d, qcaT_d, edge_d)
        items = [(kown_d[h * 128:(h + 1) * 128, :], kfull_d[h]) for h in range(4)]
        items += [(vown_d[p * 1024:(p + 1) * 1024, :], vfull_d[p]) for p in range(4)]
        items += [(edge_d, edgefull_d)]
        exchange_phase(nc, f"X{l}", items)
        attn_phase(nc, f"C{l}", qT_d, kfull_d, vfull_d, W["da_lambda_q1"][l], W["da_lambda_k1"][l], W["da_lambda_q2"][l],
                   W["da_lambda_k2"][l], W["da_subln_g"][l], lam_init, ydaT_d)
        merge_phase(nc, f"M{l}", xs[0], xs[1], ident_d, h2T_d, upT_d, edgefull_d, hmask_d, rcnt_d, qcaT_d, ydaT_d, mem_d,
                    W["pool_w"][l], W["pool_scale"][l], W["mem_norm_g"][l], W["w_mem_kv"][l], W["w_gate"][l], W["b_gate"][l],
                    W["w_br_pool"][l], W["w_br_da"][l], W["w_br_ca"][l], W["w_out"][l], W["mix_post_g"][l], dbg=False)
        dst = y_out if l == depth - 1 else xs[2]
        ffn_phase(nc, f"D{l}", xs[1], dst, W["ffn2_w_up"][l], W["ffn2_w_down"][l], W["ffn2_pre_g"][l], W["ffn2_post_g"][l],
                  ident_d, T=ffn_T)
        cur = xs[2]
    if debug:
        debug_dump(nc, [("x1", xs[0], [NTOK, D], F32), ("x2", xs[1], [NTOK, D], F32), ("cos", cos_d, [128, NTOK], F32),
                        ("sin", sin_d, [128, NTOK], F32), ("h2T", h2T_d, [128, 8, NTOK], BF16), ("qT", qT_d, [512, NTOK], BF16),
                        ("kfull0", kfull_d[0], [256, NTOK], BF16), ("vfull0", vfull_d[0], [2048, 512], BF16),
                        ("upT", upT_d, [2, 128, NTOK], F32), ("qcaT", qcaT_d, [256, NTOK], BF16),
                        ("edgefull", edgefull_d, [512, 16], F32), ("ydaT", ydaT_d, [512, NTOK], BF16)] + DBG_ITEMS)
    return nc


def host_consts():
    ident = np.eye(128, dtype=np.float32)
    inv = (np.float32(500000.0) ** (-np.arange(0, 16, 2, dtype=np.float32) / np.float32(16))).astype(np.float32)
    ropec = np.zeros((128, 2), np.float32)
    for p in range(128):
        d = p % 64
        if d < 16:
            ropec[p, 0] = inv[d % 8]
            ropec[p, 1] = -1.0 if d < 8 else 1.0
    rc = []
    pos = np.arange(SEQ)
    for w in (2, 4, 8, 16):
        lo = np.clip(pos - w // 2, 0, SEQ)
        hi = np.clip(pos + w // 2, 0, SEQ)
        rc.append(np.repeat((1.0 / (hi - lo).astype(np.float32))[None, :], 64, axis=0))
    rcnt = np.concatenate(rc, axis=0).astype(np.float32).reshape(2, 128, SEQ)
    return ident, ropec, rcnt


def make_in_maps(inputs, depth=DEPTH, ncores=NCORES):
    ident, ropec, rcnt = host_consts()
    x = np.asarray(inputs["x"])
    mem = np.asarray(inputs["mem"])
    pos = np.asarray(inputs["positions"])
    maps = []
    for core in range(ncores):
        b, j = core // 2, core % 2
        m = {"x": np.ascontiguousarray(x[b, j * NTOK:(j + 1) * NTOK]),
             "mem": np.ascontiguousarray(mem[b]),
             "positions": np.ascontiguousarray(pos[b, j * NTOK:(j + 1) * NTOK]).astype(np.int32),
             "ident": ident, "ropec": ropec,
             "hmask": np.tile(np.array([[1.0 if j == 1 else 0.0, 1.0 if j == 0 else 0.0]], np.float32), (128, 1)),
             "rcnt": np.ascontiguousarray(rcnt[:, :, j * NTOK:(j + 1) * NTOK])}
        for n in W_NAMES:
            m[n] = np.ascontiguousarray(np.asarray(inputs[n])[:depth])
        maps.append(m)
    return maps


_NC_CACHE = {}


def kernel(**inputs):
    if "nc" not in _NC_CACHE:
        _NC_CACHE["nc"] = build()
    nc = _NC_CACHE["nc"]
    maps = make_in_maps(inputs)
    res = run_bass_kernel_spmd(nc, maps, core_ids=list(range(NCORES)))
    out = np.empty((4, SEQ, D), np.float32)
    for core in range(NCORES):
        b, j = core // 2, core % 2
        out[b, j * NTOK:(j + 1) * NTOK] = res.results[core]["y"]
    return out
```

```python
import numpy as np
from contextlib import ExitStack
import concourse.bass as bass
import concourse.mybir as mybir
from concourse.bass_utils import run_bass_kernel_spmd

F32 = mybir.dt.float32
BF16 = mybir.dt.bfloat16
I32 = mybir.dt.int32
AF = mybir.ActivationFunctionType
ALU = mybir.AluOpType

D = 1024
DFF = 2816
NTOK = 4096
SEQ = 8192
DEPTH = 4
NMEM = 256
EPS = 1e-6
NCORES = 8


class Buf:
    __slots__ = ("w", "r")

    def __init__(self):
        self.w = None
        self.r = {}


class Tl:
    def __init__(self, t):
        self.t = t
        self.b = Buf()
        self.g = None


class Grp:
    def __init__(self, sem):
        self.sem = sem
        self.n = 0


class Phase:
    ENG = ("pe", "act", "dve", "pool", "sp")

    def __init__(self, nc, name):
        self.nc = nc
        self.name = name
        self.stack = ExitStack()
        self.all_sems = []
        self.sem = {e: self._sem(f"{name}_{e}") for e in self.ENG}
        self.cnt = {e: 0 for e in self.ENG}
        self.thunks = {e: [] for e in self.ENG}
        self.seen = {e: {} for e in self.ENG}
        self.unsig = {e: False for e in self.ENG}
        self.grps = []
        self.nt = 0

    def _sem(self, name):
        h = self.nc.alloc_semaphore(name=name)
        self.all_sems.append(h)
        return h

    def sbuf(self, shape, dt, name=None):
        self.nt += 1
        return Tl(self.stack.enter_context(self.nc.sbuf_tensor(f"{self.name}_{name or 't'}{self.nt}", list(shape), dt)))

    def psum(self, shape, dt, name=None):
        self.nt += 1
        return Tl(self.stack.enter_context(self.nc.psum_tensor(f"{self.name}_{name or 'p'}{self.nt}", list(shape), dt)))

    def grp(self):
        self.nt += 1
        g = Grp(self._sem(f"{self.name}_g{self.nt}"))
        self.grps.append(g)
        return g

    def _wait(self, eng, key, val):
        if key == eng and eng == "pe":
            return
        s = self.seen[eng]
        if s.get(key, 0) >= val:
            return
        s[key] = val
        sem = self.sem[key] if isinstance(key, str) else key.sem
        self.thunks[eng].append(lambda e, sem=sem, val=val: e.wait_ge(sem, val))

    def _deps(self, eng, reads, writes):
        for b in reads:
            if b.w is not None:
                self._wait(eng, *b.w)
        for b in writes:
            if b.w is not None:
                self._wait(eng, *b.w)
            for k, v in b.r.items():
                self._wait(eng, k, v)

    def _record(self, ev, reads, writes):
        k, v = ev
        for b in reads:
            if b.r.get(k, 0) < v:
                b.r[k] = v
        for b in writes:
            b.w = ev
            b.r = {}

    def op(self, eng, fn, reads=(), writes=(), signal=True):
        reads = [x.b if isinstance(x, Tl) else x for x in reads]
        writes = [x.b if isinstance(x, Tl) else x for x in writes]
        self._deps(eng, reads, writes)
        if signal:
            self.cnt[eng] += 1
            ev = (eng, self.cnt[eng])
            sem = self.sem[eng]
            self.thunks[eng].append(lambda e, fn=fn, sem=sem: fn(e).then_inc(sem, 1))
            self.unsig[eng] = False
        else:
            assert eng == "pe"
            ev = (eng, self.cnt[eng] + 1)
            self.thunks[eng].append(fn)
            self.unsig[eng] = True
        self._record(ev, reads, writes)

    def dma(self, eng, tl, out, in_, reads=(), writes=(), **kw):
        if tl.g is None:
            tl.g = self.grp()
        grp = tl.g
        reads = [x.b if isinstance(x, Tl) else x for x in reads]
        writes = [x.b if isinstance(x, Tl) else x for x in writes]
        self._deps(eng, reads, writes)
        grp.n += 1
        ev = (grp, grp.n * 16)
        sem = grp.sem
        self.thunks[eng].append(lambda e, out=out, in_=in_, sem=sem: e.dma_start(out=out, in_=in_, **kw).then_inc(sem, 16))
        self._record(ev, reads, writes)

    def flush(self):
        nc = self.nc
        for e in self.ENG:
            assert not self.unsig[e], (self.name, e)
        for g in self.grps:
            if g.n:
                self._wait("sp", g, g.n * 16)
        th = self.thunks
        with nc.Block() as block:
            @block.tensor
            def _(e):
                for f in th["pe"]:
                    f(e)

            @block.scalar
            def _(e):
                for f in th["act"]:
                    f(e)

            @block.vector
            def _(e):
                for f in th["dve"]:
                    f(e)

            @block.gpsimd
            def _(e):
                for f in th["pool"]:
                    f(e)

            @block.sync
            def _(e):
                for f in th["sp"]:
                    f(e)
        nc.clear_and_free_semaphores(self.all_sems)
        nc.all_engine_barrier()
        self.stack.close()


def rot(lst, i):
    return lst[i % len(lst)]


def emit_rstd(ph, ss, v, r, cst, scale=1.0):
    ph.op("dve", lambda e: e.tensor_scalar(out=v.t[:, 0:1], in0=ss.t[:, 0:1], scalar1=1.0 / D, scalar2=EPS,
                                           op0=ALU.mult, op1=ALU.add), reads=[ss], writes=[v])
    ph.op("pool", lambda e: e.tensor_tensor(out=r.t[:, 0:1], in0=v.t[:, 0:1], in1=cst["mhalf"].t[:, 0:1], op=ALU.pow),
          reads=[v, cst["mhalf"]], writes=[r])


def load_consts(ph, ident_d):
    cst = {}
    cst["ident"] = ph.sbuf([128, 128], BF16, "ident")
    cst["mhalf"] = ph.sbuf([128, 1], F32, "mhalf")
    ph.dma("pool", cst["ident"], cst["ident"].t[:], ident_d, writes=[cst["ident"]])
    ph.op("pool", lambda e: e.memset(cst["mhalf"].t[:], -0.5), writes=[cst["mhalf"]])
    return cst


def bcast_row(ap_row, n=128):
    return ap_row.partition_broadcast(n)


def ffn_phase(nc, name, x_src, x_dst, w_up, w_down, g_pre, g_post, ident_d, T=1024):
    ph = Phase(nc, name)
    cst = load_consts(ph, ident_d)
    NSB = NTOK // T
    TT = T // 128
    NB = T // 512
    NJ = DFF // 128
    JG = 2
    NG = NJ // JG
    w_up_v = w_up.rearrange("(kc p) n -> p kc n", p=128)
    w_dn_v = w_down.rearrange("(j p) n -> p j n", p=128)

    gpre = ph.sbuf([128, D], F32, "gpre")
    gpost = ph.sbuf([128, D], F32, "gpost")
    wd = ph.sbuf([128, NJ, D], BF16, "wd")
    hT = [ph.sbuf([128, 8, T], BF16, "hT") for _ in range(2)]
    gT = ph.sbuf([128, NJ, T], BF16, "gT")
    gTb = [Buf() for _ in range(NJ)]
    wu = [ph.sbuf([128, 8, 2 * JG * 128], BF16, "wu") for _ in range(3)]
    xt = [ph.sbuf([128, D], F32, "xt") for _ in range(3)]
    xr = [ph.sbuf([128, D], F32, "xr") for _ in range(2)]
    hb = [ph.sbuf([128, D], BF16, "hb") for _ in range(2)]
    junk = [ph.sbuf([128, D], BF16, "junk") for _ in range(2)]
    yb = [ph.sbuf([128, D], F32, "yb") for _ in range(2)]
    sa = [ph.sbuf([128, 512], F32, "sa") for _ in range(2)]
    st = [ph.sbuf([128, 4], F32, "st") for _ in range(4)]
    po = ph.psum([128, D], F32, "po")
    pT = [ph.psum([128, 8, 128], BF16, "pT") for _ in range(2)]
    pa = [ph.psum([128, 512], F32, "pa") for _ in range(2)]
    pb = [ph.psum([128, 512], F32, "pb") for _ in range(2)]

    ph.dma("sp", gpre, gpre.t[:], bcast_row(g_pre), writes=[gpre])
    ph.dma("sp", gpost, gpost.t[:], bcast_row(g_post), writes=[gpost])
    ph.op("dve", lambda e: e.tensor_scalar(out=gpost.t[:], in0=gpost.t[:], scalar1=0.5, scalar2=None, op0=ALU.mult),
          reads=[gpost], writes=[gpost])
    for q in range(0, NJ, 6):
        q1 = min(NJ, q + 6)
        ph.dma("pool", wd, wd.t[:, q:q1, :], w_dn_v[:, q:q1, :], writes=[wd])

    cnt = {"x": 0, "st": 0, "hb": 0, "pT": 0}

    def s1_tile(sb, t):
        r0 = sb * T + t * 128
        x = rot(xt, cnt["x"])
        cnt["x"] += 1
        s = rot(st, cnt["st"])
        cnt["st"] += 1
        jk = rot(junk, cnt["st"])
        h = rot(hb, cnt["hb"])
        cnt["hb"] += 1
        p = rot(pT, cnt["pT"])
        cnt["pT"] += 1
        hd = hT[sb % 2]

        def prep():
            ph.dma("sp", x, x.t[:], x_src[r0:r0 + 128, :], writes=[x])
            ph.op("act", lambda e: e.activation(out=jk.t[:], in_=x.t[:], func=AF.Square, accum_out=s.t[:, 0:1]),
                  reads=[x], writes=[jk, s])
            ph.op("dve", lambda e: e.tensor_scalar(out=s.t[:, 1:2], in0=s.t[:, 0:1], scalar1=1.0 / D, scalar2=EPS,
                                                   op0=ALU.mult, op1=ALU.add), reads=[s], writes=[s])
            ph.op("pool", lambda e: e.tensor_tensor(out=s.t[:, 2:3], in0=s.t[:, 1:2], in1=cst["mhalf"].t[:, 0:1],
                                                    op=ALU.pow), reads=[s, cst["mhalf"]], writes=[s])
            ph.op("dve", lambda e: e.scalar_tensor_tensor(out=h.t[:], in0=x.t[:], scalar=s.t[:, 2:3], in1=gpre.t[:],
                                                          op0=ALU.mult, op1=ALU.mult), reads=[x, s, gpre], writes=[h])

        def xpose():
            for kc in range(8):
                ph.op("pe", lambda e, kc=kc: e.transpose(out=p.t[:, kc, :], in_=h.t[:, kc * 128:(kc + 1) * 128],
                                                         identity=cst["ident"].t[:]),
                      reads=[h, cst["ident"]], writes=[p], signal=(kc == 7))
            ph.op("act", lambda e: e.activation(out=hd.t[:, :, t * 128:(t + 1) * 128], in_=p.t[:], func=AF.Copy),
                  reads=[p], writes=[hd])

        return prep, xpose

    def issue_w(gi):
        if gi >= NSB * NG:
            return
        w = rot(wu, gi)
        c0 = (gi % NG) * JG * 128
        W = JG * 128
        ph.dma("pool", w, w.t[:, :, 0:W], w_up_v[:, :, c0:c0 + W], writes=[w])
        ph.dma("pool", w, w.t[:, :, W:2 * W], w_up_v[:, :, DFF + c0:DFF + c0 + W], writes=[w])

    def s2_group(sb, jg, gi):
        w = rot(wu, gi)
        W = JG * 128
        hs = hT[sb % 2]
        k = 0
        for jj in range(JG):
            j = jg * JG + jj
            for nb in range(NB):
                a = rot(pa, gi * JG * NB + k)
                b = rot(pb, gi * JG * NB + k)
                s_ = rot(sa, gi * JG * NB + k)
                k += 1
                for kc in range(8):
                    ph.op("pe", lambda e, a=a, w=w, hs=hs, kc=kc, jj=jj, nb=nb: e.matmul(
                        a.t[:], w.t[:, kc, jj * 128:(jj + 1) * 128], hs.t[:, kc, nb * 512:(nb + 1) * 512],
                        start=(kc == 0), stop=(kc == 7)), reads=[w, hs], writes=[a], signal=(kc == 7))
                for kc in range(8):
                    ph.op("pe", lambda e, b=b, w=w, hs=hs, kc=kc, jj=jj, nb=nb: e.matmul(
                        b.t[:], w.t[:, kc, W + jj * 128:W + (jj + 1) * 128], hs.t[:, kc, nb * 512:(nb + 1) * 512],
                        start=(kc == 0), stop=(kc == 7)), reads=[w, hs], writes=[b], signal=(kc == 7))
                ph.op("act", lambda e, a=a, s_=s_: e.activation(out=s_.t[:], in_=a.t[:], func=AF.Silu),
                      reads=[a], writes=[s_])
                ph.op("dve", lambda e, b=b, s_=s_, j=j, nb=nb: e.tensor_tensor(
                    out=gT.t[:, j, nb * 512:(nb + 1) * 512], in0=s_.t[:], in1=b.t[:], op=ALU.mult),
                    reads=[s_, b], writes=[gTb[j]])

    def s3(sb):
        for t in range(TT):
            r0 = sb * T + t * 128
            x = rot(xr, t)
            y = rot(yb, t)
            s = rot(st, cnt["st"])
            cnt["st"] += 1
            jk = rot(junk, cnt["st"])
            ph.dma("sp", x, x.t[:], x_src[r0:r0 + 128, :], writes=[x])
            for hf in range(2):
                for j in range(NJ):
                    ph.op("pe", lambda e, j=j, hf=hf, t=t: e.matmul(
                        po.t[:, hf * 512:(hf + 1) * 512], gT.t[:, j, t * 128:(t + 1) * 128],
                        wd.t[:, j, hf * 512:(hf + 1) * 512], start=(j == 0), stop=(j == NJ - 1)),
                        reads=[gTb[j], wd], writes=[po], signal=(j == NJ - 1))
            ph.op("act", lambda e, y=y: e.activation(out=y.t[:], in_=po.t[:], func=AF.Copy), reads=[po], writes=[y])
            ph.op("act", lambda e, jk=jk, s=s, y=y: e.activation(out=jk.t[:], in_=y.t[:], func=AF.Square,
                                                                 accum_out=s.t[:, 0:1]), reads=[y], writes=[jk, s])
            ph.op("dve", lambda e, s=s: e.tensor_scalar(out=s.t[:, 1:2], in0=s.t[:, 0:1], scalar1=1.0 / D, scalar2=EPS,
                                                        op0=ALU.mult, op1=ALU.add), reads=[s], writes=[s])
            ph.op("pool", lambda e, s=s: e.tensor_tensor(out=s.t[:, 2:3], in0=s.t[:, 1:2], in1=cst["mhalf"].t[:, 0:1],
                                                         op=ALU.pow), reads=[s, cst["mhalf"]], writes=[s])
            ph.op("dve", lambda e, s=s, y=y: e.scalar_tensor_tensor(out=y.t[:], in0=y.t[:], scalar=s.t[:, 2:3],
                                                                    in1=gpost.t[:], op0=ALU.mult, op1=ALU.mult),
                  reads=[y, s, gpost], writes=[y])
            ph.op("pool", lambda e, x=x, y=y: e.tensor_tensor(out=x.t[:], in0=x.t[:], in1=y.t[:], op=ALU.add),
                  reads=[x, y], writes=[x])
            ph.dma("sp", x, x_dst[r0:r0 + 128, :], x.t[:], reads=[x])

    prev = None
    for t in range(TT):
        pr, xp = s1_tile(0, t)
        pr()
        if prev is not None:
            prev()
        prev = xp
    prev()
    gi = 0
    issue_w(0)
    issue_w(1)
    for sb in range(NSB):
        pend = None
        for jg in range(NG):
            issue_w(gi + 2)
            nxt_x = None
            if sb + 1 < NSB and jg < TT:
                pr, nxt_x = s1_tile(sb + 1, jg)
                pr()
            s2_group(sb, jg, gi)
            gi += 1
            if pend is not None:
                pend()
            pend = nxt_x
        if pend is not None:
            pend()
        s3(sb)
    ph.flush()


TWO_PI = 6.283185307179586
C1 = 6.28125
C2 = TWO_PI - 6.28125
MAGIC = 12582912.0
PI_LO = 3.1415925


def rope_phase(nc, name, pos_d, ropec_d, cos_d, sin_d):
    ph = Phase(nc, name)
    posi = ph.sbuf([128, NTOK], I32, "posi")
    ang = ph.sbuf([128, NTOK], F32, "ang")
    t1 = ph.sbuf([128, NTOK], F32, "t1")
    t2 = ph.sbuf([128, NTOK], F32, "t2")
    rc = ph.sbuf([128, 2], F32, "rc")
    one = ph.sbuf([128, 1], F32, "one")
    ph.dma("sp", posi, posi.t[:], pos_d.partition_broadcast(128), writes=[posi])
    ph.dma("sp", rc, rc.t[:], ropec_d, writes=[rc])
    ph.op("pool", lambda e: e.memset(one.t[:], 1.0), writes=[one])
    ph.op("dve", lambda e: e.tensor_copy(out=ang.t[:], in_=posi.t[:]), reads=[posi], writes=[ang])
    ph.op("dve", lambda e: e.tensor_scalar(out=ang.t[:], in0=ang.t[:], scalar1=rc.t[:, 0:1], scalar2=None, op0=ALU.mult),
          reads=[ang, rc], writes=[ang])
    for which, dst, scale_ap in (("sin", sin_d, rc), ("cos", cos_d, one)):
        off = 0.0 if which == "sin" else TWO_PI / 4
        ph.op("dve", lambda e, off=off: e.tensor_scalar(out=t1.t[:], in0=ang.t[:], scalar1=off, scalar2=None, op0=ALU.add),
              reads=[ang], writes=[t1])
        ph.op("dve", lambda e: e.tensor_scalar(out=t2.t[:], in0=t1.t[:], scalar1=1.0 / TWO_PI, scalar2=MAGIC,
                                               op0=ALU.mult, op1=ALU.add), reads=[t1], writes=[t2])
        ph.op("dve", lambda e: e.tensor_scalar(out=t2.t[:], in0=t2.t[:], scalar1=-MAGIC, scalar2=None, op0=ALU.add),
              reads=[t2], writes=[t2])
        ph.op("dve", lambda e: e.scalar_tensor_tensor(out=t1.t[:], in0=t2.t[:], scalar=-C1, in1=t1.t[:],
                                                      op0=ALU.mult, op1=ALU.add), reads=[t1, t2], writes=[t1])
        ph.op("dve", lambda e: e.scalar_tensor_tensor(out=t1.t[:], in0=t2.t[:], scalar=-C2, in1=t1.t[:],
                                                      op0=ALU.mult, op1=ALU.add), reads=[t1, t2], writes=[t1])
        ph.op("dve", lambda e: e.tensor_scalar(out=t1.t[:], in0=t1.t[:], scalar1=-PI_LO, scalar2=PI_LO,
                                               op0=ALU.max, op1=ALU.min), reads=[t1], writes=[t1])
        sc = scale_ap.t[:, 1:2] if which == "sin" else scale_ap.t[:, 0:1]
        ph.op("act", lambda e, sc=sc: e.activation(out=t2.t[:], in_=t1.t[:], func=AF.Sin, scale=sc),
              reads=[t1, scale_ap], writes=[t2])
        ph.dma("sp", t2, dst, t2.t[:], reads=[t2])
    ph.flush()


def proj_phase(nc, name, x_src, g_pre, w_in, ident_d, cos_d, sin_d, h2T_d, qT_d, kown_d, vown_d, upT_d, qcaT_d, edge_d):
    ph = Phase(nc, name)
    cst = load_consts(ph, ident_d)
    w_in_v = w_in.rearrange("(kc p) n -> p kc n", p=128)
    gpre = ph.sbuf([128, D], F32, "gpre")
    win = ph.sbuf([128, 8, 2048], BF16, "win")
    wsw = ph.sbuf([128, 8, 1024], BF16, "wsw")
    hT = [ph.sbuf([128, 8, 512], BF16, "hT") for _ in range(2)]
    xt = [ph.sbuf([128, D], F32, "xt") for _ in range(4)]
    hb = [ph.sbuf([128, D], BF16, "hb") for _ in range(4)]
    junk = [ph.sbuf([128, D], BF16, "junk") for _ in range(2)]
    st = [ph.sbuf([128, 4], F32, "st") for _ in range(4)]
    cosb = [ph.sbuf([128, 512], F32, "cosb") for _ in range(2)]
    sinb = [ph.sbuf([128, 512], F32, "sinb") for _ in range(2)]
    r1 = [ph.sbuf([128, 512], F32, "r1") for _ in range(2)]
    r2 = [ph.sbuf([128, 512], F32, "r2") for _ in range(2)]
    ob = [ph.sbuf([128, 512], BF16, "ob") for _ in range(4)]
    of = [ph.sbuf([128, 512], F32, "of") for _ in range(2)]
    pT = [ph.psum([128, 8, 128], BF16, "pT") for _ in range(2)]
    pp = [ph.psum([128, 512], F32, "pp") for _ in range(6)]
    c = {"x": 0, "st": 0, "hb": 0, "pT": 0, "pp": 0, "ob": 0, "of": 0, "r": 0}

    ph.dma("sp", gpre, gpre.t[:], bcast_row(g_pre), writes=[gpre])
    for kc in range(8):
        ph.dma("pool", win, win.t[:, kc, :], w_in_v[:, kc, :], writes=[win])
    ph.op("pool", lambda e: e.memset(wsw.t[:], 0.0), writes=[wsw])
    src4 = win.t[:, :, 256:1280].rearrange("p k (b d) -> p k b d", d=64)
    dst4 = wsw.t[:].rearrange("p k (b d) -> p k b d", d=64)
    for kc in range(8):
        ph.op("pool", lambda e, kc=kc: e.tensor_copy(out=dst4[:, kc, :, 0:8], in_=src4[:, kc, :, 8:16]), reads=[win], writes=[wsw])
        ph.op("pool", lambda e, kc=kc: e.tensor_copy(out=dst4[:, kc, :, 8:16], in_=src4[:, kc, :, 0:8]), reads=[win], writes=[wsw])

    def pbank():
        p = rot(pp, c["pp"])
        c["pp"] += 1
        return p

    def fm_proj(p, wt, col0, hs):
        for kc in range(8):
            ph.op("pe", lambda e, kc=kc: e.matmul(p.t[:], wt.t[:, kc, col0:col0 + 128], hs.t[:, kc, :],
                                                  start=(kc == 0), stop=(kc == 7)), reads=[wt, hs], writes=[p], signal=(kc == 7))

    def s1_block(blk):
        hs = hT[blk % 2]
        preps, xposes = [], []
        for t in range(4):
            r0 = blk * 512 + t * 128
            x = rot(xt, c["x"]); c["x"] += 1
            s = rot(st, c["st"]); c["st"] += 1
            jk = rot(junk, c["st"])
            h = rot(hb, c["hb"]); c["hb"] += 1
            p = rot(pT, c["pT"]); c["pT"] += 1

            def prep(x=x, s=s, jk=jk, h=h, r0=r0):
                ph.dma("sp", x, x.t[:], x_src[r0:r0 + 128, :], writes=[x])
                ph.op("act", lambda e: e.activation(out=jk.t[:], in_=x.t[:], func=AF.Square, accum_out=s.t[:, 0:1]),
                      reads=[x], writes=[jk, s])
                ph.op("dve", lambda e: e.tensor_scalar(out=s.t[:, 1:2], in0=s.t[:, 0:1], scalar1=1.0 / D, scalar2=EPS,
                                                       op0=ALU.mult, op1=ALU.add), reads=[s], writes=[s])
                ph.op("pool", lambda e: e.tensor_tensor(out=s.t[:, 2:3], in0=s.t[:, 1:2], in1=cst["mhalf"].t[:, 0:1],
                                                        op=ALU.pow), reads=[s, cst["mhalf"]], writes=[s])
                ph.op("dve", lambda e: e.scalar_tensor_tensor(out=h.t[:], in0=x.t[:], scalar=s.t[:, 2:3], in1=gpre.t[:],
                                                              op0=ALU.mult, op1=ALU.mult), reads=[x, s, gpre], writes=[h])

            def xpose(h=h, p=p, t=t, hs=hs):
                for kc in range(8):
                    ph.op("pe", lambda e, kc=kc: e.transpose(out=p.t[:, kc, :], in_=h.t[:, kc * 128:(kc + 1) * 128],
                                                             identity=cst["ident"].t[:]),
                          reads=[h, cst["ident"]], writes=[p], signal=(kc == 7))
                ph.op("act", lambda e: e.activation(out=hs.t[:, :, t * 128:(t + 1) * 128], in_=p.t[:], func=AF.Copy),
                      reads=[p], writes=[hs])

            preps.append(prep)
            xposes.append(xpose)
        return preps, xposes

    NBLK = NTOK // 512
    pr0, xp0 = s1_block(0)
    for f in pr0:
        f()
    for f in xp0:
        f()
    for blk in range(NBLK):
        hs = hT[blk % 2]
        t0 = blk * 512
        nxt = s1_block(blk + 1) if blk + 1 < NBLK else None
        ph.dma("sp", hs, h2T_d[:, :, t0:t0 + 512], hs.t[:], reads=[hs])
        cb = cosb[blk % 2]
        sb_ = sinb[blk % 2]
        ph.dma("sp", cb, cb.t[:], cos_d[:, t0:t0 + 512], writes=[cb])
        ph.dma("sp", sb_, sb_.t[:], sin_d[:, t0:t0 + 512], writes=[sb_])
        for cc in range(2):
            p = pbank()
            fm_proj(p, win, cc * 128, hs)
            o = rot(of, c["of"]); c["of"] += 1
            ph.op("act", lambda e, p=p, o=o: e.activation(out=o.t[:], in_=p.t[:], func=AF.Copy), reads=[p], writes=[o])
            ph.dma("sp", o, upT_d[cc, :, t0:t0 + 512], o.t[:], reads=[o])
            if blk == 0:
                ph.dma("sp", o, edge_d[cc * 128:(cc + 1) * 128, 0:8], o.t[:, 0:8], reads=[o])
            if blk == NTOK // 512 - 1:
                ph.dma("sp", o, edge_d[cc * 128:(cc + 1) * 128, 8:16], o.t[:, 504:512], reads=[o])
        if nxt is not None:
            for f in nxt[0]:
                f()
        for which, dstd in (("q", qT_d), ("k", kown_d)):
            base = 256 if which == "q" else 768
            if which == "k" and nxt is not None:
                for f in nxt[1]:
                    f()
            for hh in range(4):
                p = pbank()
                ps = pbank()
                fm_proj(p, win, base + hh * 128, hs)
                fm_proj(ps, wsw, (base - 256) + hh * 128, hs)
                a = rot(r1, c["r"]); b = rot(r2, c["r"]); c["r"] += 1
                o = rot(ob, c["ob"]); c["ob"] += 1
                ph.op("dve", lambda e, p=p, a=a, cb=cb: e.tensor_tensor(out=a.t[:], in0=p.t[:], in1=cb.t[:], op=ALU.mult),
                      reads=[p, cb], writes=[a])
                ph.op("dve", lambda e, ps=ps, b=b, sb_=sb_: e.tensor_tensor(out=b.t[:], in0=ps.t[:], in1=sb_.t[:], op=ALU.mult),
                      reads=[ps, sb_], writes=[b])
                ph.op("pool", lambda e, a=a, b=b, o=o: e.tensor_tensor(out=o.t[:], in0=a.t[:], in1=b.t[:], op=ALU.add),
                      reads=[a, b], writes=[o])
                ph.dma("sp", o, dstd[hh * 128:(hh + 1) * 128, t0:t0 + 512], o.t[:], reads=[o])
        for cc in range(2):
            p = pbank()
            fm_proj(p, win, 1792 + cc * 128, hs)
            o = rot(ob, c["ob"]); c["ob"] += 1
            ph.op("act", lambda e, p=p, o=o: e.activation(out=o.t[:], in_=p.t[:], func=AF.Copy), reads=[p], writes=[o])
            ph.dma("sp", o, qcaT_d[cc * 128:(cc + 1) * 128, t0:t0 + 512], o.t[:], reads=[o])
        for t in range(4):
            r0 = t0 + t * 128
            p = pbank()
            for kc in range(8):
                ph.op("pe", lambda e, p=p, kc=kc, t=t, hs=hs: e.matmul(p.t[:], hs.t[:, kc, t * 128:(t + 1) * 128],
                                                                  win.t[:, kc, 1280:1792], start=(kc == 0), stop=(kc == 7)),
                      reads=[win, hs], writes=[p], signal=(kc == 7))
            o = rot(ob, c["ob"]); c["ob"] += 1
            ph.op("act", lambda e, p=p, o=o: e.activation(out=o.t[:], in_=p.t[:], func=AF.Copy), reads=[p], writes=[o])
            ph.dma("sp", o, vown_d[r0:r0 + 128, :], o.t[:], reads=[o])
    ph.flush()


PAIRS = [[0, 1], [2, 3], [4, 5], [6, 7]]


def exchange_phase(nc, name, items):
    sems = [nc.alloc_semaphore(name=f"{name}_cc{i}") for i in range(len(items))]
    with nc.Block() as block:
        @block.gpsimd
        def _(g):
            for (src, dst), sem in zip(items, sems):
                g.collective_compute("AllGather", ALU.bypass, replica_groups=PAIRS, ins=[src], outs=[dst]).then_inc(sem)
            for sem in sems:
                g.wait_ge(sem, 1)
    nc.clear_and_free_semaphores(sems)
    nc.all_engine_barrier()


def attn_phase(nc, name, qT_d, kfull_d, vfull_d, lamq1, lamk1, lamq2, lamk2, subg, lam_init, ydaT_d):
    ph = Phase(nc, name)
    NKT = SEQ // 128
    kT = [ph.sbuf([128, SEQ], BF16, "kT") for _ in range(2)]
    vt = [ph.sbuf([128, NKT, 128], BF16, "vt") for _ in range(2)]
    ones = ph.sbuf([128, 128], BF16, "ones")
    epsb = ph.sbuf([128, 1], F32, "epsb")
    lam4 = ph.sbuf([128, 4, 64], F32, "lam4")
    lj = ph.sbuf([128, 64], F32, "lj")
    lc = ph.sbuf([128, 8], F32, "lc")
    gs = ph.sbuf([128, 1], F32, "gs")
    qb = [ph.sbuf([128, 512], BF16, "qb") for _ in range(2)]
    pe_ = [ph.sbuf([128, 512], BF16, "pe") for _ in range(6)]
    rr = [ph.sbuf([128, 512], F32, "rr") for _ in range(2)]
    o1 = [ph.sbuf([128, 512], F32, "o1") for _ in range(2)]
    o2 = [ph.sbuf([128, 512], F32, "o2") for _ in range(2)]
    sq = [ph.sbuf([128, 512], BF16, "sq") for _ in range(2)]
    rs = [ph.sbuf([128, 512], F32, "rs") for _ in range(2)]
    yo = [ph.sbuf([128, 512], BF16, "yo") for _ in range(2)]
    accO = [ph.psum([128, 512], F32, "accO") for _ in range(2)]
    accS = [ph.psum([128, 512], F32, "accS") for _ in range(2)]
    scp = [[ph.psum([128, 512], F32, "sc") for _ in range(2)] for _ in range(2)]

    ph.op("pool", lambda e: e.memset(ones.t[:], 1.0), writes=[ones])
    ph.op("pool", lambda e: e.memset(epsb.t[:], EPS), writes=[epsb])
    for i, v in enumerate((lamq1, lamk1, lamq2, lamk2)):
        ph.dma("sp", lam4, lam4.t[:, i, :], v.partition_broadcast(128), writes=[lam4])
    ph.dma("sp", gs, gs.t[:], subg.rearrange("(p o) -> p o", o=1), writes=[gs])
    for i in range(2):
        ph.op("dve", lambda e, i=i: e.scalar_tensor_tensor(out=lj.t[:], in0=lam4.t[:, 2 * i, :], scalar=1.0,
                                                            in1=lam4.t[:, 2 * i + 1, :], op0=ALU.mult, op1=ALU.mult,
                                                            accum_out=lc.t[:, i:i + 1]), reads=[lam4], writes=[lj, lc])
    ph.op("act", lambda e: e.activation(out=lc.t[:, 2:4], in_=lc.t[:, 0:2], func=AF.Exp), reads=[lc], writes=[lc])
    ph.op("dve", lambda e: e.tensor_tensor(out=lc.t[:, 4:5], in0=lc.t[:, 2:3], in1=lc.t[:, 3:4], op=ALU.subtract),
          reads=[lc], writes=[lc])
    ph.op("dve", lambda e: e.tensor_scalar(out=lc.t[:, 5:6], in0=lc.t[:, 4:5], scalar1=lam_init, scalar2=-1.0,
                                           op0=ALU.add, op1=ALU.mult), reads=[lc], writes=[lc])
    ph.op("dve", lambda e: e.tensor_scalar(out=lc.t[:, 6:7], in0=gs.t[:, 0:1], scalar1=1.0 - lam_init, scalar2=None,
                                           op0=ALU.mult), reads=[gs, lc], writes=[lc])
    ss_ = [[ph.sbuf([128, 512], F32, "ssc") for _ in range(2)] for _ in range(2)]
    prs = [[ph.sbuf([128, 512], BF16, "prs") for _ in range(3)] for _ in range(2)]
    cq = 0
    cp_ = 0

    def load_kv(h):
        k = kT[h % 2]
        v = vt[h % 2]
        for r in range(2):
            ph.dma("sp", k, k.t[:, r * NTOK:(r + 1) * NTOK], kfull_d[h][r * 128:(r + 1) * 128, :], writes=[k])
            for pc in range(4):
                kt0 = r * 32 + pc * 8
                ph.dma("sp", v, v.t[:, kt0:kt0 + 8, :],
                       vfull_d[pc][r * 1024:(r + 1) * 1024, h * 128:(h + 1) * 128].rearrange("(kt p) e -> p kt e", p=128),
                       writes=[v])

    def make_post(h, blk):
        t0 = blk * 512
        i2 = blk % 2
        a1, a2, sq_, rs_, y = o1[i2], o2[i2], sq[i2], rs[i2], yo[i2]
        s0, s1 = ss_[i2]

        def part1():
            for c, sc_, a_ in ((0, s0, a1), (1, s1, a2)):
                ph.op("act", lambda e, c=c, sc_=sc_: e.activation(out=sc_.t[:], in_=accS[c].t[:], func=AF.Copy),
                      reads=[accS[c]], writes=[sc_])
                ph.op("dve", lambda e, c=c, a_=a_: e.tensor_copy(out=a_.t[:], in_=accO[c].t[:]), reads=[accO[c]], writes=[a_])
            for sc_, a_ in ((s0, a1), (s1, a2)):
                ph.op("dve", lambda e, sc_=sc_: e.reciprocal(out=sc_.t[:], in_=sc_.t[:]), reads=[sc_], writes=[sc_])
                ph.op("dve", lambda e, sc_=sc_, a_=a_: e.tensor_tensor(out=a_.t[:], in0=a_.t[:], in1=sc_.t[:], op=ALU.mult),
                      reads=[a_, sc_], writes=[a_])
            ph.op("dve", lambda e: e.scalar_tensor_tensor(out=a1.t[:], in0=a2.t[:], scalar=lc.t[:, 5:6], in1=a1.t[:],
                                                          op0=ALU.mult, op1=ALU.add), reads=[a1, a2, lc], writes=[a1])

        def part2(npz):
            ph.op("act", lambda e: e.activation(out=sq_.t[:], in_=a1.t[:], func=AF.Square), reads=[a1], writes=[sq_])
            ph.op("pe", lambda e: e.matmul(npz.t[:], ones.t[:], sq_.t[:], start=True, stop=True),
                  reads=[ones, sq_], writes=[npz])
            ph.op("act", lambda e: e.activation(out=rs_.t[:], in_=npz.t[:], func=AF.Sqrt, scale=1.0 / 128,
                                                bias=epsb.t[:, 0:1]), reads=[npz, epsb], writes=[rs_])
            ph.op("dve", lambda e: e.reciprocal(out=rs_.t[:], in_=rs_.t[:]), reads=[rs_], writes=[rs_])
            ph.op("dve", lambda e: e.scalar_tensor_tensor(out=y.t[:], in0=a1.t[:], scalar=lc.t[:, 6:7], in1=rs_.t[:],
                                                          op0=ALU.mult, op1=ALU.mult), reads=[a1, rs_, lc], writes=[y])
            ph.dma("sp", y, ydaT_d[h * 128:(h + 1) * 128, t0:t0 + 512], y.t[:], reads=[y])

        return part1, part2

    DEFER_KT = 8
    pending = None
    load_kv(0)
    for h in range(4):
        k = kT[h % 2]
        v = vt[h % 2]
        for blk in range(NTOK // 512):
            t0 = blk * 512
            q = rot(qb, cq)
            cq += 1
            ph.dma("sp", q, q.t[:], qT_d[h * 128:(h + 1) * 128, t0:t0 + 512], writes=[q])
            if blk == 1 and h + 1 < 4:
                load_kv(h + 1)

            def score(kt, q=q, k=k):
                for c in range(2):
                    s_ = scp[c][kt % 2]
                    ph.op("pe", lambda e, s_=s_, c=c, kt=kt, q=q, k=k: e.matmul(
                        s_.t[:], k.t[c * 64:(c + 1) * 64, kt * 128:(kt + 1) * 128], q.t[c * 64:(c + 1) * 64, :],
                        start=True, stop=True), reads=[k, q], writes=[s_])

            score(0)
            pendS = []
            peven = [None, None]
            for kt in range(NKT):
                if kt == DEFER_KT and pending is not None:
                    pending(scp[0][(kt + 1) % 2])
                    pending = None
                if kt + 1 < NKT:
                    score(kt + 1)
                for f in pendS:
                    f()
                pendS = []
                for c in range(2):
                    s_ = scp[c][kt % 2]
                    p = rot(pe_, cp_)
                    cp_ += 1
                    ph.op("act", lambda e, s_=s_, p=p: e.activation(out=p.t[:], in_=s_.t[:], func=AF.Exp, scale=0.125),
                          reads=[s_], writes=[p])
                    ph.op("pe", lambda e, c=c, kt=kt, p=p, v=v: e.matmul(accO[c].t[:], v.t[:, kt, :], p.t[:],
                                                                         start=(kt == 0), stop=(kt == NKT - 1)),
                          reads=[v, p], writes=[accO[c]])
                    if kt % 2 == 0:
                        peven[c] = p
                    else:
                        pr = rot(prs[c], kt // 2)
                        pe0 = peven[c]
                        ph.op("dve", lambda e, pr=pr, pe0=pe0, p=p: e.tensor_tensor(out=pr.t[:], in0=pe0.t[:], in1=p.t[:], op=ALU.add),
                              reads=[pe0, p], writes=[pr])

                        def smm(c=c, kt=kt, pr=pr):
                            ph.op("pe", lambda e: e.matmul(accS[c].t[:], ones.t[:], pr.t[:], start=(kt == 1), stop=(kt == NKT - 1)),
                                  reads=[ones, pr], writes=[accS[c]])
                        pendS.append(smm)
            for f in pendS:
                f()
            p1, pending = make_post(h, blk)
            p1()
    pending(scp[0][0])
    ph.flush()


def merge_phase(nc, name, x_src, x_dst, ident_d, h2T_d, upT_d, edgefull_d, hmask_d, rcnt_d, qcaT_d, ydaT_d, mem_d,
                pool_w, pool_scale, mem_g, w_mem_kv, w_gate, b_gate, w_bp, w_bd, w_bc, w_out, g_post, dbg=False):
    ph = Phase(nc, name)

    def dump(nm, tl, shape, dt):
        if dbg:
            o = nc.dram_tensor("dscr_" + nm, shape, dt).ap()
            ph.dma("sp", tl, o, tl.t[:], reads=[tl])
            DBG_ITEMS.append((nm, o, shape, dt))
    cst = load_consts(ph, ident_d)
    wg = ph.sbuf([128, 8, 3072], BF16, "wg")
    wbp = ph.sbuf([128, 2, D], BF16, "wbp")
    wbd = ph.sbuf([128, 4, D], BF16, "wbd")
    wbc = ph.sbuf([128, 2, D], BF16, "wbc")
    wo = ph.sbuf([128, 8, D], BF16, "wo")
    wkv = ph.sbuf([128, 8, 512], BF16, "wkv")
    wblk = ph.sbuf([128, 2, 128], BF16, "wblk")
    bg = ph.sbuf([128, 24], F32, "bg")
    psc = ph.sbuf([128, 2], F32, "psc")
    hm = ph.sbuf([128, 2], F32, "hm")
    gpost = ph.sbuf([128, D], F32, "gpost")
    gmem = ph.sbuf([128, D], F32, "gmem")
    memT = ph.sbuf([128, 8, NMEM], BF16, "memT")
    kmT = ph.sbuf([128, 2, NMEM], BF16, "kmT")
    vpad = [ph.sbuf([128, 2, 2, 128], BF16, "vpad") for _ in range(2)]
    onesel = [ph.sbuf([128, 128], BF16, "onesel") for _ in range(2)]
    hT = [ph.sbuf([128, 8, 512], BF16, "hT") for _ in range(2)]
    U = [ph.sbuf([128, 2, 528], F32, "U") for _ in range(1)]
    A = [ph.sbuf([128, 2, 528], F32, "A") for _ in range(1)]
    Bt = [ph.sbuf([128, 2, 528], F32, "B") for _ in range(1)]
    rcn = [ph.sbuf([128, 2, 512], F32, "rcn") for _ in range(1)]
    pl = [ph.sbuf([128, 2, 512], BF16, "pl") for _ in range(1)]
    ypl = [ph.sbuf([128, 2, 512], BF16, "ypl") for _ in range(1)]
    qca = [ph.sbuf([128, 2, 512], BF16, "qca") for _ in range(1)]
    yca = [ph.sbuf([128, 2, 512], BF16, "yca") for _ in range(1)]
    yda = [ph.sbuf([128, 4, 512], BF16, "yda") for _ in range(1)]
    pex = [ph.sbuf([128, 512], BF16, "pex") for _ in range(3)]
    rcp = [ph.sbuf([128, 512], F32, "rcp") for _ in range(2)]
    tg = [ph.sbuf([128, 512], F32, "tg") for _ in range(4)]
    um = [ph.sbuf([128, 512], F32, "um") for _ in range(4)]
    mT = [ph.sbuf([128, 8, 512], BF16, "mT") for _ in range(1)]
    xt = [ph.sbuf([128, D], F32, "xt") for _ in range(2)]
    yb = [ph.sbuf([128, D], F32, "yb") for _ in range(2)]
    hb = [ph.sbuf([128, D], BF16, "hb") for _ in range(2)]
    junk = [ph.sbuf([128, D], BF16, "junk") for _ in range(1)]
    st = [ph.sbuf([128, 4], F32, "st") for _ in range(4)]
    po = ph.psum([128, D], F32, "po")
    pp = [ph.psum([128, 512], F32, "pp") for _ in range(6)]
    c = {"pp": 0, "x": 0, "st": 0, "pex": 0, "tg": 0, "um": 0}

    def pbank():
        p = rot(pp, c["pp"])
        c["pp"] += 1
        return p

    def wload(tl, src, n):
        v = src.rearrange("(kc p) n -> p kc n", p=128)
        for kc in range(n):
            ph.dma("pool", tl, tl.t[:, kc, :], v[:, kc, :], writes=[tl])

    wload(wg, w_gate, 8)
    wload(wbp, w_bp, 2)
    wload(wbd, w_bd, 4)
    wload(wbc, w_bc, 2)
    wload(wo, w_out, 8)
    wload(wkv, w_mem_kv, 8)
    for tl in (wbp, wbd, wbc):
        ph.op("pool", lambda e, tl=tl: e.tensor_scalar(out=tl.t[:], in0=tl.t[:], scalar1=0.5, scalar2=None, op0=ALU.mult),
              reads=[tl], writes=[tl])
    ph.dma("sp", bg, bg.t[:], b_gate.rearrange("(c p) -> p c", p=128), writes=[bg], allow_slow_non_contiguous=True)
    ph.op("dve", lambda e: e.tensor_scalar(out=bg.t[:], in0=bg.t[:], scalar1=0.5, scalar2=None, op0=ALU.mult),
          reads=[bg], writes=[bg])
    ph.dma("sp", psc, psc.t[:], pool_scale.rearrange("(c p) -> p c", p=128), writes=[psc], allow_slow_non_contiguous=True)
    ph.dma("sp", hm, hm.t[:], hmask_d, writes=[hm])
    ph.dma("sp", gpost, gpost.t[:], bcast_row(g_post), writes=[gpost])
    ph.dma("sp", gmem, gmem.t[:], bcast_row(mem_g), writes=[gmem])
    ph.op("pool", lambda e: e.memset(wblk.t[:], 0.0), writes=[wblk])
    for g in range(4):
        lo = (g % 2) * 64
        ph.dma("pool", wblk, wblk.t[lo:lo + 64, g // 2, lo:lo + 64], pool_w[g], writes=[wblk])
    for hh in range(2):
        ph.op("pool", lambda e, hh=hh: e.memset(onesel[hh].t[:], 0.0), writes=[onesel[hh]])
        ph.op("pool", lambda e, hh=hh: e.memset(onesel[hh].t[:, hh * 64:(hh + 1) * 64], 1.0), writes=[onesel[hh]])
        ph.op("pool", lambda e, hh=hh: e.memset(vpad[hh].t[:], 0.0), writes=[vpad[hh]])

    def norm_tile(x, g_t, out_t, src_t=None):
        s = rot(st, c["st"]); c["st"] += 1
        jk = rot(junk, c["st"])
        srcT = src_t if src_t is not None else x
        if src_t is not None:
            ph.op("act", lambda e: e.activation(out=out_t.t[:], in_=src_t.t[:], func=AF.Copy), reads=[src_t], writes=[out_t])
            srcT = out_t
        ph.op("act", lambda e: e.activation(out=jk.t[:], in_=srcT.t[:], func=AF.Square, accum_out=s.t[:, 0:1]),
              reads=[srcT], writes=[jk, s])
        ph.op("dve", lambda e: e.tensor_scalar(out=s.t[:, 1:2], in0=s.t[:, 0:1], scalar1=1.0 / D, scalar2=EPS,
                                               op0=ALU.mult, op1=ALU.add), reads=[s], writes=[s])
        ph.op("pool", lambda e: e.tensor_tensor(out=s.t[:, 2:3], in0=s.t[:, 1:2], in1=cst["mhalf"].t[:, 0:1], op=ALU.pow),
              reads=[s, cst["mhalf"]], writes=[s])
        ph.op("dve", lambda e: e.scalar_tensor_tensor(out=out_t.t[:], in0=srcT.t[:], scalar=s.t[:, 2:3], in1=g_t.t[:],
                                                      op0=ALU.mult, op1=ALU.mult), reads=[srcT, s, g_t], writes=[out_t])

    for m in range(2):
        x = rot(xt, c["x"]); c["x"] += 1
        h = hb[m]
        ph.dma("sp", x, x.t[:], mem_d[m * 128:(m + 1) * 128, :], writes=[x])
        norm_tile(x, gmem, h)
        for kc in range(8):
            pt = pbank()
            ph.op("pe", lambda e, pt=pt, h=h, kc=kc: e.transpose(out=pt.t[:].bitcast(BF16)[:, 0:128],
                                                                 in_=h.t[:, kc * 128:(kc + 1) * 128],
                                                                 identity=cst["ident"].t[:]),
                  reads=[h, cst["ident"]], writes=[pt])
            ph.op("act", lambda e, pt=pt, kc=kc, m=m: e.activation(out=memT.t[:, kc, m * 128:(m + 1) * 128],
                                                                    in_=pt.t[:].bitcast(BF16)[:, 0:128], func=AF.Copy),
                  reads=[pt], writes=[memT])
    for cc in range(2):
        p = pbank()
        for kc in range(8):
            ph.op("pe", lambda e, p=p, kc=kc, cc=cc: e.matmul(p.t[:, 0:NMEM], wkv.t[:, kc, cc * 128:(cc + 1) * 128],
                                                              memT.t[:, kc, :], start=(kc == 0), stop=(kc == 7)),
                  reads=[wkv, memT], writes=[p], signal=(kc == 7))
        ph.op("act", lambda e, p=p, cc=cc: e.activation(out=kmT.t[:, cc, :], in_=p.t[:, 0:NMEM], func=AF.Copy),
              reads=[p], writes=[kmT])
    for m in range(2):
        p = pbank()
        for kc in range(8):
            ph.op("pe", lambda e, p=p, kc=kc, m=m: e.matmul(p.t[:, 0:256], memT.t[:, kc, m * 128:(m + 1) * 128],
                                                            wkv.t[:, kc, 256:512], start=(kc == 0), stop=(kc == 7)),
                  reads=[wkv, memT], writes=[p], signal=(kc == 7))
        for cp in range(2):
            for hh in range(2):
                hd = 2 * cp + hh
                ph.op("act", lambda e, p=p, m=m, cp=cp, hh=hh, hd=hd: e.activation(
                    out=vpad[hh].t[:, m, cp, hh * 64:(hh + 1) * 64], in_=p.t[:, hd * 64:(hd + 1) * 64], func=AF.Copy),
                    reads=[p], writes=[vpad[hh]])

    NBLK = NTOK // 512
    for blk in range(NBLK):
        t0 = blk * 512
        i2 = blk % 2
        hs = hT[i2]
        ph.dma("sp", hs, hs.t[:], h2T_d[:, :, t0:t0 + 512], writes=[hs])
        u, a, b, rc_, pl_, ypl_ = U[0], A[0], Bt[0], rcn[0], pl[0], ypl[0]
        for cc in range(2):
            lo = max(t0 - 8, 0)
            hi = min(t0 + 520, NTOK)
            ph.dma("sp", u, u.t[:, cc, 8 - (t0 - lo):8 + (hi - t0)], upT_d[cc, :, lo:hi], writes=[u])
            if blk == 0:
                ph.dma("sp", u, u.t[:, cc, 0:8], edgefull_d[cc * 128:(cc + 1) * 128, 8:16], writes=[u])
            if blk == NBLK - 1:
                ph.dma("sp", u, u.t[:, cc, 520:528], edgefull_d[256 + cc * 128:256 + (cc + 1) * 128, 0:8], writes=[u])
            ph.dma("sp", rc_, rc_.t[:, cc, :], rcnt_d[cc, :, t0:t0 + 512], writes=[rc_])
        if blk == 0:
            ph.op("dve", lambda e, u=u: e.tensor_scalar(out=u.t[:, :, 0:8], in0=u.t[:, :, 0:8], scalar1=hm.t[:, 0:1],
                                                        scalar2=None, op0=ALU.mult), reads=[u, hm], writes=[u])
        if blk == NBLK - 1:
            ph.op("dve", lambda e, u=u: e.tensor_scalar(out=u.t[:, :, 520:528], in0=u.t[:, :, 520:528], scalar1=hm.t[:, 1:2],
                                                        scalar2=None, op0=ALU.mult), reads=[u, hm], writes=[u])
        ph.op("pool", lambda e, u=u, a=a: e.tensor_tensor(out=a.t[:, :, 0:527], in0=u.t[:, :, 0:527], in1=u.t[:, :, 1:528],
                                                          op=ALU.add), reads=[u], writes=[a])
        ph.op("pool", lambda e, a=a, b=b: e.tensor_tensor(out=b.t[:, :, 0:525], in0=a.t[:, :, 0:525], in1=a.t[:, :, 2:527],
                                                          op=ALU.add), reads=[a], writes=[b])
        ph.op("dve", lambda e, a=a, rc_=rc_: e.tensor_tensor(out=rc_.t[0:64, 0, :], in0=a.t[0:64, 0, 7:519],
                                                             in1=rc_.t[0:64, 0, :], op=ALU.mult), reads=[a, rc_], writes=[rc_])
        ph.op("dve", lambda e, b=b, rc_=rc_: e.tensor_tensor(out=rc_.t[64:128, 0, :], in0=b.t[64:128, 0, 6:518],
                                                             in1=rc_.t[64:128, 0, :], op=ALU.mult), reads=[b, rc_], writes=[rc_])
        ph.op("pool", lambda e, a=a, b=b: e.tensor_tensor(out=a.t[:, 1, 0:521], in0=b.t[:, 1, 0:521], in1=b.t[:, 1, 4:525],
                                                          op=ALU.add), reads=[b], writes=[a])
        ph.op("pool", lambda e, a=a, b=b: e.tensor_tensor(out=b.t[64:128, 1, 0:513], in0=a.t[64:128, 1, 0:513],
                                                          in1=a.t[64:128, 1, 8:521], op=ALU.add), reads=[a], writes=[b])
        ph.op("dve", lambda e, a=a, rc_=rc_: e.tensor_tensor(out=rc_.t[0:64, 1, :], in0=a.t[0:64, 1, 4:516],
                                                             in1=rc_.t[0:64, 1, :], op=ALU.mult), reads=[a, rc_], writes=[rc_])
        ph.op("dve", lambda e, b=b, rc_=rc_: e.tensor_tensor(out=rc_.t[64:128, 1, :], in0=b.t[64:128, 1, 0:512],
                                                             in1=rc_.t[64:128, 1, :], op=ALU.mult), reads=[b, rc_], writes=[rc_])
        ph.op("dve", lambda e, u=u, rc_=rc_, pl_=pl_: e.tensor_tensor(out=pl_.t[:], in0=rc_.t[:], in1=u.t[:, :, 8:520],
                                                                      op=ALU.subtract), reads=[u, rc_], writes=[pl_])
        for cc in range(2):
            p = pbank()
            ph.op("pe", lambda e, p=p, cc=cc, pl_=pl_: e.matmul(p.t[:], wblk.t[:, cc, :], pl_.t[:, cc, :], start=True, stop=True),
                  reads=[wblk, pl_], writes=[p])
            ph.op("act", lambda e, p=p, cc=cc, ypl_=ypl_: e.activation(out=ypl_.t[:, cc, :], in_=p.t[:], func=AF.Identity,
                                                                        scale=psc.t[:, cc:cc + 1]), reads=[p, psc], writes=[ypl_])
        qc, yc = qca[0], yca[0]
        for cc in range(2):
            ph.dma("sp", qc, qc.t[:, cc, :], qcaT_d[cc * 128:(cc + 1) * 128, t0:t0 + 512], writes=[qc])
        for cp in range(2):
            pO = pbank()
            pS = pbank()
            n = 0
            for hh in range(2):
                for m in range(2):
                    ps_ = pbank()
                    ph.op("pe", lambda e, ps_=ps_, hh=hh, m=m, cp=cp, qc=qc: e.matmul(
                        ps_.t[:], kmT.t[hh * 64:(hh + 1) * 64, cp, m * 128:(m + 1) * 128], qc.t[hh * 64:(hh + 1) * 64, cp, :],
                        start=True, stop=True), reads=[kmT, qc], writes=[ps_])
                    px = rot(pex, c["pex"]); c["pex"] += 1
                    ph.op("act", lambda e, ps_=ps_, px=px: e.activation(out=px.t[:], in_=ps_.t[:], func=AF.Exp, scale=0.125),
                          reads=[ps_], writes=[px])
                    ph.op("pe", lambda e, pO=pO, hh=hh, m=m, cp=cp, px=px, n=n: e.matmul(
                        pO.t[:], vpad[hh].t[:, m, cp, :], px.t[:], start=(n == 0), stop=(n == 3)),
                        reads=[vpad[hh], px], writes=[pO], signal=False)
                    ph.op("pe", lambda e, pS=pS, hh=hh, px=px, n=n: e.matmul(
                        pS.t[:], onesel[hh].t[:], px.t[:], start=(n == 0), stop=(n == 3)),
                        reads=[onesel[hh], px], writes=[pS])
                    n += 1
            r_ = rcp[cp]
            ph.op("dve", lambda e, r_=r_, pS=pS: e.reciprocal(out=r_.t[:], in_=pS.t[:]), reads=[pS], writes=[r_])
            ph.op("dve", lambda e, r_=r_, pO=pO, cp=cp, yc=yc: e.tensor_tensor(out=yc.t[:, cp, :], in0=pO.t[:], in1=r_.t[:],
                                                                                op=ALU.mult), reads=[pO, r_], writes=[yc])
        yd = yda[0]
        for hh in range(4):
            ph.dma("sp", yd, yd.t[:, hh, :], ydaT_d[hh * 128:(hh + 1) * 128, t0:t0 + 512], writes=[yd])
        mt = mT[0]
        for oc in range(8):
            brs = []
            for (wt, src, nk) in ((wbp, ypl_, 2), (wbd, yd, 4), (wbc, yc, 2)):
                p = pbank()
                for kc in range(nk):
                    ph.op("pe", lambda e, p=p, wt=wt, src=src, kc=kc, oc=oc, nk=nk: e.matmul(
                        p.t[:], wt.t[:, kc, oc * 128:(oc + 1) * 128], src.t[:, kc, :], start=(kc == 0), stop=(kc == nk - 1)),
                        reads=[wt, src], writes=[p], signal=(kc == nk - 1))
                brs.append(p)
            us = []
            for x_ in range(3):
                p = pbank()
                col = x_ * 1024 + oc * 128
                for kc in range(8):
                    ph.op("pe", lambda e, p=p, kc=kc, col=col, hs=hs: e.matmul(p.t[:], wg.t[:, kc, col:col + 128], hs.t[:, kc, :],
                                                                                start=(kc == 0), stop=(kc == 7)),
                          reads=[wg, hs], writes=[p], signal=(kc == 7))
                t_ = rot(tg, c["tg"]); c["tg"] += 1
                u_ = rot(um, c["um"]); c["um"] += 1
                bi = x_ * 8 + oc
                ph.op("act", lambda e, p=p, t_=t_, bi=bi: e.activation(out=t_.t[:], in_=p.t[:], func=AF.Tanh,
                                                                        bias=bg.t[:, bi:bi + 1], scale=0.5), reads=[p, bg], writes=[t_])
                br = brs[x_]
                ph.op("dve", lambda e, t_=t_, u_=u_, br=br: e.scalar_tensor_tensor(out=u_.t[:], in0=t_.t[:], scalar=1.0, in1=br.t[:],
                                                                                   op0=ALU.add, op1=ALU.mult), reads=[t_, br], writes=[u_])
                us.append(u_)
            ph.op("pool", lambda e, us=us: e.tensor_tensor(out=us[0].t[:], in0=us[0].t[:], in1=us[1].t[:], op=ALU.add),
                  reads=[us[0], us[1]], writes=[us[0]])
            ph.op("pool", lambda e, us=us, mt=mt, oc=oc: e.tensor_tensor(out=mt.t[:, oc, :], in0=us[0].t[:], in1=us[2].t[:], op=ALU.add),
                  reads=[us[0], us[2]], writes=[mt])
        if blk == 0:
            dump("memT", memT, [128, 8, NMEM], BF16)
            dump("kmT", kmT, [128, 2, NMEM], BF16)
            dump("vpad0", vpad[0], [128, 2, 2, 128], BF16)
            dump("vpad1", vpad[1], [128, 2, 2, 128], BF16)
            dump("pl", pl_, [128, 2, 512], BF16)
            dump("ypl", ypl_, [128, 2, 512], BF16)
            dump("yca", yc, [128, 2, 512], BF16)
            dump("mt", mt, [128, 8, 512], BF16)
            dump("U", u, [128, 2, 528], F32)
        for t in range(4):
            r0 = t0 + t * 128
            x = rot(xt, c["x"]); c["x"] += 1
            y = rot(yb, t)
            ph.dma("sp", x, x.t[:], x_src[r0:r0 + 128, :], writes=[x])
            for hf in range(2):
                for kc in range(8):
                    ph.op("pe", lambda e, kc=kc, hf=hf, t=t, mt=mt: e.matmul(
                        po.t[:, hf * 512:(hf + 1) * 512], mt.t[:, kc, t * 128:(t + 1) * 128], wo.t[:, kc, hf * 512:(hf + 1) * 512],
                        start=(kc == 0), stop=(kc == 7)), reads=[mt, wo], writes=[po], signal=(kc == 7))
            norm_tile(x, gpost, y, src_t=po)
            ph.op("pool", lambda e, x=x, y=y: e.tensor_tensor(out=x.t[:], in0=x.t[:], in1=y.t[:], op=ALU.add),
                  reads=[x, y], writes=[x])
            ph.dma("sp", x, x_dst[r0:r0 + 128, :], x.t[:], reads=[x])
    ph.flush()


W_NAMES = ["ffn1_pre_g", "ffn1_w_up", "ffn1_w_down", "ffn1_post_g", "mix_pre_g", "w_in", "pool_w", "pool_scale",
           "da_lambda_q1", "da_lambda_k1", "da_lambda_q2", "da_lambda_k2", "da_subln_g", "mem_norm_g", "w_mem_kv",
           "w_gate", "b_gate", "w_br_pool", "w_br_da", "w_br_ca", "w_out", "mix_post_g", "ffn2_pre_g", "ffn2_w_up",
           "ffn2_w_down", "ffn2_post_g"]
W_SHAPES = {"ffn1_pre_g": [D], "ffn1_w_up": [D, 2 * DFF], "ffn1_w_down": [DFF, D], "ffn1_post_g": [D], "mix_pre_g": [D],
            "w_in": [D, 2048], "pool_w": [4, 64, 64], "pool_scale": [256], "da_lambda_q1": [64], "da_lambda_k1": [64],
            "da_lambda_q2": [64], "da_lambda_k2": [64], "da_subln_g": [128], "mem_norm_g": [D], "w_mem_kv": [D, 512],
            "w_gate": [D, 3072], "b_gate": [3072], "w_br_pool": [256, D], "w_br_da": [512, D], "w_br_ca": [256, D],
            "w_out": [D, D], "mix_post_g": [D], "ffn2_pre_g": [D], "ffn2_w_up": [D, 2 * DFF], "ffn2_w_down": [DFF, D],
            "ffn2_post_g": [D]}


DBG_ITEMS = []


def debug_dump(nc, items):
    sem = nc.alloc_semaphore(name="dbg_sem")
    with nc.Block() as block:
        @block.sync
        def _(e):
            n = 0
            for name, ap, shape, dt in items:
                out = nc.dram_tensor("dbg_" + name, shape, dt, kind="ExternalOutput").ap()
                rows = shape[0]
                step = max(1, rows // 8)
                for r in range(0, rows, step):
                    e.dma_start(out=out[r:r + step], in_=ap[r:r + step]).then_inc(sem, 16)
                    n += 1
            e.wait_ge(sem, 16 * n)
    nc.clear_and_free_semaphores([sem])
    nc.all_engine_barrier()


def build(depth=DEPTH, ffn_T=1024, debug=False):
    import math
    nc = bass.Bass("TRN2", target_bir_lowering=False)
    x_in = nc.dram_tensor("x", [NTOK, D], F32, kind="ExternalInput").ap()
    mem_d = nc.dram_tensor("mem", [NMEM, D], F32, kind="ExternalInput").ap()
    pos_d = nc.dram_tensor("positions", [NTOK], I32, kind="ExternalInput").ap()
    ident_d = nc.dram_tensor("ident", [128, 128], F32, kind="ExternalInput").ap()
    ropec_d = nc.dram_tensor("ropec", [128, 2], F32, kind="ExternalInput").ap()
    hmask_d = nc.dram_tensor("hmask", [128, 2], F32, kind="ExternalInput").ap()
    rcnt_d = nc.dram_tensor("rcnt", [2, 128, NTOK], F32, kind="ExternalInput").ap()
    W = {}
    for n in W_NAMES:
        W[n] = nc.dram_tensor(n, [depth] + W_SHAPES[n], F32, kind="ExternalInput").ap()
    y_out = nc.dram_tensor("y", [NTOK, D], F32, kind="ExternalOutput").ap()

    def scr(name, shape, dt):
        return nc.dram_tensor(name, shape, dt).ap()

    xs = [scr(f"xs{i}", [NTOK, D], F32) for i in range(3)]
    cos_d = scr("cos_t", [128, NTOK], F32)
    sin_d = scr("sin_t", [128, NTOK], F32)
    h2T_d = scr("h2T", [128, 8, NTOK], BF16)
    qT_d = scr("qT", [512, NTOK], BF16)
    kown_d = scr("kown", [512, NTOK], BF16)
    vown_d = scr("vown", [NTOK, 512], BF16)
    upT_d = scr("upT", [2, 128, NTOK], F32)
    qcaT_d = scr("qcaT", [256, NTOK], BF16)
    edge_d = scr("edge", [256, 16], F32)
    edgefull_d = scr("edgefull", [512, 16], F32)
    kfull_d = [scr(f"kfull{h}", [256, NTOK], BF16) for h in range(4)]
    vfull_d = [scr(f"vfull{p}", [2048, 512], BF16) for p in range(4)]
    ydaT_d = scr("ydaT", [512, NTOK], BF16)

    rope_phase(nc, "R", pos_d, ropec_d, cos_d, sin_d)
    cur = x_in
    for l in range(depth):
        lam_init = 0.8 - 0.6 * math.exp(-0.3 * l)
        ffn_phase(nc, f"A{l}", cur, xs[0], W["ffn1_w_up"][l], W["ffn1_w_down"][l], W["ffn1_pre_g"][l], W["ffn1_post_g"][l],
                  ident_d, T=ffn_T)
        proj_phase(nc, f"B{l}", xs[0], W["mix_pre_g"][l], W["w_in"][l], ident_d, cos_d, sin_d, h2T_d, qT_d, kown_d, vown_d,
                   upT_d, qcaT_d, edge_d)
        items = [(kown_d[h * 128:(h + 1) * 128, :], kfull_d[h]) for h in range(4)]
        items += [(vown_d[p * 1024:(p + 1) * 1024, :], vfull_d[p]) for p in range(4)]
        items += [(edge_d, edgefull_d)]
        exchange_phase(nc, f"X{l}", items)
        attn_phase(nc, f"C{l}", qT_d, kfull_d, vfull_d, W["da_lambda_q1"][l], W["da_lambda_k1"][l], W["da_lambda_q2"][l],
                   W["da_lambda_k2"][l], W["da_subln_g"][l], lam_init, ydaT_d)
        merge_phase(nc, f"M{l}", xs[0], xs[1], ident_d, h2T_d, upT_d, edgefull_d, hmask_d, rcnt_d, qcaT_d, ydaT_d, mem_d,
                    W["pool_w"][l], W["pool_scale"][l], W["mem_norm_g"][l], W["w_mem_kv"][l], W["w_gate"][l], W["b_gate"][l],
                    W["w_br_pool"][l], W["w_br_da"][l], W["w_br_ca"][l], W["w_out"][l], W["mix_post_g"][l], dbg=False)
        dst = y_out if l == depth - 1 else xs[2]
        ffn_phase(nc, f"D{l}", xs[1], dst, W["ffn2_w_up"][l], W["ffn2_w_down"][l], W["ffn2_pre_g"][l], W["ffn2_post_g"][l],
                  ident_d, T=ffn_T)
        cur = xs[2]
    if debug:
        debug_dump(nc, [("x1", xs[0], [NTOK, D], F32), ("x2", xs[1], [NTOK, D], F32), ("cos", cos_d, [128, NTOK], F32),
                        ("sin", sin_d, [128, NTOK], F32), ("h2T", h2T_d, [128, 8, NTOK], BF16), ("qT", qT_d, [512, NTOK], BF16),
                        ("kfull0", kfull_d[0], [256, NTOK], BF16), ("vfull0", vfull_d[0], [2048, 512], BF16),
                        ("upT", upT_d, [2, 128, NTOK], F32), ("qcaT", qcaT_d, [256, NTOK], BF16),
                        ("edgefull", edgefull_d, [512, 16], F32), ("ydaT", ydaT_d, [512, NTOK], BF16)] + DBG_ITEMS)
    return nc


def host_consts():
    ident = np.eye(128, dtype=np.float32)
    inv = (np.float32(500000.0) ** (-np.arange(0, 16, 2, dtype=np.float32) / np.float32(16))).astype(np.float32)
    ropec = np.zeros((128, 2), np.float32)
    for p in range(128):
        d = p % 64
        if d < 16:
            ropec[p, 0] = inv[d % 8]
            ropec[p, 1] = -1.0 if d < 8 else 1.0
    rc = []
    pos = np.arange(SEQ)
    for w in (2, 4, 8, 16):
        lo = np.clip(pos - w // 2, 0, SEQ)
        hi = np.clip(pos + w // 2, 0, SEQ)
        rc.append(np.repeat((1.0 / (hi - lo).astype(np.float32))[None, :], 64, axis=0))
    rcnt = np.concatenate(rc, axis=0).astype(np.float32).reshape(2, 128, SEQ)
    return ident, ropec, rcnt


def make_in_maps(inputs, depth=DEPTH, ncores=NCORES):
    ident, ropec, rcnt = host_consts()
    x = np.asarray(inputs["x"])
    mem = np.asarray(inputs["mem"])
    pos = np.asarray(inputs["positions"])
    maps = []
    for core in range(ncores):
        b, j = core // 2, core % 2
        m = {"x": np.ascontiguousarray(x[b, j * NTOK:(j + 1) * NTOK]),
             "mem": np.ascontiguousarray(mem[b]),
             "positions": np.ascontiguousarray(pos[b, j * NTOK:(j + 1) * NTOK]).astype(np.int32),
             "ident": ident, "ropec": ropec,
             "hmask": np.tile(np.array([[1.0 if j == 1 else 0.0, 1.0 if j == 0 else 0.0]], np.float32), (128, 1)),
             "rcnt": np.ascontiguousarray(rcnt[:, :, j * NTOK:(j + 1) * NTOK])}
        for n in W_NAMES:
            m[n] = np.ascontiguousarray(np.asarray(inputs[n])[:depth])
        maps.append(m)
    return maps


_NC_CACHE = {}


def kernel(**inputs):
    if "nc" not in _NC_CACHE:
        _NC_CACHE["nc"] = build()
    nc = _NC_CACHE["nc"]
    maps = make_in_maps(inputs)
    res = run_bass_kernel_spmd(nc, maps, core_ids=list(range(NCORES)))
    out = np.empty((4, SEQ, D), np.float32)
    for core in range(NCORES):
        b, j = core // 2, core % 2
        out[b, j * NTOK:(j + 1) * NTOK] = res.results[core]["y"]
    return out
```

```python
import numpy as np
from contextlib import ExitStack
import concourse.bass as bass
import concourse.mybir as mybir
from concourse.bass_utils import run_bass_kernel_spmd

F32 = mybir.dt.float32
BF16 = mybir.dt.bfloat16
I32 = mybir.dt.int32
AF = mybir.ActivationFunctionType
ALU = mybir.AluOpType

D = 1024
DFF = 2816
NTOK = 4096
SEQ = 8192
DEPTH = 4
NMEM = 256
EPS = 1e-6
NCORES = 8


class Buf:
    __slots__ = ("w", "r")

    def __init__(self):
        self.w = None
        self.r = {}


class Tl:
    def __init__(self, t):
        self.t = t
        self.b = Buf()
        self.g = None


class Grp:
    def __init__(self, sem):
        self.sem = sem
        self.n = 0


class Phase:
    ENG = ("pe", "act", "dve", "pool", "sp")

    def __init__(self, nc, name):
        self.nc = nc
        self.name = name
        self.stack = ExitStack()
        self.all_sems = []
        self.sem = {e: self._sem(f"{name}_{e}") for e in self.ENG}
        self.cnt = {e: 0 for e in self.ENG}
        self.thunks = {e: [] for e in self.ENG}
        self.seen = {e: {} for e in self.ENG}
        self.unsig = {e: False for e in self.ENG}
        self.grps = []
        self.nt = 0

    def _sem(self, name):
        h = self.nc.alloc_semaphore(name=name)
        self.all_sems.append(h)
        return h

    def sbuf(self, shape, dt, name=None):
        self.nt += 1
        return Tl(self.stack.enter_context(self.nc.sbuf_tensor(f"{self.name}_{name or 't'}{self.nt}", list(shape), dt)))

    def psum(self, shape, dt, name=None):
        self.nt += 1
        return Tl(self.stack.enter_context(self.nc.psum_tensor(f"{self.name}_{name or 'p'}{self.nt}", list(shape), dt)))

    def grp(self):
        self.nt += 1
        g = Grp(self._sem(f"{self.name}_g{self.nt}"))
        self.grps.append(g)
        return g

    def _wait(self, eng, key, val):
        if key == eng and eng == "pe":
            return
        s = self.seen[eng]
        if s.get(key, 0) >= val:
            return
        s[key] = val
        sem = self.sem[key] if isinstance(key, str) else key.sem
        self.thunks[eng].append(lambda e, sem=sem, val=val: e.wait_ge(sem, val))

    def _deps(self, eng, reads, writes):
        for b in reads:
            if b.w is not None:
                self._wait(eng, *b.w)
        for b in writes:
            if b.w is not None:
                self._wait(eng, *b.w)
            for k, v in b.r.items():
                self._wait(eng, k, v)

    def _record(self, ev, reads, writes):
        k, v = ev
        for b in reads:
            if b.r.get(k, 0) < v:
                b.r[k] = v
        for b in writes:
            b.w = ev
            b.r = {}

    def op(self, eng, fn, reads=(), writes=(), signal=True):
        reads = [x.b if isinstance(x, Tl) else x for x in reads]
        writes = [x.b if isinstance(x, Tl) else x for x in writes]
        self._deps(eng, reads, writes)
        if signal:
            self.cnt[eng] += 1
            ev = (eng, self.cnt[eng])
            sem = self.sem[eng]
            self.thunks[eng].append(lambda e, fn=fn, sem=sem: fn(e).then_inc(sem, 1))
            self.unsig[eng] = False
        else:
            assert eng == "pe"
            ev = (eng, self.cnt[eng] + 1)
            self.thunks[eng].append(fn)
            self.unsig[eng] = True
        self._record(ev, reads, writes)

    def dma(self, eng, tl, out, in_, reads=(), writes=(), **kw):
        if tl.g is None:
            tl.g = self.grp()
        grp = tl.g
        reads = [x.b if isinstance(x, Tl) else x for x in reads]
        writes = [x.b if isinstance(x, Tl) else x for x in writes]
        self._deps(eng, reads, writes)
        grp.n += 1
        ev = (grp, grp.n * 16)
        sem = grp.sem
        self.thunks[eng].append(lambda e, out=out, in_=in_, sem=sem: e.dma_start(out=out, in_=in_, **kw).then_inc(sem, 16))
        self._record(ev, reads, writes)

    def flush(self):
        nc = self.nc
        for e in self.ENG:
            assert not self.unsig[e], (self.name, e)
        for g in self.grps:
            if g.n:
                self._wait("sp", g, g.n * 16)
        th = self.thunks
        with nc.Block() as block:
            @block.tensor
            def _(e):
                for f in th["pe"]:
                    f(e)

            @block.scalar
            def _(e):
                for f in th["act"]:
                    f(e)

            @block.vector
            def _(e):
                for f in th["dve"]:
                    f(e)

            @block.gpsimd
            def _(e):
                for f in th["pool"]:
                    f(e)

            @block.sync
            def _(e):
                for f in th["sp"]:
                    f(e)
        nc.clear_and_free_semaphores(self.all_sems)
        nc.all_engine_barrier()
        self.stack.close()


def rot(lst, i):
    return lst[i % len(lst)]


def emit_rstd(ph, ss, v, r, cst, scale=1.0):
    ph.op("dve", lambda e: e.tensor_scalar(out=v.t[:, 0:1], in0=ss.t[:, 0:1], scalar1=1.0 / D, scalar2=EPS,
                                           op0=ALU.mult, op1=ALU.add), reads=[ss], writes=[v])
    ph.op("pool", lambda e: e.tensor_tensor(out=r.t[:, 0:1], in0=v.t[:, 0:1], in1=cst["mhalf"].t[:, 0:1], op=ALU.pow),
          reads=[v, cst["mhalf"]], writes=[r])


def load_consts(ph, ident_d):
    cst = {}
    cst["ident"] = ph.sbuf([128, 128], BF16, "ident")
    cst["mhalf"] = ph.sbuf([128, 1], F32, "mhalf")
    ph.dma("pool", cst["ident"], cst["ident"].t[:], ident_d, writes=[cst["ident"]])
    ph.op("pool", lambda e: e.memset(cst["mhalf"].t[:], -0.5), writes=[cst["mhalf"]])
    return cst


def bcast_row(ap_row, n=128):
    return ap_row.partition_broadcast(n)


def ffn_phase(nc, name, x_src, x_dst, w_up, w_down, g_pre, g_post, ident_d, T=1024):
    ph = Phase(nc, name)
    cst = load_consts(ph, ident_d)
    NSB = NTOK // T
    TT = T // 128
    NB = T // 512
    NJ = DFF // 128
    JG = 2
    NG = NJ // JG
    w_up_v = w_up.rearrange("(kc p) n -> p kc n", p=128)
    w_dn_v = w_down.rearrange("(j p) n -> p j n", p=128)

    gpre = ph.sbuf([128, D], F32, "gpre")
    gpost = ph.sbuf([128, D], F32, "gpost")
    wd = ph.sbuf([128, NJ, D], BF16, "wd")
    hT = [ph.sbuf([128, 8, T], BF16, "hT") for _ in range(2)]
    gT = ph.sbuf([128, NJ, T], BF16, "gT")
    gTb = [Buf() for _ in range(NJ)]
    wu = [ph.sbuf([128, 8, 2 * JG * 128], BF16, "wu") for _ in range(3)]
    xt = [ph.sbuf([128, D], F32, "xt") for _ in range(3)]
    xr = [ph.sbuf([128, D], F32, "xr") for _ in range(2)]
    hb = [ph.sbuf([128, D], BF16, "hb") for _ in range(2)]
    junk = [ph.sbuf([128, D], BF16, "junk") for _ in range(2)]
    yb = [ph.sbuf([128, D], F32, "yb") for _ in range(2)]
    sa = [ph.sbuf([128, 512], F32, "sa") for _ in range(2)]
    st = [ph.sbuf([128, 4], F32, "st") for _ in range(4)]
    po = ph.psum([128, D], F32, "po")
    pT = [ph.psum([128, 8, 128], BF16, "pT") for _ in range(2)]
    pa = [ph.psum([128, 512], F32, "pa") for _ in range(2)]
    pb = [ph.psum([128, 512], F32, "pb") for _ in range(2)]

    ph.dma("sp", gpre, gpre.t[:], bcast_row(g_pre), writes=[gpre])
    ph.dma("sp", gpost, gpost.t[:], bcast_row(g_post), writes=[gpost])
    ph.op("dve", lambda e: e.tensor_scalar(out=gpost.t[:], in0=gpost.t[:], scalar1=0.5, scalar2=None, op0=ALU.mult),
          reads=[gpost], writes=[gpost])
    for q in range(0, NJ, 6):
        q1 = min(NJ, q + 6)
        ph.dma("pool", wd, wd.t[:, q:q1, :], w_dn_v[:, q:q1, :], writes=[wd])

    cnt = {"x": 0, "st": 0, "hb": 0, "pT": 0}

    def s1_tile(sb, t):
        r0 = sb * T + t * 128
        x = rot(xt, cnt["x"])
        cnt["x"] += 1
        s = rot(st, cnt["st"])
        cnt["st"] += 1
        jk = rot(junk, cnt["st"])
        h = rot(hb, cnt["hb"])
        cnt["hb"] += 1
        p = rot(pT, cnt["pT"])
        cnt["pT"] += 1
        hd = hT[sb % 2]

        def prep():
            ph.dma("sp", x, x.t[:], x_src[r0:r0 + 128, :], writes=[x])
            ph.op("act", lambda e: e.activation(out=jk.t[:], in_=x.t[:], func=AF.Square, accum_out=s.t[:, 0:1]),
                  reads=[x], writes=[jk, s])
            ph.op("dve", lambda e: e.tensor_scalar(out=s.t[:, 1:2], in0=s.t[:, 0:1], scalar1=1.0 / D, scalar2=EPS,
                                                   op0=ALU.mult, op1=ALU.add), reads=[s], writes=[s])
            ph.op("pool", lambda e: e.tensor_tensor(out=s.t[:, 2:3], in0=s.t[:, 1:2], in1=cst["mhalf"].t[:, 0:1],
                                                    op=ALU.pow), reads=[s, cst["mhalf"]], writes=[s])
            ph.op("dve", lambda e: e.scalar_tensor_tensor(out=h.t[:], in0=x.t[:], scalar=s.t[:, 2:3], in1=gpre.t[:],
                                                          op0=ALU.mult, op1=ALU.mult), reads=[x, s, gpre], writes=[h])

        def xpose():
            for kc in range(8):
                ph.op("pe", lambda e, kc=kc: e.transpose(out=p.t[:, kc, :], in_=h.t[:, kc * 128:(kc + 1) * 128],
                                                         identity=cst["ident"].t[:]),
                      reads=[h, cst["ident"]], writes=[p], signal=(kc == 7))
            ph.op("act", lambda e: e.activation(out=hd.t[:, :, t * 128:(t + 1) * 128], in_=p.t[:], func=AF.Copy),
                  reads=[p], writes=[hd])

        return prep, xpose

    def issue_w(gi):
        if gi >= NSB * NG:
            return
        w = rot(wu, gi)
        c0 = (gi % NG) * JG * 128
        W = JG * 128
        ph.dma("pool", w, w.t[:, :, 0:W], w_up_v[:, :, c0:c0 + W], writes=[w])
        ph.dma("pool", w, w.t[:, :, W:2 * W], w_up_v[:, :, DFF + c0:DFF + c0 + W], writes=[w])

    def s2_group(sb, jg, gi):
        w = rot(wu, gi)
        W = JG * 128
        hs = hT[sb % 2]
        k = 0
        for jj in range(JG):
            j = jg * JG + jj
            for nb in range(NB):
                a = rot(pa, gi * JG * NB + k)
                b = rot(pb, gi * JG * NB + k)
                s_ = rot(sa, gi * JG * NB + k)
                k += 1
                for kc in range(8):
                    ph.op("pe", lambda e, a=a, w=w, hs=hs, kc=kc, jj=jj, nb=nb: e.matmul(
                        a.t[:], w.t[:, kc, jj * 128:(jj + 1) * 128], hs.t[:, kc, nb * 512:(nb + 1) * 512],
                        start=(kc == 0), stop=(kc == 7)), reads=[w, hs], writes=[a], signal=(kc == 7))
                for kc in range(8):
                    ph.op("pe", lambda e, b=b, w=w, hs=hs, kc=kc, jj=jj, nb=nb: e.matmul(
                        b.t[:], w.t[:, kc, W + jj * 128:W + (jj + 1) * 128], hs.t[:, kc, nb * 512:(nb + 1) * 512],
                        start=(kc == 0), stop=(kc == 7)), reads=[w, hs], writes=[b], signal=(kc == 7))
                ph.op("act", lambda e, a=a, s_=s_: e.activation(out=s_.t[:], in_=a.t[:], func=AF.Silu),
                      reads=[a], writes=[s_])
                ph.op("dve", lambda e, b=b, s_=s_, j=j, nb=nb: e.tensor_tensor(
                    out=gT.t[:, j, nb * 512:(nb + 1) * 512], in0=s_.t[:], in1=b.t[:], op=ALU.mult),
                    reads=[s_, b], writes=[gTb[j]])

    def s3(sb):
        for t in range(TT):
            r0 = sb * T + t * 128
            x = rot(xr, t)
            y = rot(yb, t)
            s = rot(st, cnt["st"])
            cnt["st"] += 1
            jk = rot(junk, cnt["st"])
            ph.dma("sp", x, x.t[:], x_src[r0:r0 + 128, :], writes=[x])
            for hf in range(2):
                for j in range(NJ):
                    ph.op("pe", lambda e, j=j, hf=hf, t=t: e.matmul(
                        po.t[:, hf * 512:(hf + 1) * 512], gT.t[:, j, t * 128:(t + 1) * 128],
                        wd.t[:, j, hf * 512:(hf + 1) * 512], start=(j == 0), stop=(j == NJ - 1)),
                        reads=[gTb[j], wd], writes=[po], signal=(j == NJ - 1))
            ph.op("act", lambda e, y=y: e.activation(out=y.t[:], in_=po.t[:], func=AF.Copy), reads=[po], writes=[y])
            ph.op("act", lambda e, jk=jk, s=s, y=y: e.activation(out=jk.t[:], in_=y.t[:], func=AF.Square,
                                                                 accum_out=s.t[:, 0:1]), reads=[y], writes=[jk, s])
            ph.op("dve", lambda e, s=s: e.tensor_scalar(out=s.t[:, 1:2], in0=s.t[:, 0:1], scalar1=1.0 / D, scalar2=EPS,
                                                        op0=ALU.mult, op1=ALU.add), reads=[s], writes=[s])
            ph.op("pool", lambda e, s=s: e.tensor_tensor(out=s.t[:, 2:3], in0=s.t[:, 1:2], in1=cst["mhalf"].t[:, 0:1],
                                                         op=ALU.pow), reads=[s, cst["mhalf"]], writes=[s])
            ph.op("dve", lambda e, s=s, y=y: e.scalar_tensor_tensor(out=y.t[:], in0=y.t[:], scalar=s.t[:, 2:3],
                                                                    in1=gpost.t[:], op0=ALU.mult, op1=ALU.mult),
                  reads=[y, s, gpost], writes=[y])
            ph.op("pool", lambda e, x=x, y=y: e.tensor_tensor(out=x.t[:], in0=x.t[:], in1=y.t[:], op=ALU.add),
                  reads=[x, y], writes=[x])
            ph.dma("sp", x, x_dst[r0:r0 + 128, :], x.t[:], reads=[x])

    prev = None
    for t in range(TT):
        pr, xp = s1_tile(0, t)
        pr()
        if prev is not None:
            prev()
        prev = xp
    prev()
    gi = 0
    issue_w(0)
    issue_w(1)
    for sb in range(NSB):
        pend = None
        for jg in range(NG):
            issue_w(gi + 2)
            nxt_x = None
            if sb + 1 < NSB and jg < TT:
                pr, nxt_x = s1_tile(sb + 1, jg)
                pr()
            s2_group(sb, jg, gi)
            gi += 1
            if pend is not None:
                pend()
            pend = nxt_x
        if pend is not None:
            pend()
        s3(sb)
    ph.flush()


TWO_PI = 6.283185307179586
C1 = 6.28125
C2 = TWO_PI - 6.28125
MAGIC = 12582912.0
PI_LO = 3.1415925


def rope_phase(nc, name, pos_d, ropec_d, cos_d, sin_d):
    ph = Phase(nc, name)
    posi = ph.sbuf([128, NTOK], I32, "posi")
    ang = ph.sbuf([128, NTOK], F32, "ang")
    t1 = ph.sbuf([128, NTOK], F32, "t1")
    t2 = ph.sbuf([128, NTOK], F32, "t2")
    rc = ph.sbuf([128, 2], F32, "rc")
    one = ph.sbuf([128, 1], F32, "one")
    ph.dma("sp", posi, posi.t[:], pos_d.partition_broadcast(128), writes=[posi])
    ph.dma("sp", rc, rc.t[:], ropec_d, writes=[rc])
    ph.op("pool", lambda e: e.memset(one.t[:], 1.0), writes=[one])
    ph.op("dve", lambda e: e.tensor_copy(out=ang.t[:], in_=posi.t[:]), reads=[posi], writes=[ang])
    ph.op("dve", lambda e: e.tensor_scalar(out=ang.t[:], in0=ang.t[:], scalar1=rc.t[:, 0:1], scalar2=None, op0=ALU.mult),
          reads=[ang, rc], writes=[ang])
    for which, dst, scale_ap in (("sin", sin_d, rc), ("cos", cos_d, one)):
        off = 0.0 if which == "sin" else TWO_PI / 4
        ph.op("dve", lambda e, off=off: e.tensor_scalar(out=t1.t[:], in0=ang.t[:], scalar1=off, scalar2=None, op0=ALU.add),
              reads=[ang], writes=[t1])
        ph.op("dve", lambda e: e.tensor_scalar(out=t2.t[:], in0=t1.t[:], scalar1=1.0 / TWO_PI, scalar2=MAGIC,
                                               op0=ALU.mult, op1=ALU.add), reads=[t1], writes=[t2])
        ph.op("dve", lambda e: e.tensor_scalar(out=t2.t[:], in0=t2.t[:], scalar1=-MAGIC, scalar2=None, op0=ALU.add),
              reads=[t2], writes=[t2])
        ph.op("dve", lambda e: e.scalar_tensor_tensor(out=t1.t[:], in0=t2.t[:], scalar=-C1, in1=t1.t[:],
                                                      op0=ALU.mult, op1=ALU.add), reads=[t1, t2], writes=[t1])
        ph.op("dve", lambda e: e.scalar_tensor_tensor(out=t1.t[:], in0=t2.t[:], scalar=-C2, in1=t1.t[:],
                                                      op0=ALU.mult, op1=ALU.add), reads=[t1, t2], writes=[t1])
        ph.op("dve", lambda e: e.tensor_scalar(out=t1.t[:], in0=t1.t[:], scalar1=-PI_LO, scalar2=PI_LO,
                                               op0=ALU.max, op1=ALU.min), reads=[t1], writes=[t1])
        sc = scale_ap.t[:, 1:2] if which == "sin" else scale_ap.t[:, 0:1]
        ph.op("act", lambda e, sc=sc: e.activation(out=t2.t[:], in_=t1.t[:], func=AF.Sin, scale=sc),
              reads=[t1, scale_ap], writes=[t2])
        ph.dma("sp", t2, dst, t2.t[:], reads=[t2])
    ph.flush()


def proj_phase(nc, name, x_src, g_pre, w_in, ident_d, cos_d, sin_d, h2T_d, qT_d, kown_d, vown_d, upT_d, qcaT_d, edge_d):
    ph = Phase(nc, name)
    cst = load_consts(ph, ident_d)
    w_in_v = w_in.rearrange("(kc p) n -> p kc n", p=128)
    gpre = ph.sbuf([128, D], F32, "gpre")
    win = ph.sbuf([128, 8, 2048], BF16, "win")
    wsw = ph.sbuf([128, 8, 1024], BF16, "wsw")
    hT = [ph.sbuf([128, 8, 512], BF16, "hT") for _ in range(2)]
    xt = [ph.sbuf([128, D], F32, "xt") for _ in range(4)]
    hb = [ph.sbuf([128, D], BF16, "hb") for _ in range(4)]
    junk = [ph.sbuf([128, D], BF16, "junk") for _ in range(2)]
    st = [ph.sbuf([128, 4], F32, "st") for _ in range(4)]
    cosb = [ph.sbuf([128, 512], F32, "cosb") for _ in range(2)]
    sinb = [ph.sbuf([128, 512], F32, "sinb") for _ in range(2)]
    r1 = [ph.sbuf([128, 512], F32, "r1") for _ in range(2)]
    r2 = [ph.sbuf([128, 512], F32, "r2") for _ in range(2)]
    ob = [ph.sbuf([128, 512], BF16, "ob") for _ in range(4)]
    of = [ph.sbuf([128, 512], F32, "of") for _ in range(2)]
    pT = [ph.psum([128, 8, 128], BF16, "pT") for _ in range(2)]
    pp = [ph.psum([128, 512], F32, "pp") for _ in range(6)]
    c = {"x": 0, "st": 0, "hb": 0, "pT": 0, "pp": 0, "ob": 0, "of": 0, "r": 0}

    ph.dma("sp", gpre, gpre.t[:], bcast_row(g_pre), writes=[gpre])
    for kc in range(8):
        ph.dma("pool", win, win.t[:, kc, :], w_in_v[:, kc, :], writes=[win])
    ph.op("pool", lambda e: e.memset(wsw.t[:], 0.0), writes=[wsw])
    src4 = win.t[:, :, 256:1280].rearrange("p k (b d) -> p k b d", d=64)
    dst4 = wsw.t[:].rearrange("p k (b d) -> p k b d", d=64)
    for kc in range(8):
        ph.op("pool", lambda e, kc=kc: e.tensor_copy(out=dst4[:, kc, :, 0:8], in_=src4[:, kc, :, 8:16]), reads=[win], writes=[wsw])
        ph.op("pool", lambda e, kc=kc: e.tensor_copy(out=dst4[:, kc, :, 8:16], in_=src4[:, kc, :, 0:8]), reads=[win], writes=[wsw])

    def pbank():
        p = rot(pp, c["pp"])
        c["pp"] += 1
        return p

    def fm_proj(p, wt, col0, hs):
        for kc in range(8):
            ph.op("pe", lambda e, kc=kc: e.matmul(p.t[:], wt.t[:, kc, col0:col0 + 128], hs.t[:, kc, :],
                                                  start=(kc == 0), stop=(kc == 7)), reads=[wt, hs], writes=[p], signal=(kc == 7))

    def s1_block(blk):
        hs = hT[blk % 2]
        preps, xposes = [], []
        for t in range(4):
            r0 = blk * 512 + t * 128
            x = rot(xt, c["x"]); c["x"] += 1
            s = rot(st, c["st"]); c["st"] += 1
            jk = rot(junk, c["st"])
            h = rot(hb, c["hb"]); c["hb"] += 1
            p = rot(pT, c["pT"]); c["pT"] += 1

            def prep(x=x, s=s, jk=jk, h=h, r0=r0):
                ph.dma("sp", x, x.t[:], x_src[r0:r0 + 128, :], writes=[x])
                ph.op("act", lambda e: e.activation(out=jk.t[:], in_=x.t[:], func=AF.Square, accum_out=s.t[:, 0:1]),
                      reads=[x], writes=[jk, s])
                ph.op("dve", lambda e: e.tensor_scalar(out=s.t[:, 1:2], in0=s.t[:, 0:1], scalar1=1.0 / D, scalar2=EPS,
                                                       op0=ALU.mult, op1=ALU.add), reads=[s], writes=[s])
                ph.op("pool", lambda e: e.tensor_tensor(out=s.t[:, 2:3], in0=s.t[:, 1:2], in1=cst["mhalf"].t[:, 0:1],
                                                        op=ALU.pow), reads=[s, cst["mhalf"]], writes=[s])
                ph.op("dve", lambda e: e.scalar_tensor_tensor(out=h.t[:], in0=x.t[:], scalar=s.t[:, 2:3], in1=gpre.t[:],
                                                              op0=ALU.mult, op1=ALU.mult), reads=[x, s, gpre], writes=[h])

            def xpose(h=h, p=p, t=t, hs=hs):
                for kc in range(8):
                    ph.op("pe", lambda e, kc=kc: e.transpose(out=p.t[:, kc, :], in_=h.t[:, kc * 128:(kc + 1) * 128],
                                                             identity=cst["ident"].t[:]),
                          reads=[h, cst["ident"]], writes=[p], signal=(kc == 7))
                ph.op("act", lambda e: e.activation(out=hs.t[:, :, t * 128:(t + 1) * 128], in_=p.t[:], func=AF.Copy),
                      reads=[p], writes=[hs])

            preps.append(prep)
            xposes.append(xpose)
        return preps, xposes

    NBLK = NTOK // 512
    pr0, xp0 = s1_block(0)
    for f in pr0:
        f()
    for f in xp0:
        f()
    for blk in range(NBLK):
        hs = hT[blk % 2]
        t0 = blk * 512
        nxt = s1_block(blk + 1) if blk + 1 < NBLK else None
        ph.dma("sp", hs, h2T_d[:, :, t0:t0 + 512], hs.t[:], reads=[hs])
        cb = cosb[blk % 2]
        sb_ = sinb[blk % 2]
        ph.dma("sp", cb, cb.t[:], cos_d[:, t0:t0 + 512], writes=[cb])
        ph.dma("sp", sb_, sb_.t[:], sin_d[:, t0:t0 + 512], writes=[sb_])
        for cc in range(2):
            p = pbank()
            fm_proj(p, win, cc * 128, hs)
            o = rot(of, c["of"]); c["of"] += 1
            ph.op("act", lambda e, p=p, o=o: e.activation(out=o.t[:], in_=p.t[:], func=AF.Copy), reads=[p], writes=[o])
            ph.dma("sp", o, upT_d[cc, :, t0:t0 + 512], o.t[:], reads=[o])
            if blk == 0:
                ph.dma("sp", o, edge_d[cc * 128:(cc + 1) * 128, 0:8], o.t[:, 0:8], reads=[o])
            if blk == NTOK // 512 - 1:
                ph.dma("sp", o, edge_d[cc * 128:(cc + 1) * 128, 8:16], o.t[:, 504:512], reads=[o])
        if nxt is not None:
            for f in nxt[0]:
                f()
        for which, dstd in (("q", qT_d), ("k", kown_d)):
            base = 256 if which == "q" else 768
            if which == "k" and nxt is not None:
                for f in nxt[1]:
                    f()
            for hh in range(4):
                p = pbank()
                ps = pbank()
                fm_proj(p, win, base + hh * 128, hs)
                fm_proj(ps, wsw, (base - 256) + hh * 128, hs)
                a = rot(r1, c["r"]); b = rot(r2, c["r"]); c["r"] += 1
                o = rot(ob, c["ob"]); c["ob"] += 1
                ph.op("dve", lambda e, p=p, a=a, cb=cb: e.tensor_tensor(out=a.t[:], in0=p.t[:], in1=cb.t[:], op=ALU.mult),
                      reads=[p, cb], writes=[a])
                ph.op("dve", lambda e, ps=ps, b=b, sb_=sb_: e.tensor_tensor(out=b.t[:], in0=ps.t[:], in1=sb_.t[:], op=ALU.mult),
                      reads=[ps, sb_], writes=[b])
                ph.op("pool", lambda e, a=a, b=b, o=o: e.tensor_tensor(out=o.t[:], in0=a.t[:], in1=b.t[:], op=ALU.add),
                      reads=[a, b], writes=[o])
                ph.dma("sp", o, dstd[hh * 128:(hh + 1) * 128, t0:t0 + 512], o.t[:], reads=[o])
        for cc in range(2):
            p = pbank()
            fm_proj(p, win, 1792 + cc * 128, hs)
            o = rot(ob, c["ob"]); c["ob"] += 1
            ph.op("act", lambda e, p=p, o=o: e.activation(out=o.t[:], in_=p.t[:], func=AF.Copy), reads=[p], writes=[o])
            ph.dma("sp", o, qcaT_d[cc * 128:(cc + 1) * 128, t0:t0 + 512], o.t[:], reads=[o])
        for t in range(4):
            r0 = t0 + t * 128
            p = pbank()
            for kc in range(8):
                ph.op("pe", lambda e, p=p, kc=kc, t=t, hs=hs: e.matmul(p.t[:], hs.t[:, kc, t * 128:(t + 1) * 128],
                                                                  win.t[:, kc, 1280:1792], start=(kc == 0), stop=(kc == 7)),
                      reads=[win, hs], writes=[p], signal=(kc == 7))
            o = rot(ob, c["ob"]); c["ob"] += 1
            ph.op("act", lambda e, p=p, o=o: e.activation(out=o.t[:], in_=p.t[:], func=AF.Copy), reads=[p], writes=[o])
            ph.dma("sp", o, vown_d[r0:r0 + 128, :], o.t[:], reads=[o])
    ph.flush()


PAIRS = [[0, 1], [2, 3], [4, 5], [6, 7]]


def exchange_phase(nc, name, items):
    sems = [nc.alloc_semaphore(name=f"{name}_cc{i}") for i in range(len(items))]
    with nc.Block() as block:
        @block.gpsimd
        def _(g):
            for (src, dst), sem in zip(items, sems):
                g.collective_compute("AllGather", ALU.bypass, replica_groups=PAIRS, ins=[src], outs=[dst]).then_inc(sem)
            for sem in sems:
                g.wait_ge(sem, 1)
    nc.clear_and_free_semaphores(sems)
    nc.all_engine_barrier()


def attn_phase(nc, name, qT_d, kfull_d, vfull_d, lamq1, lamk1, lamq2, lamk2, subg, lam_init, ydaT_d):
    ph = Phase(nc, name)
    NKT = SEQ // 128
    kT = [ph.sbuf([128, SEQ], BF16, "kT") for _ in range(2)]
    vt = [ph.sbuf([128, NKT, 128], BF16, "vt") for _ in range(2)]
    ones = ph.sbuf([128, 128], BF16, "ones")
    epsb = ph.sbuf([128, 1], F32, "epsb")
    lam4 = ph.sbuf([128, 4, 64], F32, "lam4")
    lj = ph.sbuf([128, 64], F32, "lj")
    lc = ph.sbuf([128, 8], F32, "lc")
    gs = ph.sbuf([128, 1], F32, "gs")
    qb = [ph.sbuf([128, 512], BF16, "qb") for _ in range(2)]
    pe_ = [ph.sbuf([128, 512], BF16, "pe") for _ in range(6)]
    rr = [ph.sbuf([128, 512], F32, "rr") for _ in range(2)]
    o1 = [ph.sbuf([128, 512], F32, "o1") for _ in range(2)]
    o2 = [ph.sbuf([128, 512], F32, "o2") for _ in range(2)]
    sq = [ph.sbuf([128, 512], BF16, "sq") for _ in range(2)]
    rs = [ph.sbuf([128, 512], F32, "rs") for _ in range(2)]
    yo = [ph.sbuf([128, 512], BF16, "yo") for _ in range(2)]
    accO = [ph.psum([128, 512], F32, "accO") for _ in range(2)]
    accS = [ph.psum([128, 512], F32, "accS") for _ in range(2)]
    scp = [[ph.psum([128, 512], F32, "sc") for _ in range(2)] for _ in range(2)]

    ph.op("pool", lambda e: e.memset(ones.t[:], 1.0), writes=[ones])
    ph.op("pool", lambda e: e.memset(epsb.t[:], EPS), writes=[epsb])
    for i, v in enumerate((lamq1, lamk1, lamq2, lamk2)):
        ph.dma("sp", lam4, lam4.t[:, i, :], v.partition_broadcast(128), writes=[lam4])
    ph.dma("sp", gs, gs.t[:], subg.rearrange("(p o) -> p o", o=1), writes=[gs])
    for i in range(2):
        ph.op("dve", lambda e, i=i: e.scalar_tensor_tensor(out=lj.t[:], in0=lam4.t[:, 2 * i, :], scalar=1.0,
                                                            in1=lam4.t[:, 2 * i + 1, :], op0=ALU.mult, op1=ALU.mult,
                                                            accum_out=lc.t[:, i:i + 1]), reads=[lam4], writes=[lj, lc])
    ph.op("act", lambda e: e.activation(out=lc.t[:, 2:4], in_=lc.t[:, 0:2], func=AF.Exp), reads=[lc], writes=[lc])
    ph.op("dve", lambda e: e.tensor_tensor(out=lc.t[:, 4:5], in0=lc.t[:, 2:3], in1=lc.t[:, 3:4], op=ALU.subtract),
          reads=[lc], writes=[lc])
    ph.op("dve", lambda e: e.tensor_scalar(out=lc.t[:, 5:6], in0=lc.t[:, 4:5], scalar1=lam_init, scalar2=-1.0,
                                           op0=ALU.add, op1=ALU.mult), reads=[lc], writes=[lc])
    ph.op("dve", lambda e: e.tensor_scalar(out=lc.t[:, 6:7], in0=gs.t[:, 0:1], scalar1=1.0 - lam_init, scalar2=None,
                                           op0=ALU.mult), reads=[gs, lc], writes=[lc])
    ss_ = [[ph.sbuf([128, 512], F32, "ssc") for _ in range(2)] for _ in range(2)]
    prs = [[ph.sbuf([128, 512], BF16, "prs") for _ in range(3)] for _ in range(2)]
    cq = 0
    cp_ = 0

    def load_kv(h):
        k = kT[h % 2]
        v = vt[h % 2]
        for r in range(2):
            ph.dma("sp", k, k.t[:, r * NTOK:(r + 1) * NTOK], kfull_d[h][r * 128:(r + 1) * 128, :], writes=[k])
            for pc in range(4):
                kt0 = r * 32 + pc * 8
                ph.dma("sp", v, v.t[:, kt0:kt0 + 8, :],
                       vfull_d[pc][r * 1024:(r + 1) * 1024, h * 128:(h + 1) * 128].rearrange("(kt p) e -> p kt e", p=128),
                       writes=[v])

    def make_post(h, blk):
        t0 = blk * 512
        i2 = blk % 2
        a1, a2, sq_, rs_, y = o1[i2], o2[i2], sq[i2], rs[i2], yo[i2]
        s0, s1 = ss_[i2]

        def part1():
            for c, sc_, a_ in ((0, s0, a1), (1, s1, a2)):
                ph.op("act", lambda e, c=c, sc_=sc_: e.activation(out=sc_.t[:], in_=accS[c].t[:], func=AF.Copy),
                      reads=[accS[c]], writes=[sc_])
                ph.op("dve", lambda e, c=c, a_=a_: e.tensor_copy(out=a_.t[:], in_=accO[c].t[:]), reads=[accO[c]], writes=[a_])
            for sc_, a_ in ((s0, a1), (s1, a2)):
                ph.op("dve", lambda e, sc_=sc_: e.reciprocal(out=sc_.t[:], in_=sc_.t[:]), reads=[sc_], writes=[sc_])
                ph.op("dve", lambda e, sc_=sc_, a_=a_: e.tensor_tensor(out=a_.t[:], in0=a_.t[:], in1=sc_.t[:], op=ALU.mult),
                      reads=[a_, sc_], writes=[a_])
            ph.op("dve", lambda e: e.scalar_tensor_tensor(out=a1.t[:], in0=a2.t[:], scalar=lc.t[:, 5:6], in1=a1.t[:],
                                                          op0=ALU.mult, op1=ALU.add), reads=[a1, a2, lc], writes=[a1])

        def part2(npz):
            ph.op("act", lambda e: e.activation(out=sq_.t[:], in_=a1.t[:], func=AF.Square), reads=[a1], writes=[sq_])
            ph.op("pe", lambda e: e.matmul(npz.t[:], ones.t[:], sq_.t[:], start=True, stop=True),
                  reads=[ones, sq_], writes=[npz])
            ph.op("act", lambda e: e.activation(out=rs_.t[:], in_=npz.t[:], func=AF.Sqrt, scale=1.0 / 128,
                                                bias=epsb.t[:, 0:1]), reads=[npz, epsb], writes=[rs_])
            ph.op("dve", lambda e: e.reciprocal(out=rs_.t[:], in_=rs_.t[:]), reads=[rs_], writes=[rs_])
            ph.op("dve", lambda e: e.scalar_tensor_tensor(out=y.t[:], in0=a1.t[:], scalar=lc.t[:, 6:7], in1=rs_.t[:],
                                                          op0=ALU.mult, op1=ALU.mult), reads=[a1, rs_, lc], writes=[y])
            ph.dma("sp", y, ydaT_d[h * 128:(h + 1) * 128, t0:t0 + 512], y.t[:], reads=[y])

        return part1, part2

    DEFER_KT = 8
    pending = None
    load_kv(0)
    for h in range(4):
        k = kT[h % 2]
        v = vt[h % 2]
        for blk in range(NTOK // 512):
            t0 = blk * 512
            q = rot(qb, cq)
            cq += 1
            ph.dma("sp", q, q.t[:], qT_d[h * 128:(h + 1) * 128, t0:t0 + 512], writes=[q])
            if blk == 1 and h + 1 < 4:
                load_kv(h + 1)

            def score(kt, q=q, k=k):
                for c in range(2):
                    s_ = scp[c][kt % 2]
                    ph.op("pe", lambda e, s_=s_, c=c, kt=kt, q=q, k=k: e.matmul(
                        s_.t[:], k.t[c * 64:(c + 1) * 64, kt * 128:(kt + 1) * 128], q.t[c * 64:(c + 1) * 64, :],
                        start=True, stop=True), reads=[k, q], writes=[s_])

            score(0)
            pendS = []
            peven = [None, None]
            for kt in range(NKT):
                if kt == DEFER_KT and pending is not None:
                    pending(scp[0][(kt + 1) % 2])
                    pending = None
                if kt + 1 < NKT:
                    score(kt + 1)
                for f in pendS:
                    f()
                pendS = []
                for c in range(2):
                    s_ = scp[c][kt % 2]
                    p = rot(pe_, cp_)
                    cp_ += 1
                    ph.op("act", lambda e, s_=s_, p=p: e.activation(out=p.t[:], in_=s_.t[:], func=AF.Exp, scale=0.125),
                          reads=[s_], writes=[p])
                    ph.op("pe", lambda e, c=c, kt=kt, p=p, v=v: e.matmul(accO[c].t[:], v.t[:, kt, :], p.t[:],
                                                                         start=(kt == 0), stop=(kt == NKT - 1)),
                          reads=[v, p], writes=[accO[c]])
                    if kt % 2 == 0:
                        peven[c] = p
                    else:
                        pr = rot(prs[c], kt // 2)
                        pe0 = peven[c]
                        ph.op("dve", lambda e, pr=pr, pe0=pe0, p=p: e.tensor_tensor(out=pr.t[:], in0=pe0.t[:], in1=p.t[:], op=ALU.add),
                              reads=[pe0, p], writes=[pr])

                        def smm(c=c, kt=kt, pr=pr):
                            ph.op("pe", lambda e: e.matmul(accS[c].t[:], ones.t[:], pr.t[:], start=(kt == 1), stop=(kt == NKT - 1)),
                                  reads=[ones, pr], writes=[accS[c]])
                        pendS.append(smm)
            for f in pendS:
                f()
            p1, pending = make_post(h, blk)
            p1()
    pending(scp[0][0])
    ph.flush()


def merge_phase(nc, name, x_src, x_dst, ident_d, h2T_d, upT_d, edgefull_d, hmask_d, rcnt_d, qcaT_d, ydaT_d, mem_d,
                pool_w, pool_scale, mem_g, w_mem_kv, w_gate, b_gate, w_bp, w_bd, w_bc, w_out, g_post, dbg=False):
    ph = Phase(nc, name)

    def dump(nm, tl, shape, dt):
        if dbg:
            o = nc.dram_tensor("dscr_" + nm, shape, dt).ap()
            ph.dma("sp", tl, o, tl.t[:], reads=[tl])
            DBG_ITEMS.append((nm, o, shape, dt))
    cst = load_consts(ph, ident_d)
    wg = ph.sbuf([128, 8, 3072], BF16, "wg")
    wbp = ph.sbuf([128, 2, D], BF16, "wbp")
    wbd = ph.sbuf([128, 4, D], BF16, "wbd")
    wbc = ph.sbuf([128, 2, D], BF16, "wbc")
    wo = ph.sbuf([128, 8, D], BF16, "wo")
    wkv = ph.sbuf([128, 8, 512], BF16, "wkv")
    wblk = ph.sbuf([128, 2, 128], BF16, "wblk")
    bg = ph.sbuf([128, 24], F32, "bg")
    psc = ph.sbuf([128, 2], F32, "psc")
    hm = ph.sbuf([128, 2], F32, "hm")
    gpost = ph.sbuf([128, D], F32, "gpost")
    gmem = ph.sbuf([128, D], F32, "gmem")
    memT = ph.sbuf([128, 8, NMEM], BF16, "memT")
    kmT = ph.sbuf([128, 2, NMEM], BF16, "kmT")
    vpad = [ph.sbuf([128, 2, 2, 128], BF16, "vpad") for _ in range(2)]
    onesel = [ph.sbuf([128, 128], BF16, "onesel") for _ in range(2)]
    hT = [ph.sbuf([128, 8, 512], BF16, "hT") for _ in range(2)]
    U = [ph.sbuf([128, 2, 528], F32, "U") for _ in range(1)]
    A = [ph.sbuf([128, 2, 528], F32, "A") for _ in range(1)]
    Bt = [ph.sbuf([128, 2, 528], F32, "B") for _ in range(1)]
    rcn = [ph.sbuf([128, 2, 512], F32, "rcn") for _ in range(1)]
    pl = [ph.sbuf([128, 2, 512], BF16, "pl") for _ in range(1)]
    ypl = [ph.sbuf([128, 2, 512], BF16, "ypl") for _ in range(2)]
    qca = [ph.sbuf([128, 2, 512], BF16, "qca") for _ in range(2)]
    yca = [ph.sbuf([128, 2, 512], BF16, "yca") for _ in range(2)]
    yda = [ph.sbuf([128, 4, 512], BF16, "yda") for _ in range(2)]
    pex = [ph.sbuf([128, 512], BF16, "pex") for _ in range(2)]
    rcp = [ph.sbuf([128, 512], F32, "rcp") for _ in range(1)]
    tg = [ph.sbuf([128, 512], F32, "tg") for _ in range(3)]
    um = [ph.sbuf([128, 512], F32, "um") for _ in range(3)]
    mT = [ph.sbuf([128, 8, 512], BF16, "mT") for _ in range(1)]
    xt = [ph.sbuf([128, D], F32, "xt") for _ in range(2)]
    yb = [ph.sbuf([128, D], F32, "yb") for _ in range(2)]
    hb = [ph.sbuf([128, D], BF16, "hb") for _ in range(2)]
    junk = [ph.sbuf([128, D], BF16, "junk") for _ in range(1)]
    st = [ph.sbuf([128, 4], F32, "st") for _ in range(4)]
    po = ph.psum([128, D], F32, "po")
    pp = [ph.psum([128, 512], F32, "pp") for _ in range(6)]
    c = {"pp": 0, "x": 0, "st": 0, "pex": 0, "tg": 0, "um": 0}

    def pbank():
        p = rot(pp, c["pp"])
        c["pp"] += 1
        return p

    def wload(tl, src, n):
        v = src.rearrange("(kc p) n -> p kc n", p=128)
        for kc in range(n):
            ph.dma("pool", tl, tl.t[:, kc, :], v[:, kc, :], writes=[tl])

    wload(wg, w_gate, 8)
    wload(wbp, w_bp, 2)
    wload(wbd, w_bd, 4)
    wload(wbc, w_bc, 2)
    wload(wo, w_out, 8)
    wload(wkv, w_mem_kv, 8)
    for tl in (wbp, wbd, wbc):
        ph.op("pool", lambda e, tl=tl: e.tensor_scalar(out=tl.t[:], in0=tl.t[:], scalar1=0.5, scalar2=None, op0=ALU.mult),
              reads=[tl], writes=[tl])
    ph.dma("sp", bg, bg.t[:], b_gate.rearrange("(c p) -> p c", p=128), writes=[bg], allow_slow_non_contiguous=True)
    ph.op("dve", lambda e: e.tensor_scalar(out=bg.t[:], in0=bg.t[:], scalar1=0.5, scalar2=None, op0=ALU.mult),
          reads=[bg], writes=[bg])
    ph.dma("sp", psc, psc.t[:], pool_scale.rearrange("(c p) -> p c", p=128), writes=[psc], allow_slow_non_contiguous=True)
    ph.dma("sp", hm, hm.t[:], hmask_d, writes=[hm])
    ph.dma("sp", gpost, gpost.t[:], bcast_row(g_post), writes=[gpost])
    ph.dma("sp", gmem, gmem.t[:], bcast_row(mem_g), writes=[gmem])
    ph.op("pool", lambda e: e.memset(wblk.t[:], 0.0), writes=[wblk])
    for g in range(4):
        lo = (g % 2) * 64
        ph.dma("pool", wblk, wblk.t[lo:lo + 64, g // 2, lo:lo + 64], pool_w[g], writes=[wblk])
    for hh in range(2):
        ph.op("pool", lambda e, hh=hh: e.memset(onesel[hh].t[:], 0.0), writes=[onesel[hh]])
        ph.op("pool", lambda e, hh=hh: e.memset(onesel[hh].t[:, hh * 64:(hh + 1) * 64], 1.0), writes=[onesel[hh]])
        ph.op("pool", lambda e, hh=hh: e.memset(vpad[hh].t[:], 0.0), writes=[vpad[hh]])

    def norm_tile(x, g_t, out_t, src_t=None):
        s = rot(st, c["st"]); c["st"] += 1
        jk = rot(junk, c["st"])
        srcT = src_t if src_t is not None else x
        if src_t is not None:
            ph.op("act", lambda e: e.activation(out=out_t.t[:], in_=src_t.t[:], func=AF.Copy), reads=[src_t], writes=[out_t])
            srcT = out_t
        ph.op("act", lambda e: e.activation(out=jk.t[:], in_=srcT.t[:], func=AF.Square, accum_out=s.t[:, 0:1]),
              reads=[srcT], writes=[jk, s])
        ph.op("dve", lambda e: e.tensor_scalar(out=s.t[:, 1:2], in0=s.t[:, 0:1], scalar1=1.0 / D, scalar2=EPS,
                                               op0=ALU.mult, op1=ALU.add), reads=[s], writes=[s])
        ph.op("pool", lambda e: e.tensor_tensor(out=s.t[:, 2:3], in0=s.t[:, 1:2], in1=cst["mhalf"].t[:, 0:1], op=ALU.pow),
              reads=[s, cst["mhalf"]], writes=[s])
        ph.op("dve", lambda e: e.scalar_tensor_tensor(out=out_t.t[:], in0=srcT.t[:], scalar=s.t[:, 2:3], in1=g_t.t[:],
                                                      op0=ALU.mult, op1=ALU.mult), reads=[srcT, s, g_t], writes=[out_t])

    for m in range(2):
        x = rot(xt, c["x"]); c["x"] += 1
        h = hb[m]
        ph.dma("sp", x, x.t[:], mem_d[m * 128:(m + 1) * 128, :], writes=[x])
        norm_tile(x, gmem, h)
        for kc in range(8):
            pt = pbank()
            ph.op("pe", lambda e, pt=pt, h=h, kc=kc: e.transpose(out=pt.t[:].bitcast(BF16)[:, 0:128],
                                                                 in_=h.t[:, kc * 128:(kc + 1) * 128],
                                                                 identity=cst["ident"].t[:]),
                  reads=[h, cst["ident"]], writes=[pt])
            ph.op("act", lambda e, pt=pt, kc=kc, m=m: e.activation(out=memT.t[:, kc, m * 128:(m + 1) * 128],
                                                                    in_=pt.t[:].bitcast(BF16)[:, 0:128], func=AF.Copy),
                  reads=[pt], writes=[memT])
    for cc in range(2):
        p = pbank()
        for kc in range(8):
            ph.op("pe", lambda e, p=p, kc=kc, cc=cc: e.matmul(p.t[:, 0:NMEM], wkv.t[:, kc, cc * 128:(cc + 1) * 128],
                                                              memT.t[:, kc, :], start=(kc == 0), stop=(kc == 7)),
                  reads=[wkv, memT], writes=[p], signal=(kc == 7))
        ph.op("act", lambda e, p=p, cc=cc: e.activation(out=kmT.t[:, cc, :], in_=p.t[:, 0:NMEM], func=AF.Copy),
              reads=[p], writes=[kmT])
    for m in range(2):
        p = pbank()
        for kc in range(8):
            ph.op("pe", lambda e, p=p, kc=kc, m=m: e.matmul(p.t[:, 0:256], memT.t[:, kc, m * 128:(m + 1) * 128],
                                                            wkv.t[:, kc, 256:512], start=(kc == 0), stop=(kc == 7)),
                  reads=[wkv, memT], writes=[p], signal=(kc == 7))
        for cp in range(2):
            for hh in range(2):
                hd = 2 * cp + hh
                ph.op("act", lambda e, p=p, m=m, cp=cp, hh=hh, hd=hd: e.activation(
                    out=vpad[hh].t[:, m, cp, hh * 64:(hh + 1) * 64], in_=p.t[:, hd * 64:(hd + 1) * 64], func=AF.Copy),
                    reads=[p], writes=[vpad[hh]])

    NBLK = NTOK // 512

    def tiles_for(blk):
        i2 = blk % 2
        return hT[i2], ypl[i2], qca[i2], yca[i2], yda[i2]

    def front_l(blk):
        t0 = blk * 512
        hs, ypl_, qc, yc, yd = tiles_for(blk)
        ph.dma("sp", hs, hs.t[:], h2T_d[:, :, t0:t0 + 512], writes=[hs])
        for cc in range(2):
            ph.dma("sp", qc, qc.t[:, cc, :], qcaT_d[cc * 128:(cc + 1) * 128, t0:t0 + 512], writes=[qc])
        for hh in range(4):
            ph.dma("sp", yd, yd.t[:, hh, :], ydaT_d[hh * 128:(hh + 1) * 128, t0:t0 + 512], writes=[yd])

    def front_a(blk):
        t0 = blk * 512
        hs, ypl_, qc, yc, yd = tiles_for(blk)
        u, a, b, rc_, pl_ = U[0], A[0], Bt[0], rcn[0], pl[0]
        for cc in range(2):
            lo = max(t0 - 8, 0)
            hi = min(t0 + 520, NTOK)
            ph.dma("sp", u, u.t[:, cc, 8 - (t0 - lo):8 + (hi - t0)], upT_d[cc, :, lo:hi], writes=[u])
            if blk == 0:
                ph.dma("sp", u, u.t[:, cc, 0:8], edgefull_d[cc * 128:(cc + 1) * 128, 8:16], writes=[u])
            if blk == NBLK - 1:
                ph.dma("sp", u, u.t[:, cc, 520:528], edgefull_d[256 + cc * 128:256 + (cc + 1) * 128, 0:8], writes=[u])
            ph.dma("sp", rc_, rc_.t[:, cc, :], rcnt_d[cc, :, t0:t0 + 512], writes=[rc_])
        if blk == 0:
            ph.op("dve", lambda e, u=u: e.tensor_scalar(out=u.t[:, :, 0:8], in0=u.t[:, :, 0:8], scalar1=hm.t[:, 0:1],
                                                        scalar2=None, op0=ALU.mult), reads=[u, hm], writes=[u])
        if blk == NBLK - 1:
            ph.op("dve", lambda e, u=u: e.tensor_scalar(out=u.t[:, :, 520:528], in0=u.t[:, :, 520:528], scalar1=hm.t[:, 1:2],
                                                        scalar2=None, op0=ALU.mult), reads=[u, hm], writes=[u])
        ph.op("pool", lambda e, u=u, a=a: e.tensor_tensor(out=a.t[:, :, 0:527], in0=u.t[:, :, 0:527], in1=u.t[:, :, 1:528],
                                                          op=ALU.add), reads=[u], writes=[a])
        ph.op("pool", lambda e, a=a, b=b: e.tensor_tensor(out=b.t[:, :, 0:525], in0=a.t[:, :, 0:525], in1=a.t[:, :, 2:527],
                                                          op=ALU.add), reads=[a], writes=[b])
        ph.op("dve", lambda e, a=a, rc_=rc_: e.tensor_tensor(out=rc_.t[0:64, 0, :], in0=a.t[0:64, 0, 7:519],
                                                             in1=rc_.t[0:64, 0, :], op=ALU.mult), reads=[a, rc_], writes=[rc_])
        ph.op("dve", lambda e, b=b, rc_=rc_: e.tensor_tensor(out=rc_.t[64:128, 0, :], in0=b.t[64:128, 0, 6:518],
                                                             in1=rc_.t[64:128, 0, :], op=ALU.mult), reads=[b, rc_], writes=[rc_])
        ph.op("pool", lambda e, a=a, b=b: e.tensor_tensor(out=a.t[:, 1, 0:521], in0=b.t[:, 1, 0:521], in1=b.t[:, 1, 4:525],
                                                          op=ALU.add), reads=[b], writes=[a])
        ph.op("pool", lambda e, a=a, b=b: e.tensor_tensor(out=b.t[64:128, 1, 0:513], in0=a.t[64:128, 1, 0:513],
                                                          in1=a.t[64:128, 1, 8:521], op=ALU.add), reads=[a], writes=[b])
        ph.op("dve", lambda e, a=a, rc_=rc_: e.tensor_tensor(out=rc_.t[0:64, 1, :], in0=a.t[0:64, 1, 4:516],
                                                             in1=rc_.t[0:64, 1, :], op=ALU.mult), reads=[a, rc_], writes=[rc_])
        ph.op("dve", lambda e, b=b, rc_=rc_: e.tensor_tensor(out=rc_.t[64:128, 1, :], in0=b.t[64:128, 1, 0:512],
                                                             in1=rc_.t[64:128, 1, :], op=ALU.mult), reads=[b, rc_], writes=[rc_])
        ph.op("dve", lambda e, u=u, rc_=rc_, pl_=pl_: e.tensor_tensor(out=pl_.t[:], in0=rc_.t[:], in1=u.t[:, :, 8:520],
                                                                      op=ALU.subtract), reads=[u, rc_], writes=[pl_])

    def front_b(blk):
        t0 = blk * 512
        hs, ypl_, qc, yc, yd = tiles_for(blk)
        pl_ = pl[0]
        for cc in range(2):
            p = pbank()
            ph.op("pe", lambda e, p=p, cc=cc, pl_=pl_: e.matmul(p.t[:], wblk.t[:, cc, :], pl_.t[:, cc, :], start=True, stop=True),
                  reads=[wblk, pl_], writes=[p])
            ph.op("act", lambda e, p=p, cc=cc, ypl_=ypl_: e.activation(out=ypl_.t[:, cc, :], in_=p.t[:], func=AF.Identity,
                                                                        scale=psc.t[:, cc:cc + 1]), reads=[p, psc], writes=[ypl_])
        for cp in range(2):
            pO = pbank()
            pS = pbank()
            n = 0
            for hh in range(2):
                for m in range(2):
                    ps_ = pbank()
                    ph.op("pe", lambda e, ps_=ps_, hh=hh, m=m, cp=cp, qc=qc: e.matmul(
                        ps_.t[:], kmT.t[hh * 64:(hh + 1) * 64, cp, m * 128:(m + 1) * 128], qc.t[hh * 64:(hh + 1) * 64, cp, :],
                        start=True, stop=True), reads=[kmT, qc], writes=[ps_])
                    px = rot(pex, c["pex"]); c["pex"] += 1
                    ph.op("act", lambda e, ps_=ps_, px=px: e.activation(out=px.t[:], in_=ps_.t[:], func=AF.Exp, scale=0.125),
                          reads=[ps_], writes=[px])
                    ph.op("pe", lambda e, pO=pO, hh=hh, m=m, cp=cp, px=px, n=n: e.matmul(
                        pO.t[:], vpad[hh].t[:, m, cp, :], px.t[:], start=(n == 0), stop=(n == 3)),
                        reads=[vpad[hh], px], writes=[pO], signal=False)
                    ph.op("pe", lambda e, pS=pS, hh=hh, px=px, n=n: e.matmul(
                        pS.t[:], onesel[hh].t[:], px.t[:], start=(n == 0), stop=(n == 3)),
                        reads=[onesel[hh], px], writes=[pS])
                    n += 1
            r_ = rcp[0]
            ph.op("dve", lambda e, r_=r_, pS=pS: e.reciprocal(out=r_.t[:], in_=pS.t[:]), reads=[pS], writes=[r_])
            ph.op("dve", lambda e, r_=r_, pO=pO, cp=cp, yc=yc: e.tensor_tensor(out=yc.t[:, cp, :], in0=pO.t[:], in1=r_.t[:],
                                                                                op=ALU.mult), reads=[pO, r_], writes=[yc])

    def back(blk):
        t0 = blk * 512
        hs, ypl_, qc, yc, yd = tiles_for(blk)
        mt = mT[0]
        for oc in range(8):
            if oc == 4:
                if blk + 1 < NBLK:
                    front_b(blk + 1)
                if blk + 2 < NBLK:
                    front_a(blk + 2)
            brs = []
            for (wt, src, nk) in ((wbp, ypl_, 2), (wbd, yd, 4), (wbc, yc, 2)):
                p = pbank()
                for kc in range(nk):
                    ph.op("pe", lambda e, p=p, wt=wt, src=src, kc=kc, oc=oc, nk=nk: e.matmul(
                        p.t[:], wt.t[:, kc, oc * 128:(oc + 1) * 128], src.t[:, kc, :], start=(kc == 0), stop=(kc == nk - 1)),
                        reads=[wt, src], writes=[p], signal=(kc == nk - 1))
                brs.append(p)
            us = []
            for x_ in range(3):
                p = pbank()
                col = x_ * 1024 + oc * 128
                for kc in range(8):
                    ph.op("pe", lambda e, p=p, kc=kc, col=col, hs=hs: e.matmul(p.t[:], wg.t[:, kc, col:col + 128], hs.t[:, kc, :],
                                                                                start=(kc == 0), stop=(kc == 7)),
                          reads=[wg, hs], writes=[p], signal=(kc == 7))
                t_ = rot(tg, c["tg"]); c["tg"] += 1
                u_ = rot(um, c["um"]); c["um"] += 1
                bi = x_ * 8 + oc
                ph.op("act", lambda e, p=p, t_=t_, bi=bi: e.activation(out=t_.t[:], in_=p.t[:], func=AF.Tanh,
                                                                        bias=bg.t[:, bi:bi + 1], scale=0.5), reads=[p, bg], writes=[t_])
                br = brs[x_]
                ph.op("dve", lambda e, t_=t_, u_=u_, br=br: e.scalar_tensor_tensor(out=u_.t[:], in0=t_.t[:], scalar=1.0, in1=br.t[:],
                                                                                   op0=ALU.add, op1=ALU.mult), reads=[t_, br], writes=[u_])
                us.append(u_)
            ph.op("pool", lambda e, us=us: e.tensor_tensor(out=us[0].t[:], in0=us[0].t[:], in1=us[1].t[:], op=ALU.add),
                  reads=[us[0], us[1]], writes=[us[0]])
            ph.op("pool", lambda e, us=us, mt=mt, oc=oc: e.tensor_tensor(out=mt.t[:, oc, :], in0=us[0].t[:], in1=us[2].t[:], op=ALU.add),
                  reads=[us[0], us[2]], writes=[mt])
        if blk + 2 < NBLK:
            front_l(blk + 2)
        for t in range(4):
            r0 = t0 + t * 128
            x = rot(xt, c["x"]); c["x"] += 1
            y = rot(yb, t)
            ph.dma("sp", x, x.t[:], x_src[r0:r0 + 128, :], writes=[x])
            for hf in range(2):
                for kc in range(8):
                    ph.op("pe", lambda e, kc=kc, hf=hf, t=t, mt=mt: e.matmul(
                        po.t[:, hf * 512:(hf + 1) * 512], mt.t[:, kc, t * 128:(t + 1) * 128], wo.t[:, kc, hf * 512:(hf + 1) * 512],
                        start=(kc == 0), stop=(kc == 7)), reads=[mt, wo], writes=[po], signal=(kc == 7))
            norm_tile(x, gpost, y, src_t=po)
            ph.op("pool", lambda e, x=x, y=y: e.tensor_tensor(out=x.t[:], in0=x.t[:], in1=y.t[:], op=ALU.add),
                  reads=[x, y], writes=[x])
            ph.dma("sp", x, x_dst[r0:r0 + 128, :], x.t[:], reads=[x])

    front_l(0)
    front_a(0)
    front_b(0)
    front_l(1)
    front_a(1)
    for blk in range(NBLK):
        back(blk)
    ph.flush()


W_NAMES = ["ffn1_pre_g", "ffn1_w_up", "ffn1_w_down", "ffn1_post_g", "mix_pre_g", "w_in", "pool_w", "pool_scale",
           "da_lambda_q1", "da_lambda_k1", "da_lambda_q2", "da_lambda_k2", "da_subln_g", "mem_norm_g", "w_mem_kv",
           "w_gate", "b_gate", "w_br_pool", "w_br_da", "w_br_ca", "w_out", "mix_post_g", "ffn2_pre_g", "ffn2_w_up",
           "ffn2_w_down", "ffn2_post_g"]
W_SHAPES = {"ffn1_pre_g": [D], "ffn1_w_up": [D, 2 * DFF], "ffn1_w_down": [DFF, D], "ffn1_post_g": [D], "mix_pre_g": [D],
            "w_in": [D, 2048], "pool_w": [4, 64, 64], "pool_scale": [256], "da_lambda_q1": [64], "da_lambda_k1": [64],
            "da_lambda_q2": [64], "da_lambda_k2": [64], "da_subln_g": [128], "mem_norm_g": [D], "w_mem_kv": [D, 512],
            "w_gate": [D, 3072], "b_gate": [3072], "w_br_pool": [256, D], "w_br_da": [512, D], "w_br_ca": [256, D],
            "w_out": [D, D], "mix_post_g": [D], "ffn2_pre_g": [D], "ffn2_w_up": [D, 2 * DFF], "ffn2_w_down": [DFF, D],
            "ffn2_post_g": [D]}


DBG_ITEMS = []


def debug_dump(nc, items):
    sem = nc.alloc_semaphore(name="dbg_sem")
    with nc.Block() as block:
        @block.sync
        def _(e):
            n = 0
            for name, ap, shape, dt in items:
                out = nc.dram_tensor("dbg_" + name, shape, dt, kind="ExternalOutput").ap()
                rows = shape[0]
                step = max(1, rows // 8)
                for r in range(0, rows, step):
                    e.dma_start(out=out[r:r + step], in_=ap[r:r + step]).then_inc(sem, 16)
                    n += 1
            e.wait_ge(sem, 16 * n)
    nc.clear_and_free_semaphores([sem])
    nc.all_engine_barrier()


def build(depth=DEPTH, ffn_T=1024, debug=False):
    import math
    nc = bass.Bass("TRN2", target_bir_lowering=False)
    x_in = nc.dram_tensor("x", [NTOK, D], F32, kind="ExternalInput").ap()
    mem_d = nc.dram_tensor("mem", [NMEM, D], F32, kind="ExternalInput").ap()
    pos_d = nc.dram_tensor("positions", [NTOK], I32, kind="ExternalInput").ap()
    ident_d = nc.dram_tensor("ident", [128, 128], F32, kind="ExternalInput").ap()
    ropec_d = nc.dram_tensor("ropec", [128, 2], F32, kind="ExternalInput").ap()
    hmask_d = nc.dram_tensor("hmask", [128, 2], F32, kind="ExternalInput").ap()
    rcnt_d = nc.dram_tensor("rcnt", [2, 128, NTOK], F32, kind="ExternalInput").ap()
    W = {}
    for n in W_NAMES:
        W[n] = nc.dram_tensor(n, [depth] + W_SHAPES[n], F32, kind="ExternalInput").ap()
    y_out = nc.dram_tensor("y", [NTOK, D], F32, kind="ExternalOutput").ap()

    def scr(name, shape, dt):
        return nc.dram_tensor(name, shape, dt).ap()

    xs = [scr(f"xs{i}", [NTOK, D], F32) for i in range(3)]
    cos_d = scr("cos_t", [128, NTOK], F32)
    sin_d = scr("sin_t", [128, NTOK], F32)
    h2T_d = scr("h2T", [128, 8, NTOK], BF16)
    qT_d = scr("qT", [512, NTOK], BF16)
    kown_d = scr("kown", [512, NTOK], BF16)
    vown_d = scr("vown", [NTOK, 512], BF16)
    upT_d = scr("upT", [2, 128, NTOK], F32)
    qcaT_d = scr("qcaT", [256, NTOK], BF16)
    edge_d = scr("edge", [256, 16], F32)
    edgefull_d = scr("edgefull", [512, 16], F32)
    kfull_d = [scr(f"kfull{h}", [256, NTOK], BF16) for h in range(4)]
    vfull_d = [scr(f"vfull{p}", [2048, 512], BF16) for p in range(4)]
    ydaT_d = scr("ydaT", [512, NTOK], BF16)

    rope_phase(nc, "R", pos_d, ropec_d, cos_d, sin_d)
    cur = x_in
    for l in range(depth):
        lam_init = 0.8 - 0.6 * math.exp(-0.3 * l)
        ffn_phase(nc, f"A{l}", cur, xs[0], W["ffn1_w_up"][l], W["ffn1_w_down"][l], W["ffn1_pre_g"][l], W["ffn1_post_g"][l],
                  ident_d, T=ffn_T)
        proj_phase(nc, f"B{l}", xs[0], W["mix_pre_g"][l], W["w_in"][l], ident_d, cos_d, sin_d, h2T_d, qT_d, kown_d, vown_d,
                   upT_d, qcaT_d, edge_d)
        items = [(kown_d[h * 128:(h + 1) * 128, :], kfull_d[h]) for h in range(4)]
        items += [(vown_d[p * 1024:(p + 1) * 1024, :], vfull_d[p]) for p in range(4)]
        items += [(edge_d, edgefull_d)]
        exchange_phase(nc, f"X{l}", items)
        attn_phase(nc, f"C{l}", qT_d, kfull_d, vfull_d, W["da_lambda_q1"][l], W["da_lambda_k1"][l], W["da_lambda_q2"][l],
                   W["da_lambda_k2"][l], W["da_subln_g"][l], lam_init, ydaT_d)
        merge_phase(nc, f"M{l}", xs[0], xs[1], ident_d, h2T_d, upT_d, edgefull_d, hmask_d, rcnt_d, qcaT_d, ydaT_d, mem_d,
                    W["pool_w"][l], W["pool_scale"][l], W["mem_norm_g"][l], W["w_mem_kv"][l], W["w_gate"][l], W["b_gate"][l],
                    W["w_br_pool"][l], W["w_br_da"][l], W["w_br_ca"][l], W["w_out"][l], W["mix_post_g"][l], dbg=False)
        dst = y_out if l == depth - 1 else xs[2]
        ffn_phase(nc, f"D{l}", xs[1], dst, W["ffn2_w_up"][l], W["ffn2_w_down"][l], W["ffn2_pre_g"][l], W["ffn2_post_g"][l],
                  ident_d, T=ffn_T)
        cur = xs[2]
    if debug:
        debug_dump(nc, [("x1", xs[0], [NTOK, D], F32), ("x2", xs[1], [NTOK, D], F32), ("cos", cos_d, [128, NTOK], F32),
                        ("sin", sin_d, [128, NTOK], F32), ("h2T", h2T_d, [128, 8, NTOK], BF16), ("qT", qT_d, [512, NTOK], BF16),
                        ("kfull0", kfull_d[0], [256, NTOK], BF16), ("vfull0", vfull_d[0], [2048, 512], BF16),
                        ("upT", upT_d, [2, 128, NTOK], F32), ("qcaT", qcaT_d, [256, NTOK], BF16),
                        ("edgefull", edgefull_d, [512, 16], F32), ("ydaT", ydaT_d, [512, NTOK], BF16)] + DBG_ITEMS)
    return nc


def host_consts():
    ident = np.eye(128, dtype=np.float32)
    inv = (np.float32(500000.0) ** (-np.arange(0, 16, 2, dtype=np.float32) / np.float32(16))).astype(np.float32)
    ropec = np.zeros((128, 2), np.float32)
    for p in range(128):
        d = p % 64
        if d < 16:
            ropec[p, 0] = inv[d % 8]
            ropec[p, 1] = -1.0 if d < 8 else 1.0
    rc = []
    pos = np.arange(SEQ)
    for w in (2, 4, 8, 16):
        lo = np.clip(pos - w // 2, 0, SEQ)
        hi = np.clip(pos + w // 2, 0, SEQ)
        rc.append(np.repeat((1.0 / (hi - lo).astype(np.float32))[None, :], 64, axis=0))
    rcnt = np.concatenate(rc, axis=0).astype(np.float32).reshape(2, 128, SEQ)
    return ident, ropec, rcnt


def make_in_maps(inputs, depth=DEPTH, ncores=NCORES):
    ident, ropec, rcnt = host_consts()
    x = np.asarray(inputs["x"])
    mem = np.asarray(inputs["mem"])
    pos = np.asarray(inputs["positions"])
    maps = []
    for core in range(ncores):
        b, j = core // 2, core % 2
        m = {"x": np.ascontiguousarray(x[b, j * NTOK:(j + 1) * NTOK]),
             "mem": np.ascontiguousarray(mem[b]),
             "positions": np.ascontiguousarray(pos[b, j * NTOK:(j + 1) * NTOK]).astype(np.int32),
             "ident": ident, "ropec": ropec,
             "hmask": np.tile(np.array([[1.0 if j == 1 else 0.0, 1.0 if j == 0 else 0.0]], np.float32), (128, 1)),
             "rcnt": np.ascontiguousarray(rcnt[:, :, j * NTOK:(j + 1) * NTOK])}
        for n in W_NAMES:
            m[n] = np.ascontiguousarray(np.asarray(inputs[n])[:depth])
        maps.append(m)
    return maps


_NC_CACHE = {}


def kernel(**inputs):
    if "nc" not in _NC_CACHE:
        _NC_CACHE["nc"] = build()
    nc = _NC_CACHE["nc"]
    maps = make_in_maps(inputs)
    res = run_bass_kernel_spmd(nc, maps, core_ids=list(range(NCORES)))
    out = np.empty((4, SEQ, D), np.float32)
    for core in range(NCORES):
        b, j = core // 2, core % 2
        out[b, j * NTOK:(j + 1) * NTOK] = res.results[core]["y"]
    return out
```

```python
import numpy as np
from contextlib import ExitStack
import concourse.bass as bass
import concourse.mybir as mybir
from concourse.bass_utils import run_bass_kernel_spmd

F32 = mybir.dt.float32
BF16 = mybir.dt.bfloat16
I32 = mybir.dt.int32
AF = mybir.ActivationFunctionType
ALU = mybir.AluOpType

D = 1024
DFF = 2816
NTOK = 4096
SEQ = 8192
DEPTH = 4
NMEM = 256
EPS = 1e-6
NCORES = 8


class Buf:
    __slots__ = ("w", "r")

    def __init__(self):
        self.w = None
        self.r = {}


class Tl:
    def __init__(self, t):
        self.t = t
        self.b = Buf()
        self.g = None


class Grp:
    def __init__(self, sem):
        self.sem = sem
        self.n = 0


class Phase:
    ENG = ("pe", "act", "dve", "pool", "sp")

    def __init__(self, nc, name):
        self.nc = nc
        self.name = name
        self.stack = ExitStack()
        self.all_sems = []
        self.sem = {e: self._sem(f"{name}_{e}") for e in self.ENG}
        self.cnt = {e: 0 for e in self.ENG}
        self.thunks = {e: [] for e in self.ENG}
        self.seen = {e: {} for e in self.ENG}
        self.unsig = {e: False for e in self.ENG}
        self.grps = []
        self.nt = 0

    def _sem(self, name):
        h = self.nc.alloc_semaphore(name=name)
        self.all_sems.append(h)
        return h

    def sbuf(self, shape, dt, name=None):
        self.nt += 1
        return Tl(self.stack.enter_context(self.nc.sbuf_tensor(f"{self.name}_{name or 't'}{self.nt}", list(shape), dt)))

    def psum(self, shape, dt, name=None):
        self.nt += 1
        return Tl(self.stack.enter_context(self.nc.psum_tensor(f"{self.name}_{name or 'p'}{self.nt}", list(shape), dt)))

    def grp(self):
        self.nt += 1
        g = Grp(self._sem(f"{self.name}_g{self.nt}"))
        self.grps.append(g)
        return g

    def _wait(self, eng, key, val):
        if key == eng and eng == "pe":
            return
        s = self.seen[eng]
        if s.get(key, 0) >= val:
            return
        s[key] = val
        sem = self.sem[key] if isinstance(key, str) else key.sem
        self.thunks[eng].append(lambda e, sem=sem, val=val: e.wait_ge(sem, val))

    def _deps(self, eng, reads, writes):
        for b in reads:
            if b.w is not None:
                self._wait(eng, *b.w)
        for b in writes:
            if b.w is not None:
                self._wait(eng, *b.w)
            for k, v in b.r.items():
                self._wait(eng, k, v)

    def _record(self, ev, reads, writes):
        k, v = ev
        for b in reads:
            if b.r.get(k, 0) < v:
                b.r[k] = v
        for b in writes:
            b.w = ev
            b.r = {}

    def op(self, eng, fn, reads=(), writes=(), signal=True):
        reads = [x.b if isinstance(x, Tl) else x for x in reads]
        writes = [x.b if isinstance(x, Tl) else x for x in writes]
        self._deps(eng, reads, writes)
        if signal:
            self.cnt[eng] += 1
            ev = (eng, self.cnt[eng])
            sem = self.sem[eng]
            self.thunks[eng].append(lambda e, fn=fn, sem=sem: fn(e).then_inc(sem, 1))
            self.unsig[eng] = False
        else:
            assert eng == "pe"
            ev = (eng, self.cnt[eng] + 1)
            self.thunks[eng].append(fn)
            self.unsig[eng] = True
        self._record(ev, reads, writes)

    def dma(self, eng, tl, out, in_, reads=(), writes=(), **kw):
        if tl.g is None:
            tl.g = self.grp()
        grp = tl.g
        reads = [x.b if isinstance(x, Tl) else x for x in reads]
        writes = [x.b if isinstance(x, Tl) else x for x in writes]
        self._deps(eng, reads, writes)
        grp.n += 1
        ev = (grp, grp.n * 16)
        sem = grp.sem
        self.thunks[eng].append(lambda e, out=out, in_=in_, sem=sem: e.dma_start(out=out, in_=in_, **kw).then_inc(sem, 16))
        self._record(ev, reads, writes)

    def flush(self):
        nc = self.nc
        for e in self.ENG:
            assert not self.unsig[e], (self.name, e)
        for g in self.grps:
            if g.n:
                self._wait("sp", g, g.n * 16)
        th = self.thunks
        with nc.Block() as block:
            @block.tensor
            def _(e):
                for f in th["pe"]:
                    f(e)

            @block.scalar
            def _(e):
                for f in th["act"]:
                    f(e)

            @block.vector
            def _(e):
                for f in th["dve"]:
                    f(e)

            @block.gpsimd
            def _(e):
                for f in th["pool"]:
                    f(e)

            @block.sync
            def _(e):
                for f in th["sp"]:
                    f(e)
        nc.clear_and_free_semaphores(self.all_sems)
        nc.all_engine_barrier()
        self.stack.close()


def rot(lst, i):
    return lst[i % len(lst)]


def emit_rstd(ph, ss, v, r, cst, scale=1.0):
    ph.op("dve", lambda e: e.tensor_scalar(out=v.t[:, 0:1], in0=ss.t[:, 0:1], scalar1=1.0 / D, scalar2=EPS,
                                           op0=ALU.mult, op1=ALU.add), reads=[ss], writes=[v])
    ph.op("pool", lambda e: e.tensor_tensor(out=r.t[:, 0:1], in0=v.t[:, 0:1], in1=cst["mhalf"].t[:, 0:1], op=ALU.pow),
          reads=[v, cst["mhalf"]], writes=[r])


def load_consts(ph, ident_d):
    cst = {}
    cst["ident"] = ph.sbuf([128, 128], BF16, "ident")
    cst["mhalf"] = ph.sbuf([128, 1], F32, "mhalf")
    ph.dma("pool", cst["ident"], cst["ident"].t[:], ident_d, writes=[cst["ident"]])
    ph.op("pool", lambda e: e.memset(cst["mhalf"].t[:], -0.5), writes=[cst["mhalf"]])
    return cst


def bcast_row(ap_row, n=128):
    return ap_row.partition_broadcast(n)


def ffn_phase(nc, name, x_src, x_dst, w_up, w_down, g_pre, g_post, ident_d, T=1024):
    ph = Phase(nc, name)
    cst = load_consts(ph, ident_d)
    NSB = NTOK // T
    TT = T // 128
    NB = T // 512
    NJ = DFF // 128
    JG = 2
    NG = NJ // JG
    w_up_v = w_up.rearrange("(kc p) n -> p kc n", p=128)
    w_dn_v = w_down.rearrange("(j p) n -> p j n", p=128)

    gpre = ph.sbuf([128, D], F32, "gpre")
    gpost = ph.sbuf([128, D], F32, "gpost")
    wd = ph.sbuf([128, NJ, D], BF16, "wd")
    hT = [ph.sbuf([128, 8, T], BF16, "hT") for _ in range(2)]
    gT = ph.sbuf([128, NJ, T], BF16, "gT")
    gTb = [Buf() for _ in range(NJ)]
    wu = [ph.sbuf([128, 8, 2 * JG * 128], BF16, "wu") for _ in range(3)]
    xt = [ph.sbuf([128, D], F32, "xt") for _ in range(3)]
    xr = [ph.sbuf([128, D], F32, "xr") for _ in range(2)]
    hb = [ph.sbuf([128, D], BF16, "hb") for _ in range(2)]
    junk = [ph.sbuf([128, D], BF16, "junk") for _ in range(2)]
    yb = [ph.sbuf([128, D], F32, "yb") for _ in range(2)]
    sa = [ph.sbuf([128, 512], F32, "sa") for _ in range(2)]
    st = [ph.sbuf([128, 4], F32, "st") for _ in range(4)]
    po = ph.psum([128, D], F32, "po")
    pT = [ph.psum([128, 8, 128], BF16, "pT") for _ in range(2)]
    pa = [ph.psum([128, 512], F32, "pa") for _ in range(2)]
    pb = [ph.psum([128, 512], F32, "pb") for _ in range(2)]

    ph.dma("sp", gpre, gpre.t[:], bcast_row(g_pre), writes=[gpre])
    ph.dma("sp", gpost, gpost.t[:], bcast_row(g_post), writes=[gpost])
    ph.op("dve", lambda e: e.tensor_scalar(out=gpost.t[:], in0=gpost.t[:], scalar1=0.5, scalar2=None, op0=ALU.mult),
          reads=[gpost], writes=[gpost])

    cnt = {"x": 0, "st": 0, "hb": 0, "pT": 0}

    def s1_tile(sb, t):
        r0 = sb * T + t * 128
        x = rot(xt, cnt["x"])
        cnt["x"] += 1
        s = rot(st, cnt["st"])
        cnt["st"] += 1
        jk = rot(junk, cnt["st"])
        h = rot(hb, cnt["hb"])
        cnt["hb"] += 1
        p = rot(pT, cnt["pT"])
        cnt["pT"] += 1
        hd = hT[sb % 2]

        def prep():
            ph.dma("sp", x, x.t[:], x_src[r0:r0 + 128, :], writes=[x])
            ph.op("act", lambda e: e.activation(out=jk.t[:], in_=x.t[:], func=AF.Square, accum_out=s.t[:, 0:1]),
                  reads=[x], writes=[jk, s])
            ph.op("dve", lambda e: e.tensor_scalar(out=s.t[:, 1:2], in0=s.t[:, 0:1], scalar1=1.0 / D, scalar2=EPS,
                                                   op0=ALU.mult, op1=ALU.add), reads=[s], writes=[s])
            ph.op("pool", lambda e: e.tensor_tensor(out=s.t[:, 2:3], in0=s.t[:, 1:2], in1=cst["mhalf"].t[:, 0:1],
                                                    op=ALU.pow), reads=[s, cst["mhalf"]], writes=[s])
            ph.op("dve", lambda e: e.scalar_tensor_tensor(out=h.t[:], in0=x.t[:], scalar=s.t[:, 2:3], in1=gpre.t[:],
                                                          op0=ALU.mult, op1=ALU.mult), reads=[x, s, gpre], writes=[h])

        def xpose():
            for kc in range(8):
                ph.op("pe", lambda e, kc=kc: e.transpose(out=p.t[:, kc, :], in_=h.t[:, kc * 128:(kc + 1) * 128],
                                                         identity=cst["ident"].t[:]),
                      reads=[h, cst["ident"]], writes=[p], signal=(kc == 7))
            ph.op("act", lambda e: e.activation(out=hd.t[:, :, t * 128:(t + 1) * 128], in_=p.t[:], func=AF.Copy),
                  reads=[p], writes=[hd])

        return prep, xpose

    def issue_w(gi):
        if gi >= NSB * NG:
            return
        w = rot(wu, gi)
        c0 = (gi % NG) * JG * 128
        W = JG * 128
        ph.dma("pool", w, w.t[:, :, 0:W], w_up_v[:, :, c0:c0 + W], writes=[w])
        ph.dma("pool", w, w.t[:, :, W:2 * W], w_up_v[:, :, DFF + c0:DFF + c0 + W], writes=[w])

    def s2_group(sb, jg, gi):
        w = rot(wu, gi)
        W = JG * 128
        hs = hT[sb % 2]
        k = 0
        for jj in range(JG):
            j = jg * JG + jj
            for nb in range(NB):
                a = rot(pa, gi * JG * NB + k)
                b = rot(pb, gi * JG * NB + k)
                s_ = rot(sa, gi * JG * NB + k)
                k += 1
                for kc in range(8):
                    ph.op("pe", lambda e, a=a, w=w, hs=hs, kc=kc, jj=jj, nb=nb: e.matmul(
                        a.t[:], w.t[:, kc, jj * 128:(jj + 1) * 128], hs.t[:, kc, nb * 512:(nb + 1) * 512],
                        start=(kc == 0), stop=(kc == 7)), reads=[w, hs], writes=[a], signal=(kc == 7))
                for kc in range(8):
                    ph.op("pe", lambda e, b=b, w=w, hs=hs, kc=kc, jj=jj, nb=nb: e.matmul(
                        b.t[:], w.t[:, kc, W + jj * 128:W + (jj + 1) * 128], hs.t[:, kc, nb * 512:(nb + 1) * 512],
                        start=(kc == 0), stop=(kc == 7)), reads=[w, hs], writes=[b], signal=(kc == 7))
                ph.op("act", lambda e, a=a, s_=s_: e.activation(out=s_.t[:], in_=a.t[:], func=AF.Silu),
                      reads=[a], writes=[s_])
                ph.op("dve", lambda e, b=b, s_=s_, j=j, nb=nb: e.tensor_tensor(
                    out=gT.t[:, j, nb * 512:(nb + 1) * 512], in0=s_.t[:], in1=b.t[:], op=ALU.mult),
                    reads=[s_, b], writes=[gTb[j]])

    def s3(sb):
        for t in range(TT):
            r0 = sb * T + t * 128
            x = rot(xr, t)
            y = rot(yb, t)
            s = rot(st, cnt["st"])
            cnt["st"] += 1
            jk = rot(junk, cnt["st"])
            ph.dma("sp", x, x.t[:], x_src[r0:r0 + 128, :], writes=[x])
            for hf in range(2):
                for j in range(NJ):
                    ph.op("pe", lambda e, j=j, hf=hf, t=t: e.matmul(
                        po.t[:, hf * 512:(hf + 1) * 512], gT.t[:, j, t * 128:(t + 1) * 128],
                        wd.t[:, j, hf * 512:(hf + 1) * 512], start=(j == 0), stop=(j == NJ - 1)),
                        reads=[gTb[j], wd], writes=[po], signal=(j == NJ - 1))
            ph.op("act", lambda e, y=y: e.activation(out=y.t[:], in_=po.t[:], func=AF.Copy), reads=[po], writes=[y])
            ph.op("act", lambda e, jk=jk, s=s, y=y: e.activation(out=jk.t[:], in_=y.t[:], func=AF.Square,
                                                                 accum_out=s.t[:, 0:1]), reads=[y], writes=[jk, s])
            ph.op("dve", lambda e, s=s: e.tensor_scalar(out=s.t[:, 1:2], in0=s.t[:, 0:1], scalar1=1.0 / D, scalar2=EPS,
                                                        op0=ALU.mult, op1=ALU.add), reads=[s], writes=[s])
            ph.op("pool", lambda e, s=s: e.tensor_tensor(out=s.t[:, 2:3], in0=s.t[:, 1:2], in1=cst["mhalf"].t[:, 0:1],
                                                         op=ALU.pow), reads=[s, cst["mhalf"]], writes=[s])
            ph.op("dve", lambda e, s=s, y=y: e.scalar_tensor_tensor(out=y.t[:], in0=y.t[:], scalar=s.t[:, 2:3],
                                                                    in1=gpost.t[:], op0=ALU.mult, op1=ALU.mult),
                  reads=[y, s, gpost], writes=[y])
            ph.op("pool", lambda e, x=x, y=y: e.tensor_tensor(out=x.t[:], in0=x.t[:], in1=y.t[:], op=ALU.add),
                  reads=[x, y], writes=[x])
            ph.dma("sp", x, x_dst[r0:r0 + 128, :], x.t[:], reads=[x])

    prev = None
    for t in range(TT):
        pr, xp = s1_tile(0, t)
        pr()
        if prev is not None:
            prev()
        prev = xp
    prev()
    gi = 0
    issue_w(0)
    issue_w(1)
    for sb in range(NSB):
        pend = None
        for jg in range(NG):
            issue_w(gi + 2)
            if sb == 0 and 1 <= jg <= 4:
                q = (jg - 1) * 6
                q1 = min(NJ, q + 6)
                ph.dma("pool", wd, wd.t[:, q:q1, :], w_dn_v[:, q:q1, :], writes=[wd])
            nxt_x = None
            if sb + 1 < NSB and jg < TT:
                pr, nxt_x = s1_tile(sb + 1, jg)
                pr()
            s2_group(sb, jg, gi)
            gi += 1
            if pend is not None:
                pend()
            pend = nxt_x
        if pend is not None:
            pend()
        s3(sb)
    ph.flush()


TWO_PI = 6.283185307179586
C1 = 6.28125
C2 = TWO_PI - 6.28125
MAGIC = 12582912.0
PI_LO = 3.1415925


def rope_phase(nc, name, pos_d, ropec_d, cos_d, sin_d):
    ph = Phase(nc, name)
    posi = ph.sbuf([128, NTOK], I32, "posi")
    ang = ph.sbuf([128, NTOK], F32, "ang")
    t1 = ph.sbuf([128, NTOK], F32, "t1")
    t2 = ph.sbuf([128, NTOK], F32, "t2")
    rc = ph.sbuf([128, 2], F32, "rc")
    one = ph.sbuf([128, 1], F32, "one")
    ph.dma("sp", posi, posi.t[:], pos_d.partition_broadcast(128), writes=[posi])
    ph.dma("sp", rc, rc.t[:], ropec_d, writes=[rc])
    ph.op("pool", lambda e: e.memset(one.t[:], 1.0), writes=[one])
    ph.op("dve", lambda e: e.tensor_copy(out=ang.t[:], in_=posi.t[:]), reads=[posi], writes=[ang])
    ph.op("dve", lambda e: e.tensor_scalar(out=ang.t[:], in0=ang.t[:], scalar1=rc.t[:, 0:1], scalar2=None, op0=ALU.mult),
          reads=[ang, rc], writes=[ang])
    for which, dst, scale_ap in (("sin", sin_d, rc), ("cos", cos_d, one)):
        off = 0.0 if which == "sin" else TWO_PI / 4
        ph.op("dve", lambda e, off=off: e.tensor_scalar(out=t1.t[:], in0=ang.t[:], scalar1=off, scalar2=None, op0=ALU.add),
              reads=[ang], writes=[t1])
        ph.op("dve", lambda e: e.tensor_scalar(out=t2.t[:], in0=t1.t[:], scalar1=1.0 / TWO_PI, scalar2=MAGIC,
                                               op0=ALU.mult, op1=ALU.add), reads=[t1], writes=[t2])
        ph.op("dve", lambda e: e.tensor_scalar(out=t2.t[:], in0=t2.t[:], scalar1=-MAGIC, scalar2=None, op0=ALU.add),
              reads=[t2], writes=[t2])
        ph.op("dve", lambda e: e.scalar_tensor_tensor(out=t1.t[:], in0=t2.t[:], scalar=-C1, in1=t1.t[:],
                                                      op0=ALU.mult, op1=ALU.add), reads=[t1, t2], writes=[t1])
        ph.op("dve", lambda e: e.scalar_tensor_tensor(out=t1.t[:], in0=t2.t[:], scalar=-C2, in1=t1.t[:],
                                                      op0=ALU.mult, op1=ALU.add), reads=[t1, t2], writes=[t1])
        ph.op("dve", lambda e: e.tensor_scalar(out=t1.t[:], in0=t1.t[:], scalar1=-PI_LO, scalar2=PI_LO,
                                               op0=ALU.max, op1=ALU.min), reads=[t1], writes=[t1])
        sc = scale_ap.t[:, 1:2] if which == "sin" else scale_ap.t[:, 0:1]
        ph.op("act", lambda e, sc=sc: e.activation(out=t2.t[:], in_=t1.t[:], func=AF.Sin, scale=sc),
              reads=[t1, scale_ap], writes=[t2])
        ph.dma("sp", t2, dst, t2.t[:], reads=[t2])
    ph.flush()


def proj_phase(nc, name, x_src, g_pre, w_in, ident_d, cos_d, sin_d, h2T_d, qT_d, kown_d, vown_d, upT_d, qcaT_d, edge_d):
    ph = Phase(nc, name)
    cst = load_consts(ph, ident_d)
    w_in_v = w_in.rearrange("(kc p) n -> p kc n", p=128)
    gpre = ph.sbuf([128, D], F32, "gpre")
    win = ph.sbuf([128, 8, 2048], BF16, "win")
    wsw = ph.sbuf([128, 8, 1024], BF16, "wsw")
    hT = [ph.sbuf([128, 8, 512], BF16, "hT") for _ in range(2)]
    xt = [ph.sbuf([128, D], F32, "xt") for _ in range(4)]
    hb = [ph.sbuf([128, D], BF16, "hb") for _ in range(4)]
    junk = [ph.sbuf([128, D], BF16, "junk") for _ in range(2)]
    st = [ph.sbuf([128, 4], F32, "st") for _ in range(4)]
    cosb = [ph.sbuf([128, 512], F32, "cosb") for _ in range(2)]
    sinb = [ph.sbuf([128, 512], F32, "sinb") for _ in range(2)]
    r1 = [ph.sbuf([128, 512], F32, "r1") for _ in range(2)]
    r2 = [ph.sbuf([128, 512], F32, "r2") for _ in range(2)]
    ob = [ph.sbuf([128, 512], BF16, "ob") for _ in range(4)]
    of = [ph.sbuf([128, 512], F32, "of") for _ in range(2)]
    pT = [ph.psum([128, 8, 128], BF16, "pT") for _ in range(2)]
    pp = [ph.psum([128, 512], F32, "pp") for _ in range(6)]
    c = {"x": 0, "st": 0, "hb": 0, "pT": 0, "pp": 0, "ob": 0, "of": 0, "r": 0}

    ph.dma("sp", gpre, gpre.t[:], bcast_row(g_pre), writes=[gpre])
    for c0, c1 in ((0, 256), (256, 768), (768, 1280), (1792, 2048), (1280, 1792)):
        ph.dma("pool", win, win.t[:, :, c0:c1], w_in_v[:, :, c0:c1], writes=[win])
    ph.op("pool", lambda e: e.memset(wsw.t[:], 0.0), writes=[wsw])
    src4 = win.t[:, :, 256:1280].rearrange("p k (b d) -> p k b d", d=64)
    dst4 = wsw.t[:].rearrange("p k (b d) -> p k b d", d=64)
    for kc in range(8):
        ph.op("pool", lambda e, kc=kc: e.tensor_copy(out=dst4[:, kc, :, 0:8], in_=src4[:, kc, :, 8:16]), reads=[win], writes=[wsw])
        ph.op("pool", lambda e, kc=kc: e.tensor_copy(out=dst4[:, kc, :, 8:16], in_=src4[:, kc, :, 0:8]), reads=[win], writes=[wsw])

    def pbank():
        p = rot(pp, c["pp"])
        c["pp"] += 1
        return p

    def fm_proj(p, wt, col0, hs):
        for kc in range(8):
            ph.op("pe", lambda e, kc=kc: e.matmul(p.t[:], wt.t[:, kc, col0:col0 + 128], hs.t[:, kc, :],
                                                  start=(kc == 0), stop=(kc == 7)), reads=[wt, hs], writes=[p], signal=(kc == 7))

    def s1_block(blk):
        hs = hT[blk % 2]
        preps, xposes = [], []
        for t in range(4):
            r0 = blk * 512 + t * 128
            x = rot(xt, c["x"]); c["x"] += 1
            s = rot(st, c["st"]); c["st"] += 1
            jk = rot(junk, c["st"])
            h = rot(hb, c["hb"]); c["hb"] += 1
            p = rot(pT, c["pT"]); c["pT"] += 1

            def prep(x=x, s=s, jk=jk, h=h, r0=r0):
                ph.dma("sp", x, x.t[:], x_src[r0:r0 + 128, :], writes=[x])
                ph.op("act", lambda e: e.activation(out=jk.t[:], in_=x.t[:], func=AF.Square, accum_out=s.t[:, 0:1]),
                      reads=[x], writes=[jk, s])
                ph.op("dve", lambda e: e.tensor_scalar(out=s.t[:, 1:2], in0=s.t[:, 0:1], scalar1=1.0 / D, scalar2=EPS,
                                                       op0=ALU.mult, op1=ALU.add), reads=[s], writes=[s])
                ph.op("pool", lambda e: e.tensor_tensor(out=s.t[:, 2:3], in0=s.t[:, 1:2], in1=cst["mhalf"].t[:, 0:1],
                                                        op=ALU.pow), reads=[s, cst["mhalf"]], writes=[s])
                ph.op("dve", lambda e: e.scalar_tensor_tensor(out=h.t[:], in0=x.t[:], scalar=s.t[:, 2:3], in1=gpre.t[:],
                                                              op0=ALU.mult, op1=ALU.mult), reads=[x, s, gpre], writes=[h])

            def xpose(h=h, p=p, t=t, hs=hs):
                for kc in range(8):
                    ph.op("pe", lambda e, kc=kc: e.transpose(out=p.t[:, kc, :], in_=h.t[:, kc * 128:(kc + 1) * 128],
                                                             identity=cst["ident"].t[:]),
                          reads=[h, cst["ident"]], writes=[p], signal=(kc == 7))
                ph.op("act", lambda e: e.activation(out=hs.t[:, :, t * 128:(t + 1) * 128], in_=p.t[:], func=AF.Copy),
                      reads=[p], writes=[hs])

            preps.append(prep)
            xposes.append(xpose)
        return preps, xposes

    NBLK = NTOK // 512
    pr0, xp0 = s1_block(0)
    for f in pr0:
        f()
    for f in xp0:
        f()
    for blk in range(NBLK):
        hs = hT[blk % 2]
        t0 = blk * 512
        nxt = s1_block(blk + 1) if blk + 1 < NBLK else None
        ph.dma("sp", hs, h2T_d[:, :, t0:t0 + 512], hs.t[:], reads=[hs])
        cb = cosb[blk % 2]
        sb_ = sinb[blk % 2]
        ph.dma("sp", cb, cb.t[:], cos_d[:, t0:t0 + 512], writes=[cb])
        ph.dma("sp", sb_, sb_.t[:], sin_d[:, t0:t0 + 512], writes=[sb_])
        for cc in range(2):
            p = pbank()
            fm_proj(p, win, cc * 128, hs)
            o = rot(of, c["of"]); c["of"] += 1
            ph.op("act", lambda e, p=p, o=o: e.activation(out=o.t[:], in_=p.t[:], func=AF.Copy), reads=[p], writes=[o])
            ph.dma("sp", o, upT_d[cc, :, t0:t0 + 512], o.t[:], reads=[o])
            if blk == 0:
                ph.dma("sp", o, edge_d[cc * 128:(cc + 1) * 128, 0:8], o.t[:, 0:8], reads=[o])
            if blk == NTOK // 512 - 1:
                ph.dma("sp", o, edge_d[cc * 128:(cc + 1) * 128, 8:16], o.t[:, 504:512], reads=[o])
        if nxt is not None:
            for f in nxt[0]:
                f()
        for which, dstd in (("q", qT_d), ("k", kown_d)):
            base = 256 if which == "q" else 768
            if which == "k" and nxt is not None:
                for f in nxt[1]:
                    f()
            for hh in range(4):
                p = pbank()
                ps = pbank()
                fm_proj(p, win, base + hh * 128, hs)
                fm_proj(ps, wsw, (base - 256) + hh * 128, hs)
                a = rot(r1, c["r"]); b = rot(r2, c["r"]); c["r"] += 1
                o = rot(ob, c["ob"]); c["ob"] += 1
                ph.op("dve", lambda e, p=p, a=a, cb=cb: e.tensor_tensor(out=a.t[:], in0=p.t[:], in1=cb.t[:], op=ALU.mult),
                      reads=[p, cb], writes=[a])
                ph.op("dve", lambda e, ps=ps, b=b, sb_=sb_: e.tensor_tensor(out=b.t[:], in0=ps.t[:], in1=sb_.t[:], op=ALU.mult),
                      reads=[ps, sb_], writes=[b])
                ph.op("pool", lambda e, a=a, b=b, o=o: e.tensor_tensor(out=o.t[:], in0=a.t[:], in1=b.t[:], op=ALU.add),
                      reads=[a, b], writes=[o])
                ph.dma("sp", o, dstd[hh * 128:(hh + 1) * 128, t0:t0 + 512], o.t[:], reads=[o])
        for cc in range(2):
            p = pbank()
            fm_proj(p, win, 1792 + cc * 128, hs)
            o = rot(ob, c["ob"]); c["ob"] += 1
            ph.op("act", lambda e, p=p, o=o: e.activation(out=o.t[:], in_=p.t[:], func=AF.Copy), reads=[p], writes=[o])
            ph.dma("sp", o, qcaT_d[cc * 128:(cc + 1) * 128, t0:t0 + 512], o.t[:], reads=[o])
        for t in range(4):
            r0 = t0 + t * 128
            p = pbank()
            for kc in range(8):
                ph.op("pe", lambda e, p=p, kc=kc, t=t, hs=hs: e.matmul(p.t[:], hs.t[:, kc, t * 128:(t + 1) * 128],
                                                                  win.t[:, kc, 1280:1792], start=(kc == 0), stop=(kc == 7)),
                      reads=[win, hs], writes=[p], signal=(kc == 7))
            o = rot(ob, c["ob"]); c["ob"] += 1
            ph.op("act", lambda e, p=p, o=o: e.activation(out=o.t[:], in_=p.t[:], func=AF.Copy), reads=[p], writes=[o])
            ph.dma("sp", o, vown_d[r0:r0 + 128, :], o.t[:], reads=[o])
    ph.flush()


PAIRS = [[0, 1], [2, 3], [4, 5], [6, 7]]


def exchange_phase(nc, name, items):
    sems = [nc.alloc_semaphore(name=f"{name}_cc{i}") for i in range(len(items))]
    with nc.Block() as block:
        @block.gpsimd
        def _(g):
            for (src, dst), sem in zip(items, sems):
                g.collective_compute("AllGather", ALU.bypass, replica_groups=PAIRS, ins=[src], outs=[dst]).then_inc(sem)
            for sem in sems:
                g.wait_ge(sem, 1)
    nc.clear_and_free_semaphores(sems)
    nc.all_engine_barrier()


def attn_phase(nc, name, qT_d, kfull_d, vfull_d, lamq1, lamk1, lamq2, lamk2, subg, lam_init, ydaT_d):
    ph = Phase(nc, name)
    NKT = SEQ // 128
    kT = [ph.sbuf([128, SEQ], BF16, "kT") for _ in range(2)]
    vt = [ph.sbuf([128, NKT, 128], BF16, "vt") for _ in range(2)]
    ones = ph.sbuf([128, 128], BF16, "ones")
    epsb = ph.sbuf([128, 1], F32, "epsb")
    lam4 = ph.sbuf([128, 4, 64], F32, "lam4")
    lj = ph.sbuf([128, 64], F32, "lj")
    lc = ph.sbuf([128, 8], F32, "lc")
    gs = ph.sbuf([128, 1], F32, "gs")
    qb = [ph.sbuf([128, 512], BF16, "qb") for _ in range(2)]
    pe_ = [ph.sbuf([128, 512], BF16, "pe") for _ in range(6)]
    rr = [ph.sbuf([128, 512], F32, "rr") for _ in range(2)]
    o1 = [ph.sbuf([128, 512], F32, "o1") for _ in range(2)]
    o2 = [ph.sbuf([128, 512], F32, "o2") for _ in range(2)]
    sq = [ph.sbuf([128, 512], BF16, "sq") for _ in range(2)]
    rs = [ph.sbuf([128, 512], F32, "rs") for _ in range(2)]
    yo = [ph.sbuf([128, 512], BF16, "yo") for _ in range(2)]
    accO = [ph.psum([128, 512], F32, "accO") for _ in range(2)]
    accS = [ph.psum([128, 512], F32, "accS") for _ in range(2)]
    scp = [[ph.psum([128, 512], F32, "sc") for _ in range(2)] for _ in range(2)]

    ph.op("pool", lambda e: e.memset(ones.t[:], 1.0), writes=[ones])
    ph.op("pool", lambda e: e.memset(epsb.t[:], EPS), writes=[epsb])
    for i, v in enumerate((lamq1, lamk1, lamq2, lamk2)):
        ph.dma("sp", lam4, lam4.t[:, i, :], v.partition_broadcast(128), writes=[lam4])
    ph.dma("sp", gs, gs.t[:], subg.rearrange("(p o) -> p o", o=1), writes=[gs])
    for i in range(2):
        ph.op("dve", lambda e, i=i: e.scalar_tensor_tensor(out=lj.t[:], in0=lam4.t[:, 2 * i, :], scalar=1.0,
                                                            in1=lam4.t[:, 2 * i + 1, :], op0=ALU.mult, op1=ALU.mult,
                                                            accum_out=lc.t[:, i:i + 1]), reads=[lam4], writes=[lj, lc])
    ph.op("act", lambda e: e.activation(out=lc.t[:, 2:4], in_=lc.t[:, 0:2], func=AF.Exp), reads=[lc], writes=[lc])
    ph.op("dve", lambda e: e.tensor_tensor(out=lc.t[:, 4:5], in0=lc.t[:, 2:3], in1=lc.t[:, 3:4], op=ALU.subtract),
          reads=[lc], writes=[lc])
    ph.op("dve", lambda e: e.tensor_scalar(out=lc.t[:, 5:6], in0=lc.t[:, 4:5], scalar1=lam_init, scalar2=-1.0,
                                           op0=ALU.add, op1=ALU.mult), reads=[lc], writes=[lc])
    ph.op("dve", lambda e: e.tensor_scalar(out=lc.t[:, 6:7], in0=gs.t[:, 0:1], scalar1=1.0 - lam_init, scalar2=None,
                                           op0=ALU.mult), reads=[gs, lc], writes=[lc])
    ss_ = [[ph.sbuf([128, 512], F32, "ssc") for _ in range(2)] for _ in range(2)]
    prs = [[ph.sbuf([128, 512], BF16, "prs") for _ in range(3)] for _ in range(2)]
    cq = 0
    cp_ = 0

    def load_kv(h):
        k = kT[h % 2]
        v = vt[h % 2]
        for r in range(2):
            ph.dma("sp", k, k.t[:, r * NTOK:(r + 1) * NTOK], kfull_d[h][r * 128:(r + 1) * 128, :], writes=[k])
            for pc in range(4):
                kt0 = r * 32 + pc * 8
                ph.dma("sp", v, v.t[:, kt0:kt0 + 8, :],
                       vfull_d[pc][r * 1024:(r + 1) * 1024, h * 128:(h + 1) * 128].rearrange("(kt p) e -> p kt e", p=128),
                       writes=[v])

    def make_post(h, blk):
        t0 = blk * 512
        i2 = blk % 2
        a1, a2, sq_, rs_, y = o1[i2], o2[i2], sq[i2], rs[i2], yo[i2]
        s0, s1 = ss_[i2]

        def part1():
            for c, sc_, a_ in ((0, s0, a1), (1, s1, a2)):
                ph.op("act", lambda e, c=c, sc_=sc_: e.activation(out=sc_.t[:], in_=accS[c].t[:], func=AF.Copy),
                      reads=[accS[c]], writes=[sc_])
                ph.op("dve", lambda e, c=c, a_=a_: e.tensor_copy(out=a_.t[:], in_=accO[c].t[:]), reads=[accO[c]], writes=[a_])
            for sc_, a_ in ((s0, a1), (s1, a2)):
                ph.op("dve", lambda e, sc_=sc_: e.reciprocal(out=sc_.t[:], in_=sc_.t[:]), reads=[sc_], writes=[sc_])
                ph.op("dve", lambda e, sc_=sc_, a_=a_: e.tensor_tensor(out=a_.t[:], in0=a_.t[:], in1=sc_.t[:], op=ALU.mult),
                      reads=[a_, sc_], writes=[a_])
            ph.op("dve", lambda e: e.scalar_tensor_tensor(out=a1.t[:], in0=a2.t[:], scalar=lc.t[:, 5:6], in1=a1.t[:],
                                                          op0=ALU.mult, op1=ALU.add), reads=[a1, a2, lc], writes=[a1])

        def part2(npz):
            ph.op("act", lambda e: e.activation(out=sq_.t[:], in_=a1.t[:], func=AF.Square), reads=[a1], writes=[sq_])
            ph.op("pe", lambda e: e.matmul(npz.t[:], ones.t[:], sq_.t[:], start=True, stop=True),
                  reads=[ones, sq_], writes=[npz])
            ph.op("act", lambda e: e.activation(out=rs_.t[:], in_=npz.t[:], func=AF.Sqrt, scale=1.0 / 128,
                                                bias=epsb.t[:, 0:1]), reads=[npz, epsb], writes=[rs_])
            ph.op("dve", lambda e: e.reciprocal(out=rs_.t[:], in_=rs_.t[:]), reads=[rs_], writes=[rs_])
            ph.op("dve", lambda e: e.scalar_tensor_tensor(out=y.t[:], in0=a1.t[:], scalar=lc.t[:, 6:7], in1=rs_.t[:],
                                                          op0=ALU.mult, op1=ALU.mult), reads=[a1, rs_, lc], writes=[y])
            ph.dma("sp", y, ydaT_d[h * 128:(h + 1) * 128, t0:t0 + 512], y.t[:], reads=[y])

        return part1, part2

    DEFER_KT = 8
    pending = None
    load_kv(0)
    for h in range(4):
        k = kT[h % 2]
        v = vt[h % 2]
        for blk in range(NTOK // 512):
            t0 = blk * 512
            q = rot(qb, cq)
            cq += 1
            ph.dma("sp", q, q.t[:], qT_d[h * 128:(h + 1) * 128, t0:t0 + 512], writes=[q])
            if blk == 1 and h + 1 < 4:
                load_kv(h + 1)

            def score(kt, q=q, k=k):
                for c in range(2):
                    s_ = scp[c][kt % 2]
                    ph.op("pe", lambda e, s_=s_, c=c, kt=kt, q=q, k=k: e.matmul(
                        s_.t[:], k.t[c * 64:(c + 1) * 64, kt * 128:(kt + 1) * 128], q.t[c * 64:(c + 1) * 64, :],
                        start=True, stop=True), reads=[k, q], writes=[s_])

            score(0)
            pendS = []
            peven = [None, None]
            for kt in range(NKT):
                if kt == DEFER_KT and pending is not None:
                    pending(scp[0][(kt + 1) % 2])
                    pending = None
                if kt + 1 < NKT:
                    score(kt + 1)
                for f in pendS:
                    f()
                pendS = []
                for c in range(2):
                    s_ = scp[c][kt % 2]
                    p = rot(pe_, cp_)
                    cp_ += 1
                    ph.op("act", lambda e, s_=s_, p=p: e.activation(out=p.t[:], in_=s_.t[:], func=AF.Exp, scale=0.125),
                          reads=[s_], writes=[p])
                    ph.op("pe", lambda e, c=c, kt=kt, p=p, v=v: e.matmul(accO[c].t[:], v.t[:, kt, :], p.t[:],
                                                                         start=(kt == 0), stop=(kt == NKT - 1)),
                          reads=[v, p], writes=[accO[c]])
                    if kt % 2 == 0:
                        peven[c] = p
                    else:
                        pr = rot(prs[c], kt // 2)
                        pe0 = peven[c]
                        ph.op("dve", lambda e, pr=pr, pe0=pe0, p=p: e.tensor_tensor(out=pr.t[:], in0=pe0.t[:], in1=p.t[:], op=ALU.add),
                              reads=[pe0, p], writes=[pr])

                        def smm(c=c, kt=kt, pr=pr):
                            ph.op("pe", lambda e: e.matmul(accS[c].t[:], ones.t[:], pr.t[:], start=(kt == 1), stop=(kt == NKT - 1)),
                                  reads=[ones, pr], writes=[accS[c]])
                        pendS.append(smm)
            for f in pendS:
                f()
            p1, pending = make_post(h, blk)
            p1()
    pending(scp[0][0])
    ph.flush()


def merge_phase(nc, name, x_src, x_dst, ident_d, h2T_d, upT_d, edgefull_d, hmask_d, rcnt_d, qcaT_d, ydaT_d, mem_d,
                pool_w, pool_scale, mem_g, w_mem_kv, w_gate, b_gate, w_bp, w_bd, w_bc, w_out, g_post, dbg=False):
    ph = Phase(nc, name)

    def dump(nm, tl, shape, dt):
        if dbg:
            o = nc.dram_tensor("dscr_" + nm, shape, dt).ap()
            ph.dma("sp", tl, o, tl.t[:], reads=[tl])
            DBG_ITEMS.append((nm, o, shape, dt))
    cst = load_consts(ph, ident_d)
    wg = ph.sbuf([128, 8, 3072], BF16, "wg")
    wbp = ph.sbuf([128, 2, D], BF16, "wbp")
    wbd = ph.sbuf([128, 4, D], BF16, "wbd")
    wbc = ph.sbuf([128, 2, D], BF16, "wbc")
    wo = ph.sbuf([128, 8, D], BF16, "wo")
    wkv = ph.sbuf([128, 8, 512], BF16, "wkv")
    wblk = ph.sbuf([128, 2, 128], BF16, "wblk")
    bg = ph.sbuf([128, 24], F32, "bg")
    psc = ph.sbuf([128, 2], F32, "psc")
    hm = ph.sbuf([128, 2], F32, "hm")
    gpost = ph.sbuf([128, D], F32, "gpost")
    gmem = ph.sbuf([128, D], F32, "gmem")
    memT = ph.sbuf([128, 8, NMEM], BF16, "memT")
    kmT = ph.sbuf([128, 2, NMEM], BF16, "kmT")
    vpad = [ph.sbuf([128, 2, 2, 128], BF16, "vpad") for _ in range(2)]
    onesel = [ph.sbuf([128, 128], BF16, "onesel") for _ in range(2)]
    hT = [ph.sbuf([128, 8, 512], BF16, "hT") for _ in range(2)]
    U = [ph.sbuf([128, 2, 528], F32, "U") for _ in range(1)]
    A = [ph.sbuf([128, 2, 528], F32, "A") for _ in range(1)]
    Bt = [ph.sbuf([128, 2, 528], F32, "B") for _ in range(1)]
    rcn = [ph.sbuf([128, 2, 512], F32, "rcn") for _ in range(1)]
    pl = [ph.sbuf([128, 2, 512], BF16, "pl") for _ in range(1)]
    ypl = [ph.sbuf([128, 2, 512], BF16, "ypl") for _ in range(2)]
    qca = [ph.sbuf([128, 2, 512], BF16, "qca") for _ in range(2)]
    yca = [ph.sbuf([128, 2, 512], BF16, "yca") for _ in range(2)]
    yda = [ph.sbuf([128, 4, 512], BF16, "yda") for _ in range(2)]
    pex = [ph.sbuf([128, 512], BF16, "pex") for _ in range(2)]
    rcp = [ph.sbuf([128, 512], F32, "rcp") for _ in range(1)]
    tg = [ph.sbuf([128, 512], F32, "tg") for _ in range(3)]
    um = [ph.sbuf([128, 512], F32, "um") for _ in range(3)]
    mT = [ph.sbuf([128, 8, 512], BF16, "mT") for _ in range(1)]
    xt = [ph.sbuf([128, D], F32, "xt") for _ in range(2)]
    yb = [ph.sbuf([128, D], F32, "yb") for _ in range(2)]
    hb = [ph.sbuf([128, D], BF16, "hb") for _ in range(2)]
    junk = [ph.sbuf([128, D], BF16, "junk") for _ in range(1)]
    st = [ph.sbuf([128, 4], F32, "st") for _ in range(4)]
    po = ph.psum([128, D], F32, "po")
    pp = [ph.psum([128, 512], F32, "pp") for _ in range(6)]
    c = {"pp": 0, "x": 0, "st": 0, "pex": 0, "tg": 0, "um": 0}

    def pbank():
        p = rot(pp, c["pp"])
        c["pp"] += 1
        return p

    def wload(tl, src, n, parts=1):
        v = src.rearrange("(kc p) n -> p kc n", p=128)
        step = n // parts
        for k0 in range(0, n, step):
            ph.dma("pool", tl, tl.t[:, k0:k0 + step, :], v[:, k0:k0 + step, :], writes=[tl])

    wload(wkv, w_mem_kv, 8)
    wload(wbp, w_bp, 2)
    wload(wbd, w_bd, 4)
    wload(wbc, w_bc, 2)
    wload(wg, w_gate, 8, parts=4)
    wload(wo, w_out, 8, parts=2)
    for tl in (wbp, wbd, wbc):
        ph.op("pool", lambda e, tl=tl: e.tensor_scalar(out=tl.t[:], in0=tl.t[:], scalar1=0.5, scalar2=None, op0=ALU.mult),
              reads=[tl], writes=[tl])
    ph.dma("sp", bg, bg.t[:], b_gate.rearrange("(c p) -> p c", p=128), writes=[bg], allow_slow_non_contiguous=True)
    ph.op("dve", lambda e: e.tensor_scalar(out=bg.t[:], in0=bg.t[:], scalar1=0.5, scalar2=None, op0=ALU.mult),
          reads=[bg], writes=[bg])
    ph.dma("sp", psc, psc.t[:], pool_scale.rearrange("(c p) -> p c", p=128), writes=[psc], allow_slow_non_contiguous=True)
    ph.dma("sp", hm, hm.t[:], hmask_d, writes=[hm])
    ph.dma("sp", gpost, gpost.t[:], bcast_row(g_post), writes=[gpost])
    ph.dma("sp", gmem, gmem.t[:], bcast_row(mem_g), writes=[gmem])
    ph.op("pool", lambda e: e.memset(wblk.t[:], 0.0), writes=[wblk])
    for g in range(4):
        lo = (g % 2) * 64
        ph.dma("pool", wblk, wblk.t[lo:lo + 64, g // 2, lo:lo + 64], pool_w[g], writes=[wblk])
    for hh in range(2):
        ph.op("pool", lambda e, hh=hh: e.memset(onesel[hh].t[:], 0.0), writes=[onesel[hh]])
        ph.op("pool", lambda e, hh=hh: e.memset(onesel[hh].t[:, hh * 64:(hh + 1) * 64], 1.0), writes=[onesel[hh]])
        ph.op("pool", lambda e, hh=hh: e.memset(vpad[hh].t[:], 0.0), writes=[vpad[hh]])

    def norm_tile(x, g_t, out_t, src_t=None):
        s = rot(st, c["st"]); c["st"] += 1
        jk = rot(junk, c["st"])
        srcT = src_t if src_t is not None else x
        if src_t is not None:
            ph.op("act", lambda e: e.activation(out=out_t.t[:], in_=src_t.t[:], func=AF.Copy), reads=[src_t], writes=[out_t])
            srcT = out_t
        ph.op("act", lambda e: e.activation(out=jk.t[:], in_=srcT.t[:], func=AF.Square, accum_out=s.t[:, 0:1]),
              reads=[srcT], writes=[jk, s])
        ph.op("dve", lambda e: e.tensor_scalar(out=s.t[:, 1:2], in0=s.t[:, 0:1], scalar1=1.0 / D, scalar2=EPS,
                                               op0=ALU.mult, op1=ALU.add), reads=[s], writes=[s])
        ph.op("pool", lambda e: e.tensor_tensor(out=s.t[:, 2:3], in0=s.t[:, 1:2], in1=cst["mhalf"].t[:, 0:1], op=ALU.pow),
              reads=[s, cst["mhalf"]], writes=[s])
        ph.op("dve", lambda e: e.scalar_tensor_tensor(out=out_t.t[:], in0=srcT.t[:], scalar=s.t[:, 2:3], in1=g_t.t[:],
                                                      op0=ALU.mult, op1=ALU.mult), reads=[srcT, s, g_t], writes=[out_t])

    for m in range(2):
        x = rot(xt, c["x"]); c["x"] += 1
        h = hb[m]
        ph.dma("sp", x, x.t[:], mem_d[m * 128:(m + 1) * 128, :], writes=[x])
        norm_tile(x, gmem, h)
        for kc in range(8):
            pt = pbank()
            ph.op("pe", lambda e, pt=pt, h=h, kc=kc: e.transpose(out=pt.t[:].bitcast(BF16)[:, 0:128],
                                                                 in_=h.t[:, kc * 128:(kc + 1) * 128],
                                                                 identity=cst["ident"].t[:]),
                  reads=[h, cst["ident"]], writes=[pt])
            ph.op("act", lambda e, pt=pt, kc=kc, m=m: e.activation(out=memT.t[:, kc, m * 128:(m + 1) * 128],
                                                                    in_=pt.t[:].bitcast(BF16)[:, 0:128], func=AF.Copy),
                  reads=[pt], writes=[memT])
    for cc in range(2):
        p = pbank()
        for kc in range(8):
            ph.op("pe", lambda e, p=p, kc=kc, cc=cc: e.matmul(p.t[:, 0:NMEM], wkv.t[:, kc, cc * 128:(cc + 1) * 128],
                                                              memT.t[:, kc, :], start=(kc == 0), stop=(kc == 7)),
                  reads=[wkv, memT], writes=[p], signal=(kc == 7))
        ph.op("act", lambda e, p=p, cc=cc: e.activation(out=kmT.t[:, cc, :], in_=p.t[:, 0:NMEM], func=AF.Copy),
              reads=[p], writes=[kmT])
    for m in range(2):
        p = pbank()
        for kc in range(8):
            ph.op("pe", lambda e, p=p, kc=kc, m=m: e.matmul(p.t[:, 0:256], memT.t[:, kc, m * 128:(m + 1) * 128],
                                                            wkv.t[:, kc, 256:512], start=(kc == 0), stop=(kc == 7)),
                  reads=[wkv, memT], writes=[p], signal=(kc == 7))
        for cp in range(2):
            for hh in range(2):
                hd = 2 * cp + hh
                ph.op("act", lambda e, p=p, m=m, cp=cp, hh=hh, hd=hd: e.activation(
                    out=vpad[hh].t[:, m, cp, hh * 64:(hh + 1) * 64], in_=p.t[:, hd * 64:(hd + 1) * 64], func=AF.Copy),
                    reads=[p], writes=[vpad[hh]])

    NBLK = NTOK // 512

    def tiles_for(blk):
        i2 = blk % 2
        return hT[i2], ypl[i2], qca[i2], yca[i2], yda[i2]

    def front_l(blk):
        t0 = blk * 512
        hs, ypl_, qc, yc, yd = tiles_for(blk)
        ph.dma("sp", hs, hs.t[:], h2T_d[:, :, t0:t0 + 512], writes=[hs])
        for cc in range(2):
            ph.dma("sp", qc, qc.t[:, cc, :], qcaT_d[cc * 128:(cc + 1) * 128, t0:t0 + 512], writes=[qc])
        for hh in range(4):
            ph.dma("sp", yd, yd.t[:, hh, :], ydaT_d[hh * 128:(hh + 1) * 128, t0:t0 + 512], writes=[yd])

    def front_a(blk):
        t0 = blk * 512
        hs, ypl_, qc, yc, yd = tiles_for(blk)
        u, a, b, rc_, pl_ = U[0], A[0], Bt[0], rcn[0], pl[0]
        for cc in range(2):
            lo = max(t0 - 8, 0)
            hi = min(t0 + 520, NTOK)
            ph.dma("sp", u, u.t[:, cc, 8 - (t0 - lo):8 + (hi - t0)], upT_d[cc, :, lo:hi], writes=[u])
            if blk == 0:
                ph.dma("sp", u, u.t[:, cc, 0:8], edgefull_d[cc * 128:(cc + 1) * 128, 8:16], writes=[u])
            if blk == NBLK - 1:
                ph.dma("sp", u, u.t[:, cc, 520:528], edgefull_d[256 + cc * 128:256 + (cc + 1) * 128, 0:8], writes=[u])
            ph.dma("sp", rc_, rc_.t[:, cc, :], rcnt_d[cc, :, t0:t0 + 512], writes=[rc_])
        if blk == 0:
            ph.op("dve", lambda e, u=u: e.tensor_scalar(out=u.t[:, :, 0:8], in0=u.t[:, :, 0:8], scalar1=hm.t[:, 0:1],
                                                        scalar2=None, op0=ALU.mult), reads=[u, hm], writes=[u])
        if blk == NBLK - 1:
            ph.op("dve", lambda e, u=u: e.tensor_scalar(out=u.t[:, :, 520:528], in0=u.t[:, :, 520:528], scalar1=hm.t[:, 1:2],
                                                        scalar2=None, op0=ALU.mult), reads=[u, hm], writes=[u])
        ph.op("pool", lambda e, u=u, a=a: e.tensor_tensor(out=a.t[:, :, 0:527], in0=u.t[:, :, 0:527], in1=u.t[:, :, 1:528],
                                                          op=ALU.add), reads=[u], writes=[a])
        ph.op("pool", lambda e, a=a, b=b: e.tensor_tensor(out=b.t[:, :, 0:525], in0=a.t[:, :, 0:525], in1=a.t[:, :, 2:527],
                                                          op=ALU.add), reads=[a], writes=[b])
        ph.op("dve", lambda e, a=a, rc_=rc_: e.tensor_tensor(out=rc_.t[0:64, 0, :], in0=a.t[0:64, 0, 7:519],
                                                             in1=rc_.t[0:64, 0, :], op=ALU.mult), reads=[a, rc_], writes=[rc_])
        ph.op("dve", lambda e, b=b, rc_=rc_: e.tensor_tensor(out=rc_.t[64:128, 0, :], in0=b.t[64:128, 0, 6:518],
                                                             in1=rc_.t[64:128, 0, :], op=ALU.mult), reads=[b, rc_], writes=[rc_])
        ph.op("pool", lambda e, a=a, b=b: e.tensor_tensor(out=a.t[:, 1, 0:521], in0=b.t[:, 1, 0:521], in1=b.t[:, 1, 4:525],
                                                          op=ALU.add), reads=[b], writes=[a])
        ph.op("pool", lambda e, a=a, b=b: e.tensor_tensor(out=b.t[64:128, 1, 0:513], in0=a.t[64:128, 1, 0:513],
                                                          in1=a.t[64:128, 1, 8:521], op=ALU.add), reads=[a], writes=[b])
        ph.op("dve", lambda e, a=a, rc_=rc_: e.tensor_tensor(out=rc_.t[0:64, 1, :], in0=a.t[0:64, 1, 4:516],
                                                             in1=rc_.t[0:64, 1, :], op=ALU.mult), reads=[a, rc_], writes=[rc_])
        ph.op("dve", lambda e, b=b, rc_=rc_: e.tensor_tensor(out=rc_.t[64:128, 1, :], in0=b.t[64:128, 1, 0:512],
                                                             in1=rc_.t[64:128, 1, :], op=ALU.mult), reads=[b, rc_], writes=[rc_])
        ph.op("dve", lambda e, u=u, rc_=rc_, pl_=pl_: e.tensor_tensor(out=pl_.t[:], in0=rc_.t[:], in1=u.t[:, :, 8:520],
                                                                      op=ALU.subtract), reads=[u, rc_], writes=[pl_])

    def front_b(blk):
        t0 = blk * 512
        hs, ypl_, qc, yc, yd = tiles_for(blk)
        pl_ = pl[0]
        for cc in range(2):
            p = pbank()
            ph.op("pe", lambda e, p=p, cc=cc, pl_=pl_: e.matmul(p.t[:], wblk.t[:, cc, :], pl_.t[:, cc, :], start=True, stop=True),
                  reads=[wblk, pl_], writes=[p])
            ph.op("act", lambda e, p=p, cc=cc, ypl_=ypl_: e.activation(out=ypl_.t[:, cc, :], in_=p.t[:], func=AF.Identity,
                                                                        scale=psc.t[:, cc:cc + 1]), reads=[p, psc], writes=[ypl_])
        for cp in range(2):
            pO = pbank()
            pS = pbank()
            n = 0
            for hh in range(2):
                for m in range(2):
                    ps_ = pbank()
                    ph.op("pe", lambda e, ps_=ps_, hh=hh, m=m, cp=cp, qc=qc: e.matmul(
                        ps_.t[:], kmT.t[hh * 64:(hh + 1) * 64, cp, m * 128:(m + 1) * 128], qc.t[hh * 64:(hh + 1) * 64, cp, :],
                        start=True, stop=True), reads=[kmT, qc], writes=[ps_])
                    px = rot(pex, c["pex"]); c["pex"] += 1
                    ph.op("act", lambda e, ps_=ps_, px=px: e.activation(out=px.t[:], in_=ps_.t[:], func=AF.Exp, scale=0.125),
                          reads=[ps_], writes=[px])
                    ph.op("pe", lambda e, pO=pO, hh=hh, m=m, cp=cp, px=px, n=n: e.matmul(
                        pO.t[:], vpad[hh].t[:, m, cp, :], px.t[:], start=(n == 0), stop=(n == 3)),
                        reads=[vpad[hh], px], writes=[pO], signal=False)
                    ph.op("pe", lambda e, pS=pS, hh=hh, px=px, n=n: e.matmul(
                        pS.t[:], onesel[hh].t[:], px.t[:], start=(n == 0), stop=(n == 3)),
                        reads=[onesel[hh], px], writes=[pS])
                    n += 1
            r_ = rcp[0]
            ph.op("dve", lambda e, r_=r_, pS=pS: e.reciprocal(out=r_.t[:], in_=pS.t[:]), reads=[pS], writes=[r_])
            ph.op("dve", lambda e, r_=r_, pO=pO, cp=cp, yc=yc: e.tensor_tensor(out=yc.t[:, cp, :], in0=pO.t[:], in1=r_.t[:],
                                                                                op=ALU.mult), reads=[pO, r_], writes=[yc])

    def back(blk):
        t0 = blk * 512
        hs, ypl_, qc, yc, yd = tiles_for(blk)
        mt = mT[0]
        for oc in range(8):
            if oc == 4:
                if blk + 1 < NBLK:
                    front_b(blk + 1)
                if blk + 2 < NBLK:
                    front_a(blk + 2)
            brs = []
            for (wt, src, nk) in ((wbp, ypl_, 2), (wbd, yd, 4), (wbc, yc, 2)):
                p = pbank()
                for kc in range(nk):
                    ph.op("pe", lambda e, p=p, wt=wt, src=src, kc=kc, oc=oc, nk=nk: e.matmul(
                        p.t[:], wt.t[:, kc, oc * 128:(oc + 1) * 128], src.t[:, kc, :], start=(kc == 0), stop=(kc == nk - 1)),
                        reads=[wt, src], writes=[p], signal=(kc == nk - 1))
                brs.append(p)
            us = []
            for x_ in range(3):
                p = pbank()
                col = x_ * 1024 + oc * 128
                for kc in range(8):
                    ph.op("pe", lambda e, p=p, kc=kc, col=col, hs=hs: e.matmul(p.t[:], wg.t[:, kc, col:col + 128], hs.t[:, kc, :],
                                                                                start=(kc == 0), stop=(kc == 7)),
                          reads=[wg, hs], writes=[p], signal=(kc == 7))
                t_ = rot(tg, c["tg"]); c["tg"] += 1
                u_ = rot(um, c["um"]); c["um"] += 1
                bi = x_ * 8 + oc
                ph.op("act", lambda e, p=p, t_=t_, bi=bi: e.activation(out=t_.t[:], in_=p.t[:], func=AF.Tanh,
                                                                        bias=bg.t[:, bi:bi + 1], scale=0.5), reads=[p, bg], writes=[t_])
                br = brs[x_]
                ph.op("dve", lambda e, t_=t_, u_=u_, br=br: e.scalar_tensor_tensor(out=u_.t[:], in0=t_.t[:], scalar=1.0, in1=br.t[:],
                                                                                   op0=ALU.add, op1=ALU.mult), reads=[t_, br], writes=[u_])
                us.append(u_)
            ph.op("pool", lambda e, us=us: e.tensor_tensor(out=us[0].t[:], in0=us[0].t[:], in1=us[1].t[:], op=ALU.add),
                  reads=[us[0], us[1]], writes=[us[0]])
            ph.op("pool", lambda e, us=us, mt=mt, oc=oc: e.tensor_tensor(out=mt.t[:, oc, :], in0=us[0].t[:], in1=us[2].t[:], op=ALU.add),
                  reads=[us[0], us[2]], writes=[mt])
        if blk + 2 < NBLK:
            front_l(blk + 2)
        for t in range(4):
            r0 = t0 + t * 128
            x = rot(xt, c["x"]); c["x"] += 1
            y = rot(yb, t)
            ph.dma("sp", x, x.t[:], x_src[r0:r0 + 128, :], writes=[x])
            for hf in range(2):
                for kc in range(8):
                    ph.op("pe", lambda e, kc=kc, hf=hf, t=t, mt=mt: e.matmul(
                        po.t[:, hf * 512:(hf + 1) * 512], mt.t[:, kc, t * 128:(t + 1) * 128], wo.t[:, kc, hf * 512:(hf + 1) * 512],
                        start=(kc == 0), stop=(kc == 7)), reads=[mt, wo], writes=[po], signal=(kc == 7))
            norm_tile(x, gpost, y, src_t=po)
            ph.op("pool", lambda e, x=x, y=y: e.tensor_tensor(out=x.t[:], in0=x.t[:], in1=y.t[:], op=ALU.add),
                  reads=[x, y], writes=[x])
            ph.dma("sp", x, x_dst[r0:r0 + 128, :], x.t[:], reads=[x])

    front_l(0)
    front_a(0)
    front_b(0)
    front_l(1)
    front_a(1)
    for blk in range(NBLK):
        back(blk)
    ph.flush()


W_NAMES = ["ffn1_pre_g", "ffn1_w_up", "ffn1_w_down", "ffn1_post_g", "mix_pre_g", "w_in", "pool_w", "pool_scale",
           "da_lambda_q1", "da_lambda_k1", "da_lambda_q2", "da_lambda_k2", "da_subln_g", "mem_norm_g", "w_mem_kv",
           "w_gate", "b_gate", "w_br_pool", "w_br_da", "w_br_ca", "w_out", "mix_post_g", "ffn2_pre_g", "ffn2_w_up",
           "ffn2_w_down", "ffn2_post_g"]
W_SHAPES = {"ffn1_pre_g": [D], "ffn1_w_up": [D, 2 * DFF], "ffn1_w_down": [DFF, D], "ffn1_post_g": [D], "mix_pre_g": [D],
            "w_in": [D, 2048], "pool_w": [4, 64, 64], "pool_scale": [256], "da_lambda_q1": [64], "da_lambda_k1": [64],
            "da_lambda_q2": [64], "da_lambda_k2": [64], "da_subln_g": [128], "mem_norm_g": [D], "w_mem_kv": [D, 512],
            "w_gate": [D, 3072], "b_gate": [3072], "w_br_pool": [256, D], "w_br_da": [512, D], "w_br_ca": [256, D],
            "w_out": [D, D], "mix_post_g": [D], "ffn2_pre_g": [D], "ffn2_w_up": [D, 2 * DFF], "ffn2_w_down": [DFF, D],
            "ffn2_post_g": [D]}


DBG_ITEMS = []


def debug_dump(nc, items):
    sem = nc.alloc_semaphore(name="dbg_sem")
    with nc.Block() as block:
        @block.sync
        def _(e):
            n = 0
            for name, ap, shape, dt in items:
                out = nc.dram_tensor("dbg_" + name, shape, dt, kind="ExternalOutput").ap()
                rows = shape[0]
                step = max(1, rows // 8)
                for r in range(0, rows, step):
                    e.dma_start(out=out[r:r + step], in_=ap[r:r + step]).then_inc(sem, 16)
                    n += 1
            e.wait_ge(sem, 16 * n)
    nc.clear_and_free_semaphores([sem])
    nc.all_engine_barrier()


def build(depth=DEPTH, ffn_T=1024, debug=False):
    import math
    nc = bass.Bass("TRN2", target_bir_lowering=False)
    x_in = nc.dram_tensor("x", [NTOK, D], F32, kind="ExternalInput").ap()
    mem_d = nc.dram_tensor("mem", [NMEM, D], F32, kind="ExternalInput").ap()
    pos_d = nc.dram_tensor("positions", [NTOK], I32, kind="ExternalInput").ap()
    ident_d = nc.dram_tensor("ident", [128, 128], F32, kind="ExternalInput").ap()
    ropec_d = nc.dram_tensor("ropec", [128, 2], F32, kind="ExternalInput").ap()
    hmask_d = nc.dram_tensor("hmask", [128, 2], F32, kind="ExternalInput").ap()
    rcnt_d = nc.dram_tensor("rcnt", [2, 128, NTOK], F32, kind="ExternalInput").ap()
    W = {}
    for n in W_NAMES:
        W[n] = nc.dram_tensor(n, [depth] + W_SHAPES[n], F32, kind="ExternalInput").ap()
    y_out = nc.dram_tensor("y", [NTOK, D], F32, kind="ExternalOutput").ap()

    def scr(name, shape, dt):
        return nc.dram_tensor(name, shape, dt).ap()

    xs = [scr(f"xs{i}", [NTOK, D], F32) for i in range(3)]
    cos_d = scr("cos_t", [128, NTOK], F32)
    sin_d = scr("sin_t", [128, NTOK], F32)
    h2T_d = scr("h2T", [128, 8, NTOK], BF16)
    qT_d = scr("qT", [512, NTOK], BF16)
    kown_d = scr("kown", [512, NTOK], BF16)
    vown_d = scr("vown", [NTOK, 512], BF16)
    upT_d = scr("upT", [2, 128, NTOK], F32)
    qcaT_d = scr("qcaT", [256, NTOK], BF16)
    edge_d = scr("edge", [256, 16], F32)
    edgefull_d = scr("edgefull", [512, 16], F32)
    kfull_d = [scr(f"kfull{h}", [256, NTOK], BF16) for h in range(4)]
    vfull_d = [scr(f"vfull{p}", [2048, 512], BF16) for p in range(4)]
    ydaT_d = scr("ydaT", [512, NTOK], BF16)

    rope_phase(nc, "R", pos_d, ropec_d, cos_d, sin_d)
    cur = x_in
    for l in range(depth):
        lam_init = 0.8 - 0.6 * math.exp(-0.3 * l)
        ffn_phase(nc, f"A{l}", cur, xs[0], W["ffn1_w_up"][l], W["ffn1_w_down"][l], W["ffn1_pre_g"][l], W["ffn1_post_g"][l],
                  ident_d, T=ffn_T)
        proj_phase(nc, f"B{l}", xs[0], W["mix_pre_g"][l], W["w_in"][l], ident_d, cos_d, sin_d, h2T_d, qT_d, kown_d, vown_d,
                   upT_d, qcaT_d, edge_d)
        items = [(kown_d[h * 128:(h + 1) * 128, :], kfull_d[h]) for h in range(4)]
        items += [(vown_d[p * 1024:(p + 1) * 1024, :], vfull_d[p]) for p in range(4)]
        items += [(edge_d, edgefull_d)]
        exchange_phase(nc, f"X{l}", items)
        attn_phase(nc, f"C{l}", qT_d, kfull_d, vfull_d, W["da_lambda_q1"][l], W["da_lambda_k1"][l], W["da_lambda_q2"][l],
                   W["da_lambda_k2"][l], W["da_subln_g"][l], lam_init, ydaT_d)
        merge_phase(nc, f"M{l}", xs[0], xs[1], ident_d, h2T_d, upT_d, edgefull_d, hmask_d, rcnt_d, qcaT_d, ydaT_d, mem_d,
                    W["pool_w"][l], W["pool_scale"][l], W["mem_norm_g"][l], W["w_mem_kv"][l], W["w_gate"][l], W["b_gate"][l],
                    W["w_br_pool"][l], W["w_br_da"][l], W["w_br_ca"][l], W["w_out"][l], W["mix_post_g"][l], dbg=False)
        dst = y_out if l == depth - 1 else xs[2]
        ffn_phase(nc, f"D{l}", xs[1], dst, W["ffn2_w_up"][l], W["ffn2_w_down"][l], W["ffn2_pre_g"][l], W["ffn2_post_g"][l],
                  ident_d, T=ffn_T)
        cur = xs[2]
    if debug:
        debug_dump(nc, [("x1", xs[0], [NTOK, D], F32), ("x2", xs[1], [NTOK, D], F32), ("cos", cos_d, [128, NTOK], F32),
                        ("sin", sin_d, [128, NTOK], F32), ("h2T", h2T_d, [128, 8, NTOK], BF16), ("qT", qT_d, [512, NTOK], BF16),
                        ("kfull0", kfull_d[0], [256, NTOK], BF16), ("vfull0", vfull_d[0], [2048, 512], BF16),
                        ("upT", upT_d, [2, 128, NTOK], F32), ("qcaT", qcaT_d, [256, NTOK], BF16),
                        ("edgefull", edgefull_d, [512, 16], F32), ("ydaT", ydaT_d, [512, NTOK], BF16)] + DBG_ITEMS)
    return nc


def host_consts():
    ident = np.eye(128, dtype=np.float32)
    inv = (np.float32(500000.0) ** (-np.arange(0, 16, 2, dtype=np.float32) / np.float32(16))).astype(np.float32)
    ropec = np.zeros((128, 2), np.float32)
    for p in range(128):
        d = p % 64
        if d < 16:
            ropec[p, 0] = inv[d % 8]
            ropec[p, 1] = -1.0 if d < 8 else 1.0
    rc = []
    pos = np.arange(SEQ)
    for w in (2, 4, 8, 16):
        lo = np.clip(pos - w // 2, 0, SEQ)
        hi = np.clip(pos + w // 2, 0, SEQ)
        rc.append(np.repeat((1.0 / (hi - lo).astype(np.float32))[None, :], 64, axis=0))
    rcnt = np.concatenate(rc, axis=0).astype(np.float32).reshape(2, 128, SEQ)
    return ident, ropec, rcnt


def make_in_maps(inputs, depth=DEPTH, ncores=NCORES):
    ident, ropec, rcnt = host_consts()
    x = np.asarray(inputs["x"])
    mem = np.asarray(inputs["mem"])
    pos = np.asarray(inputs["positions"])
    maps = []
    for core in range(ncores):
        b, j = core // 2, core % 2
        m = {"x": np.ascontiguousarray(x[b, j * NTOK:(j + 1) * NTOK]),
             "mem": np.ascontiguousarray(mem[b]),
             "positions": np.ascontiguousarray(pos[b, j * NTOK:(j + 1) * NTOK]).astype(np.int32),
             "ident": ident, "ropec": ropec,
             "hmask": np.tile(np.array([[1.0 if j == 1 else 0.0, 1.0 if j == 0 else 0.0]], np.float32), (128, 1)),
             "rcnt": np.ascontiguousarray(rcnt[:, :, j * NTOK:(j + 1) * NTOK])}
        for n in W_NAMES:
            m[n] = np.ascontiguousarray(np.asarray(inputs[n])[:depth])
        maps.append(m)
    return maps


_NC_CACHE = {}


def kernel(**inputs):
    if "nc" not in _NC_CACHE:
        _NC_CACHE["nc"] = build()
    nc = _NC_CACHE["nc"]
    maps = make_in_maps(inputs)
    res = run_bass_kernel_spmd(nc, maps, core_ids=list(range(NCORES)))
    out = np.empty((4, SEQ, D), np.float32)
    for core in range(NCORES):
        b, j = core // 2, core % 2
        out[b, j * NTOK:(j + 1) * NTOK] = res.results[core]["y"]
    return out
```
